# Optimizing a Trainium2 kernel written in Bass

```python
import math
import jax, jax.numpy as jnp
from jax import lax
import numpy as np

D_MODEL = 1024
BATCH = 16
SEQ = 256
DEPTH = 2
DEC_BATCH = 8
DEC_SEQ = 1024
PAST_LEN = 512

GRID_W = 64
EPS = 1e-6
DN_HEADS = 4
DN_HEAD_DIM = 128
DN_WIDTH = DN_HEADS * DN_HEAD_DIM
DN_CHUNK = 64
SHORT_CONV = 3
SSM_WIDTH = 512
SSM_GROUP_CH = 16
SSM_GROUPS = SSM_WIDTH // SSM_GROUP_CH
SSM_STATE = 64
POOL_WINDOWS = (2, 4, 8, 16)
POOL_WIDTH = 512
POOL_GROUP = POOL_WIDTH // len(POOL_WINDOWS)
N_BRANCH = 3
D_FF = 2816
FFN_CONV = 3
IN_SPLITS = (DN_WIDTH, DN_WIDTH, DN_WIDTH, DN_WIDTH, 2 * DN_HEADS, 2 * DN_HEADS,
             SSM_WIDTH, POOL_WIDTH, N_BRANCH * D_MODEL)
IN_COLS = sum(IN_SPLITS)

kernel_name = "hybrid_dit_deltanet_s5_pool_step"


def rmsnorm(x, g):
    xf = x.astype(jnp.float32)
    y = xf * lax.rsqrt(jnp.mean(xf * xf, axis=-1, keepdims=True) + EPS)
    return (y * g.astype(jnp.float32)).astype(x.dtype)


def l2norm(x):
    return x * lax.rsqrt(jnp.sum(x * x, axis=-1, keepdims=True) + EPS)


def dwconv(x, w):
    k = w.shape[0]
    return lax.conv_general_dilated(
        x, w[:, None, :].astype(x.dtype), window_strides=(1,),
        padding=[(k // 2, k // 2)], dimension_numbers=("NWC", "WIO", "NWC"),
        feature_group_count=x.shape[-1])


def grid_pos_embed(rows, dim):
    quarter = dim // 4
    omega = 1.0 / (10000.0 ** (jnp.arange(quarter, dtype=jnp.float32) / quarter))
    r = jnp.broadcast_to(jnp.arange(rows, dtype=jnp.float32)[:, None], (rows, GRID_W)).reshape(-1)
    col = jnp.broadcast_to(jnp.arange(GRID_W, dtype=jnp.float32)[None, :], (rows, GRID_W)).reshape(-1)

    def sincos(p):
        ang = p[:, None] * omega[None, :]
        return jnp.concatenate([jnp.sin(ang), jnp.cos(ang)], axis=-1)

    return jnp.concatenate([sincos(r), sincos(col)], axis=-1)


def gated_delta_chunked(q, k, v, g, beta, s0):
    bsz, L, H, _ = q.shape
    dv = v.shape[-1]
    n = L // DN_CHUNK

    def chunks(t):
        return t.reshape(bsz, n, DN_CHUNK, H, -1).transpose(1, 0, 3, 2, 4)

    qc, kc, vc = chunks(q), chunks(k), chunks(v)
    gc = jnp.cumsum(g.reshape(bsz, n, DN_CHUNK, H).transpose(1, 0, 3, 2), axis=-1)
    bc = beta.reshape(bsz, n, DN_CHUNK, H).transpose(1, 0, 3, 2)[..., None]
    idx = jnp.arange(DN_CHUNK)
    causal = idx[:, None] >= idx[None, :]
    strict = idx[:, None] > idx[None, :]
    decay = jnp.exp(jnp.where(causal, gc[..., :, None] - gc[..., None, :], -jnp.inf))
    k_beta = kc * bc
    a_mat = jnp.where(strict, jnp.einsum("nbhid,nbhjd->nbhij", k_beta, kc) * decay, 0.0)
    tri = a_mat + jnp.eye(DN_CHUNK, dtype=a_mat.dtype)
    u_val = lax.linalg.triangular_solve(tri, vc * bc, left_side=True, lower=True, unit_diagonal=True)
    w_key = lax.linalg.triangular_solve(tri, k_beta * jnp.exp(gc)[..., None],
                                        left_side=True, lower=True, unit_diagonal=True)
    qk = jnp.einsum("nbhid,nbhjd->nbhij", qc, kc) * decay

    def step(S, inp):
        q_i, k_i, u_i, w_i, g_i, qk_i = inp
        v_new = u_i - jnp.einsum("bhik,bhkv->bhiv", w_i, S)
        o_i = (jnp.einsum("bhik,bhkv->bhiv", q_i * jnp.exp(g_i)[..., None], S)
               + jnp.einsum("bhij,bhjv->bhiv", qk_i, v_new))
        g_last = g_i[..., -1:]
        S = (S * jnp.exp(g_last)[..., None]
             + jnp.einsum("bhik,bhiv->bhkv", k_i * jnp.exp(g_last - g_i)[..., None], v_new))
        return S, o_i

    s_final, o = lax.scan(step, s0.astype(jnp.float32), (qc, kc, u_val, w_key, gc, qk))
    return o.transpose(1, 0, 3, 2, 4).reshape(bsz, L, H, dv), s_final


def deltanet_mixer(q, k, v, z, a, b, conv_w, a_log, dt_bias, norm_g, s0):
    bsz, L, _ = q.shape
    qkv = jax.nn.silu(dwconv(jnp.concatenate([q, k, v], axis=-1), conv_w)).astype(jnp.float32)
    qh, kh, vh = [t.reshape(bsz, L, DN_HEADS, DN_HEAD_DIM) for t in jnp.split(qkv, 3, axis=-1)]
    qh = l2norm(qh) * (DN_HEAD_DIM ** -0.5)
    kh = l2norm(kh)
    a = a.astype(jnp.float32).reshape(bsz, L, 2, DN_HEADS)
    b = b.astype(jnp.float32).reshape(bsz, L, 2, DN_HEADS)
    g = -jnp.exp(a_log.astype(jnp.float32)) * jax.nn.softplus(a + dt_bias.astype(jnp.float32))
    beta = jax.nn.sigmoid(b)
    o_f, s_f = gated_delta_chunked(qh, kh, vh, g[:, :, 0], beta[:, :, 0], s0[:, 0])
    o_b, s_b = gated_delta_chunked(qh[:, ::-1], kh[:, ::-1], vh[:, ::-1],
                                   g[:, ::-1, 1], beta[:, ::-1, 1], s0[:, 1])
    o = o_f + o_b[:, ::-1]
    o = rmsnorm(o, norm_g) * jax.nn.silu(z.astype(jnp.float32).reshape(bsz, L, DN_HEADS, DN_HEAD_DIM))
    return o.reshape(bsz, L, DN_WIDTH).astype(q.dtype), jnp.stack([s_f, s_b], axis=1)


def s5_scan(u, lam_re, lam_im, log_step, b_re, b_im, h0_re, h0_im):
    step = jnp.exp(log_step)[:, None]
    mag = jnp.exp(lam_re * step)
    ang = lam_im * step
    lb_re, lb_im = mag * jnp.cos(ang), mag * jnp.sin(ang)
    nr, ni = lb_re - 1.0, lb_im
    den = lam_re * lam_re + lam_im * lam_im
    f_re = (nr * lam_re + ni * lam_im) / den
    f_im = (ni * lam_re - nr * lam_im) / den
    bb_re = f_re[..., None] * b_re - f_im[..., None] * b_im
    bb_im = f_re[..., None] * b_im + f_im[..., None] * b_re
    bu_re = jnp.einsum("blgc,gpc->blgp", u, bb_re)
    bu_im = jnp.einsum("blgc,gpc->blgp", u, bb_im)
    bu_re = bu_re.at[:, 0].add(lb_re * h0_re - lb_im * h0_im)
    bu_im = bu_im.at[:, 0].add(lb_re * h0_im + lb_im * h0_re)
    a_re = jnp.broadcast_to(lb_re, bu_re.shape)
    a_im = jnp.broadcast_to(lb_im, bu_im.shape)

    def combine(e1, e2):
        a1r, a1i, b1r, b1i = e1
        a2r, a2i, b2r, b2i = e2
        return (a2r * a1r - a2i * a1i, a2r * a1i + a2i * a1r,
                a2r * b1r - a2i * b1i + b2r, a2r * b1i + a2i * b1r + b2i)

    _, _, x_re, x_im = lax.associative_scan(combine, (a_re, a_im, bu_re, bu_im), axis=1)
    return x_re, x_im


def s5_mixer(u, lam_re, lam_im, log_step, b_re, b_im, c_re, c_im, d_skip, glu_w, glu_b, h0_re, h0_im):
    bsz, L, _ = u.shape
    f32 = jnp.float32
    uf = u.astype(f32)
    ug = uf.reshape(bsz, L, SSM_GROUPS, SSM_GROUP_CH)
    lam_re, lam_im, log_step = lam_re.astype(f32), lam_im.astype(f32), log_step.astype(f32)
    b_re, b_im = b_re.astype(f32), b_im.astype(f32)
    h0_re, h0_im = h0_re.astype(f32), h0_im.astype(f32)
    xf_re, xf_im = s5_scan(ug, lam_re[0], lam_im[0], log_step[0], b_re, b_im, h0_re[:, 0], h0_im[:, 0])
    xb_re, xb_im = s5_scan(ug[:, ::-1], lam_re[1], lam_im[1], log_step[1], b_re, b_im,
                           h0_re[:, 1], h0_im[:, 1])
    fin_re = jnp.stack([xf_re[:, -1], xb_re[:, -1]], axis=1)
    fin_im = jnp.stack([xf_im[:, -1], xb_im[:, -1]], axis=1)
    x_re = xf_re + xb_re[:, ::-1]
    x_im = xf_im + xb_im[:, ::-1]
    y = (jnp.einsum("blgp,gcp->blgc", x_re, c_re.astype(f32))
         - jnp.einsum("blgp,gcp->blgc", x_im, c_im.astype(f32)))
    y = y.reshape(bsz, L, SSM_WIDTH) + d_skip.astype(f32) * uf
    y = jax.nn.gelu(y)
    y = y * jax.nn.sigmoid(y @ glu_w.astype(f32) + glu_b.astype(f32))
    return y.astype(u.dtype), fin_re, fin_im


def pool_mixer(u, pool_w, pool_scale):
    bsz, L, _ = u.shape
    uf = u.astype(jnp.float32)
    cs = jnp.concatenate([jnp.zeros((bsz, 1, POOL_WIDTH), jnp.float32), jnp.cumsum(uf, axis=1)], axis=1)
    t = jnp.arange(L)
    outs = []
    for gi, w in enumerate(POOL_WINDOWS):
        csg = cs[..., gi * POOL_GROUP:(gi + 1) * POOL_GROUP]
        lo = jnp.clip(t - w // 2, 0, L)
        hi = jnp.clip(t + w // 2, 0, L)
        s = jnp.take(csg, hi, axis=1) - jnp.take(csg, lo, axis=1)
        outs.append(s / (hi - lo).astype(jnp.float32)[None, :, None])
    pooled = (jnp.concatenate(outs, axis=-1) - uf).reshape(bsz, L, len(POOL_WINDOWS), POOL_GROUP)
    mixed = jnp.einsum("blgc,gcd->blgd", pooled, pool_w.astype(jnp.float32)).reshape(bsz, L, POOL_WIDTH)
    return (mixed * pool_scale.astype(jnp.float32)).astype(u.dtype)


def conv_ffn(h, w_up, conv_w, w_down):
    hu = dwconv(h @ w_up, conv_w)
    gate, val = jnp.split(hu, 2, axis=-1)
    return (jax.nn.silu(gate) * val) @ w_down


def trunk_layer(x, cond, s_dn, s_re, s_im, norm1_g, norm2_g, w_ada, b_ada, w_in, dn_conv, dn_a_log,
                dn_dt_bias, dn_norm_g, ssm_lambda_re, ssm_lambda_im, ssm_log_step, ssm_b_re, ssm_b_im,
                ssm_c_re, ssm_c_im, ssm_d, ssm_glu_w, ssm_glu_b, pool_w, pool_scale, w_branch_dn,
                w_branch_ssm, w_branch_pool, w_out, ffn_w_up, ffn_conv, ffn_w_down):
    mod = (jax.nn.silu(cond) @ w_ada + b_ada)[:, None, :]
    shift1, scale1, gate1, shift2, scale2, gate2 = jnp.split(mod, 6, axis=-1)
    h = rmsnorm(x, norm1_g) * (1 + scale1) + shift1
    offsets = [int(o) for o in np.cumsum(IN_SPLITS)[:-1]]
    q, k, v, z, dn_a, dn_b, u_ssm, u_pool, gates = jnp.split(h @ w_in, offsets, axis=-1)
    o_dn, st_dn = deltanet_mixer(q, k, v, z, dn_a, dn_b, dn_conv, dn_a_log, dn_dt_bias, dn_norm_g, s_dn)
    o_ssm, st_re, st_im = s5_mixer(u_ssm, ssm_lambda_re, ssm_lambda_im, ssm_log_step, ssm_b_re, ssm_b_im,
                                   ssm_c_re, ssm_c_im, ssm_d, ssm_glu_w, ssm_glu_b, s_re, s_im)
    o_pool = pool_mixer(u_pool, pool_w, pool_scale)
    g_dn, g_ssm, g_pool = jnp.split(jax.nn.sigmoid(gates), N_BRANCH, axis=-1)
    merged = g_dn * (o_dn @ w_branch_dn) + g_ssm * (o_ssm @ w_branch_ssm) + g_pool * (o_pool @ w_branch_pool)
    x = x + gate1 * (merged @ w_out)
    h = rmsnorm(x, norm2_g) * (1 + scale2) + shift2
    x = x + gate2 * conv_ffn(h, ffn_w_up, ffn_conv, ffn_w_down)
    return x, st_dn, st_re, st_im


def setup_inputs(seed: int = 0) -> dict:
    key = jax.random.key(seed)
    ks = jax.random.split(key, 40)
    nrm = jax.random.normal
    D = D_MODEL
    dt = jnp.exp(jax.random.uniform(ks[14], (DEPTH, 2, DN_HEADS), minval=math.log(1e-3), maxval=math.log(1e-1)))
    return {
        "x_prompt": nrm(ks[0], (BATCH, SEQ, D)),
        "x_sample": nrm(ks[1], (DEC_BATCH, DEC_SEQ, D)),
        "state_dn": 0.1 * nrm(ks[2], (DEC_BATCH, DEPTH, 2, DN_HEADS, DN_HEAD_DIM, DN_HEAD_DIM)),
        "state_ssm_re": 0.1 * nrm(ks[3], (DEC_BATCH, DEPTH, 2, SSM_GROUPS, SSM_STATE)),
        "state_ssm_im": 0.1 * nrm(ks[4], (DEC_BATCH, DEPTH, 2, SSM_GROUPS, SSM_STATE)),
        "c": nrm(ks[5], (DEC_BATCH, D)),
        "c_ctx": nrm(ks[6], (D,)),
        "norm1_g": 1.0 + 0.02 * nrm(ks[7], (DEPTH, D)),
        "norm2_g": 1.0 + 0.02 * nrm(ks[8], (DEPTH, D)),
        "w_ada": 0.5 * D ** -0.5 * nrm(ks[9], (DEPTH, D, 6 * D)),
        "b_ada": 0.02 * nrm(ks[10], (DEPTH, 6 * D)),
        "w_in": D ** -0.5 * nrm(ks[11], (DEPTH, D, IN_COLS)),
        "dn_conv": SHORT_CONV ** -0.5 * nrm(ks[12], (DEPTH, SHORT_CONV, 3 * DN_WIDTH)),
        "dn_a_log": jnp.log(jax.random.uniform(ks[13], (DEPTH, 2, DN_HEADS), minval=1.0, maxval=16.0)),
        "dn_dt_bias": dt + jnp.log(-jnp.expm1(-dt)),
        "dn_norm_g": 1.0 + 0.02 * nrm(ks[15], (DEPTH, DN_HEAD_DIM)),
        "ssm_lambda_re": -0.5 + 0.01 * nrm(ks[16], (DEPTH, 2, SSM_GROUPS, SSM_STATE)),
        "ssm_lambda_im": jnp.pi * jnp.arange(SSM_STATE, dtype=jnp.float32)
                         + 0.01 * nrm(ks[17], (DEPTH, 2, SSM_GROUPS, SSM_STATE)),
        "ssm_log_step": jax.random.uniform(ks[18], (DEPTH, 2, SSM_GROUPS),
                                           minval=math.log(1e-3), maxval=math.log(1e-1)),
        "ssm_b_re": (2 * SSM_GROUP_CH) ** -0.5 * nrm(ks[19], (DEPTH, SSM_GROUPS, SSM_STATE, SSM_GROUP_CH)),
        "ssm_b_im": (2 * SSM_GROUP_CH) ** -0.5 * nrm(ks[20], (DEPTH, SSM_GROUPS, SSM_STATE, SSM_GROUP_CH)),
        "ssm_c_re": SSM_STATE ** -0.5 * nrm(ks[21], (DEPTH, SSM_GROUPS, SSM_GROUP_CH, SSM_STATE)),
        "ssm_c_im": SSM_STATE ** -0.5 * nrm(ks[22], (DEPTH, SSM_GROUPS, SSM_GROUP_CH, SSM_STATE)),
        "ssm_d": nrm(ks[23], (DEPTH, SSM_WIDTH)),
        "ssm_glu_w": SSM_WIDTH ** -0.5 * nrm(ks[24], (DEPTH, SSM_WIDTH, SSM_WIDTH)),
        "ssm_glu_b": 0.02 * nrm(ks[25], (DEPTH, SSM_WIDTH)),
        "pool_w": POOL_GROUP ** -0.5 * nrm(ks[26], (DEPTH, len(POOL_WINDOWS), POOL_GROUP, POOL_GROUP)),
        "pool_scale": 1.0 + 0.1 * nrm(ks[27], (DEPTH, POOL_WIDTH)),
        "w_branch_dn": DN_WIDTH ** -0.5 * nrm(ks[28], (DEPTH, DN_WIDTH, D)),
        "w_branch_ssm": SSM_WIDTH ** -0.5 * nrm(ks[29], (DEPTH, SSM_WIDTH, D)),
        "w_branch_pool": POOL_WIDTH ** -0.5 * nrm(ks[30], (DEPTH, POOL_WIDTH, D)),
        "w_out": D ** -0.5 * nrm(ks[31], (DEPTH, D, D)),
        "ffn_w_up": D ** -0.5 * nrm(ks[32], (DEPTH, D, 2 * D_FF)),
        "ffn_conv": FFN_CONV ** -0.5 * nrm(ks[33], (DEPTH, FFN_CONV, 2 * D_FF)),
        "ffn_w_down": D_FF ** -0.5 * nrm(ks[34], (DEPTH, D_FF, D)),
        "final_norm_g": 1.0 + 0.02 * nrm(ks[35], (D,)),
    }


def reference(x_prompt, x_sample, state_dn, state_ssm_re, state_ssm_im, c, c_ctx, norm1_g, norm2_g,
              w_ada, b_ada, w_in, dn_conv, dn_a_log, dn_dt_bias, dn_norm_g, ssm_lambda_re, ssm_lambda_im,
              ssm_log_step, ssm_b_re, ssm_b_im, ssm_c_re, ssm_c_im, ssm_d, ssm_glu_w, ssm_glu_b, pool_w,
              pool_scale, w_branch_dn, w_branch_ssm, w_branch_pool, w_out, ffn_w_up, ffn_conv, ffn_w_down,
              final_norm_g):
    bsz = x_prompt.shape[0]
    rows = x_sample.shape[1] // GRID_W
    cond_ctx = c_ctx[None, :]
    x_ctx = x_prompt
    x_lat = x_sample + grid_pos_embed(rows, D_MODEL).astype(x_sample.dtype)[None]
    zero_dn = jnp.zeros((bsz, 2, DN_HEADS, DN_HEAD_DIM, DN_HEAD_DIM), jnp.float32)
    zero_ssm = jnp.zeros((bsz, 2, SSM_GROUPS, SSM_STATE), jnp.float32)
    new_dn, new_re, new_im = [], [], []
    for l in range(DEPTH):
        def run(x, cond, s_dn, s_re, s_im):
            return trunk_layer(x, cond, s_dn, s_re, s_im, norm1_g[l], norm2_g[l], w_ada[l], b_ada[l], w_in[l],
                               dn_conv[l], dn_a_log[l], dn_dt_bias[l], dn_norm_g[l], ssm_lambda_re[l],
                               ssm_lambda_im[l], ssm_log_step[l], ssm_b_re[l], ssm_b_im[l], ssm_c_re[l],
                               ssm_c_im[l], ssm_d[l], ssm_glu_w[l], ssm_glu_b[l], pool_w[l], pool_scale[l],
                               w_branch_dn[l], w_branch_ssm[l], w_branch_pool[l], w_out[l], ffn_w_up[l],
                               ffn_conv[l], ffn_w_down[l])
        x_ctx, st_dn, st_re, st_im = run(x_ctx, cond_ctx, zero_dn, zero_ssm, zero_ssm)
        new_dn.append(st_dn)
        new_re.append(st_re)
        new_im.append(st_im)
        x_lat, _, _, _ = run(x_lat, c, state_dn[:, l], state_ssm_re[:, l], state_ssm_im[:, l])
    y_prompt = rmsnorm(x_ctx, final_norm_g)
    y_sample = rmsnorm(x_lat, final_norm_g)
    new_state_dn = jnp.stack(new_dn, axis=1)
    new_state_ssm_re = jnp.stack(new_re, axis=1)
    new_state_ssm_im = jnp.stack(new_im, axis=1)
    return (y_prompt, y_sample, new_state_dn, new_state_ssm_re, new_state_ssm_im)
```

```python
import math
import os
from contextlib import ExitStack
import numpy as np
import concourse.bass as bass
import concourse.mybir as mybir
from concourse.bass_utils import run_bass_kernel_spmd

F32 = mybir.dt.float32
BF16 = mybir.dt.bfloat16
I32 = mybir.dt.int32
AF = mybir.ActivationFunctionType
ALU = mybir.AluOpType
SEM_ROLL = 30000
DSZ = {F32: 4, BF16: 2, I32: 4}


class View:
    __slots__ = ("ap", "tile", "p0", "p1", "lo", "hi")

    def __init__(self, ap, tile, p0, p1, lo, hi):
        self.ap, self.tile, self.p0, self.p1, self.lo, self.hi = ap, tile, p0, p1, lo, hi

    def with_ap(self, ap):
        return View(ap, self.tile, self.p0, self.p1, self.lo, self.hi)

    def rev(self):
        ap = self.ap
        pat = [list(p) for p in ap.ap]
        assert pat[-1][0] == 1
        off = ap.offset + (pat[-1][1] - 1)
        pat[-1][0] = -1
        return self.with_ap(bass.AP(ap.tensor, off, pat))

    def bcast(self, n):
        ap = self.ap
        pat = [list(p) for p in ap.ap]
        pat[-1] = [0, n]
        return self.with_ap(bass.AP(ap.tensor, ap.offset, pat))


class Tile:
    def __init__(self, name, shape, dtype, handle, track=None, base=0):
        self.name, self.shape, self.dtype, self.h = name, list(shape), dtype, handle
        self.esz = DSZ[dtype]
        st = [1] * len(shape)
        for i in range(len(shape) - 2, 0, -1):
            st[i] = st[i + 1] * shape[i + 1]
        self.strides = st
        self.track = track if track is not None else self
        self.base = base
        self.recs = []
        self.dma_in = 0
        self.dma_in_sem = None
        self.dma_out = 0
        self.dma_out_sem = None
        self.whole = False

    def __getitem__(self, key):
        if not isinstance(key, tuple):
            key = (key,)
        key = tuple(key) + (slice(None),) * (len(self.shape) - len(key))
        rng = []
        for k, n in zip(key, self.shape):
            if isinstance(k, slice):
                a, b, s = k.indices(n)
                assert s == 1 and b > a, (self.name, key)
                rng.append((a, b))
            else:
                assert 0 <= k < n, (self.name, key)
                rng.append((k, k + 1))
        p0, p1 = rng[0]
        lo = sum(r[0] * s for r, s in zip(rng[1:], self.strides[1:]))
        hi = sum((r[1] - 1) * s for r, s in zip(rng[1:], self.strides[1:])) + 1
        if self.whole:
            return View(self.h[key], self.track, 0, 128, 0, 1 << 20)
        return View(self.h[key], self.track, p0, p1, self.base + lo * self.esz, self.base + hi * self.esz)

    def all(self):
        return self[tuple(slice(None) for _ in self.shape)]


class Op:
    __slots__ = ("eng", "fn", "deps", "is_dma", "dma_sem_tile", "dma_kind", "need_inc", "val", "idx", "dma_waits",
                 "deps_need", "clk")


class FW:
    ENGS = ("pe", "act", "dve", "pool", "sp")

    def __init__(self, nc, stack):
        self.nc, self.stack = nc, stack
        self.ops, self.tiles = [], []

    def sbuf(self, name, shape, dtype=F32):
        h = self.stack.enter_context(self.nc.sbuf_tensor("t_" + name, list(shape), dtype))
        t = Tile(name, shape, dtype, h)
        self.tiles.append(t)
        return t

    def psum(self, name, shape, dtype=F32):
        h = self.stack.enter_context(self.nc.psum_tensor("t_" + name, list(shape), dtype))
        t = Tile(name, shape, dtype, h)
        t.whole = True
        self.tiles.append(t)
        return t

    def carve(self, arena, off_bytes, name, shape, dtype=F32):
        n = int(np.prod(shape[1:])) * DSZ[dtype]
        assert off_bytes % 4 == 0 and n % 4 == 0
        assert off_bytes + n <= arena.shape[1] * 4, (name, off_bytes, n)
        ap = arena.h[:, off_bytes // 4:(off_bytes + n) // 4]
        if dtype != F32:
            ap = ap.bitcast(dtype)
        if len(shape) == 3:
            ap = ap.rearrange("p (a b) -> p a b", a=shape[1])
        elif len(shape) == 4:
            ap = ap.rearrange("p (a b c) -> p a b c", a=shape[1], b=shape[2])
        return Tile(name, shape, dtype, ap, track=arena, base=off_bytes)

    @staticmethod
    def _ov(r, v):
        return r[0] < v.p1 and v.p0 < r[1] and r[2] < v.hi and v.lo < r[3]

    def _track(self, op, reads, writes, whole_tile_war=False):
        deps = []
        for v in reads:
            for r in v.tile.recs:
                if r[5] and self._ov(r, v):
                    deps.append(r[4])
                elif v.tile.whole and (not r[5]) and r[4].eng != op.eng:
                    deps.append(r[4])
        for v in writes:
            for r in v.tile.recs:
                if self._ov(r, v) or (whole_tile_war and not (r[5] and r[4].is_dma)):
                    deps.append(r[4])
        for v in reads:
            t = v.tile
            key = (v.p0, v.p1, v.lo, v.hi)
            new = [r for r in t.recs
                   if not ((not r[5]) and (not op.is_dma) and (not r[4].is_dma) and r[4].eng == op.eng and r[:4] == key)]
            new.append((v.p0, v.p1, v.lo, v.hi, op, False))
            t.recs = new
        for v in writes:
            t = v.tile
            new = [r for r in t.recs
                   if not (v.p0 <= r[0] and r[1] <= v.p1 and v.lo <= r[2] and r[3] <= v.hi)]
            new.append((v.p0, v.p1, v.lo, v.hi, op, True))
            t.recs = new
        return deps

    def _newop(self, eng, is_dma):
        op = Op()
        op.eng, op.is_dma, op.need_inc, op.dma_waits, op.idx = eng, is_dma, False, [], len(self.ops)
        op.val = 0
        return op

    def _finish(self, op, deps):
        cd = {}
        for d in deps:
            if d is op:
                continue
            if d.is_dma:
                t = d.dma_sem_tile
                op.dma_waits.append((t, d.dma_kind, t.dma_in if d.dma_kind == "in" else t.dma_out))
            else:
                if d.eng == "pe" and op.eng == "pe":
                    continue
                cd[d.idx] = d
        op.deps = list(cd.values())
        for d in op.deps:
            d.need_inc = True
        self.ops.append(op)

    def add(self, eng, fn, reads=(), writes=()):
        op = self._newop(eng, False)
        op.fn = fn
        self._finish(op, self._track(op, [r for r in reads if r is not None], writes))
        return op

    def dma(self, queue, out, in_, **kw):
        op = self._newop(queue, True)
        if isinstance(out, View):
            t = out.tile
            deps = self._track(op, [], [out], whole_tile_war=True)
            op.dma_sem_tile, op.dma_kind = t, "in"
            oap, iap = out.ap, in_
        else:
            t = in_.tile
            deps = [r[4] for r in t.recs if r[5]]
            t.recs.append((in_.p0, in_.p1, in_.lo, in_.hi, op, False))
            op.dma_sem_tile, op.dma_kind = t, "out"
            oap, iap = out, in_.ap
        self._finish(op, deps)
        if op.dma_kind == "in":
            t.dma_in += 1
        else:
            t.dma_out += 1
        op.fn = lambda eng: eng.dma_start(out=oap, in_=iap, **kw)
        return op

    def emit(self):
        nc = self.nc
        counts = {e: 0 for e in self.ENGS}
        for op in self.ops:
            if op.is_dma or not op.need_inc:
                continue
            counts[op.eng] += 1
            op.val = counts[op.eng]
        esems = {e: [self.stack.enter_context(nc.semaphore(f"s_{e}_{i}")) for i in range(counts[e] // SEM_ROLL + 1)]
                 for e in self.ENGS}
        for t in self.tiles:
            if t.dma_in:
                t.dma_in_sem = self.stack.enter_context(nc.semaphore(f"di_{t.name}"))
            if t.dma_out:
                t.dma_out_sem = self.stack.enter_context(nc.semaphore(f"do_{t.name}"))
        per_eng = {e: [] for e in self.ENGS}
        for op in self.ops:
            per_eng[op.eng].append(op)
        finals = [(t.dma_out_sem, 16 * t.dma_out) for t in self.tiles if t.dma_out]
        finals += [(t.dma_in_sem, 16 * t.dma_in) for t in self.tiles if t.dma_in]
        last = {e: counts[e] for e in self.ENGS}

        clock = {e: {} for e in self.ENGS}
        for op in self.ops:
            ck = clock[op.eng]
            need = {}
            for d in op.deps:
                si, v = divmod(d.val - 1, SEM_ROLL)
                key = ("e", d.eng, si)
                if need.get(key, (None, 0))[1] < v + 1:
                    need[key] = (esems[d.eng][si], v + 1)
            for (t, kind, cnt) in op.dma_waits:
                s_ = t.dma_in_sem if kind == "in" else t.dma_out_sem
                key = ("d", t.name, kind)
                if need.get(key, (None, 0))[1] < 16 * cnt:
                    need[key] = (s_, 16 * cnt)
            op.deps_need = [(key, s_, v) for key, (s_, v) in need.items() if ck.get(key, 0) < v]
            for key, (s_, v) in need.items():
                if ck.get(key, 0) < v:
                    ck[key] = v
            for d in op.deps:
                for k2, v2 in d.clk.items():
                    if ck.get(k2, 0) < v2:
                        ck[k2] = v2
            if op.is_dma:
                op.clk = {}
            else:
                op.clk = dict(ck)
                if op.need_inc:
                    si, v = divmod(op.val - 1, SEM_ROLL)
                    op.clk[("e", op.eng, si)] = v + 1

        def run(engname, eng):
            for op in per_eng[engname]:
                todo = list(op.deps_need)
                embed = None
                if todo and not op.is_dma:
                    embed = todo.pop()
                for key, s, v in todo:
                    eng.wait_ge(s, v)
                ins = op.fn(eng)
                if embed is not None:
                    ins._wait_ge(embed[1], embed[2])
                if op.is_dma:
                    t = op.dma_sem_tile
                    ins.then_inc(t.dma_in_sem if op.dma_kind == "in" else t.dma_out_sem, 16)
                elif op.need_inc:
                    ins.then_inc(esems[engname][(op.val - 1) // SEM_ROLL], 1)
            if engname == "sp":
                for s, v in finals:
                    eng.wait_ge(s, v)

        with nc.Block() as block:
            @block.sync
            def _(eng):
                run("sp", eng)

            @block.tensor
            def _(eng):
                run("pe", eng)

            @block.scalar
            def _(eng):
                run("act", eng)

            @block.vector
            def _(eng):
                run("dve", eng)

            @block.gpsimd
            def _(eng):
                run("pool", eng)


D = 1024
DEPTH = 2
NT = 1536
SEGS = [(0, 256, 0), (256, 256, 0), (512, 1024, 1)]
TTS = [(0, 512), (512, 512), (1024, 512)]
NCH = NT // 128
IN_COLS = 6160
D_FF = 2816
EPS = 1e-6
BIG = 1.0e9
TWO_PI = 2.0 * math.pi
CST_N = 7 * 128 + 1026 + 256 + 7 * 128

FM_LAYOUT = [("n1g", 16), ("n2g", 16), ("bada", 96), ("dnconv", 36), ("dnng", 1), ("ssmd", 4), ("glub", 4),
             ("pscale", 4), ("fconv", 132)]
FM_OFF = {}
_o = 0
for _n, _s in FM_LAYOUT:
    FM_OFF[_n] = _o
    _o += _s
FM_L = _o
FM_TOT = FM_L * DEPTH + 8


SKIP_MIX = False
DN_DBG_P = 1


class Stop(Exception):
    pass


class KB:
    def __init__(self, nc, stack, dram, dbg=None):
        self.nc, self.dram, self.dbg = nc, dram, dbg
        self.fw = FW(nc, stack)
        fw = self.fw
        self.ps = [fw.psum(f"ps{i}", [128, 512]) for i in range(8)]
        self.pbi = 0
        self.X = fw.sbuf("X", [128, 8, NT])
        self.H = fw.sbuf("H", [128, 8, NT], BF16)
        self.ws = [fw.sbuf(f"ws{i}", [128, 8, 256], BF16) for i in range(3)]
        self.wsi = 0
        self.arena = fw.sbuf("arena", [128, 23500])
        self.cst = fw.sbuf("cst", [128, CST_N])
        self.fm = fw.sbuf("fm", [128, FM_TOT])
        self.bc = fw.sbuf("bc", [128, 32])
        self.sm = fw.sbuf("sm", [128, DEPTH * 2 * 3 * 16])
        self.modt = fw.sbuf("modt", [128, 48, 2])
        self.mods = fw.sbuf("mods", [128, 6, 8, 2])
        self.onesb = fw.sbuf("onesb", [128, 128], BF16)
        self.ones128b = fw.sbuf("ones128b", [128, 128], BF16)
        self.ones = fw.sbuf("ones", [128, 128])
        self.epsc = fw.sbuf("epsc", [128, 2])
        self.kc = fw.sbuf("kc", [128, 4])
        self.ones1b = fw.sbuf("ones1b", [128, 128], BF16)
        self.identb = fw.sbuf("identb", [128, 128], BF16)
        self.onesi128b = fw.sbuf("onesi128b", [128, 128], BF16)
        self.fin = fw.sbuf("fin", [128, 2, 128])
        self.bbt = [fw.sbuf("bbt0", [128, 2, 128], BF16)] * 2
        self.cbt = [fw.sbuf(f"cbt{i}", [128, 2, 128], BF16) for i in range(2)]
        self.h0t = fw.sbuf("h0t", [128, 2, 2, 16])
        self.pwt = fw.sbuf("pwt", [128, 128], BF16)
        self.S = [[fw.sbuf(f"S{a}{b}", [128, 128]) for b in range(2)] for a in range(3)]
        self.S2 = [[fw.sbuf(f"Sx{a}{b}", [128, 128]) for b in range(2)] for a in range(3)]
        self.Sb = [[fw.sbuf(f"Sb{a}{b}", [128, 128], BF16) for b in range(2)] for a in range(3)]
        self.AB = fw.sbuf("AB", [128, 12, 16])
        self.GT = fw.sbuf("GT", [128, 12, 8])
        self.BT = fw.sbuf("BT", [128, 12, 8])
        self.nb = 8
        self.aoff = 0

    def bank(self):
        b = self.ps[self.pbi % self.nb]
        self.pbi += 1
        return b

    def carve(self, name, shape, dtype=F32):
        n = int(np.prod(shape[1:])) * DSZ[dtype]
        n = (n + 63) // 64 * 64
        t = self.fw.carve(self.arena, self.aoff, name, shape, dtype)
        self.aoff += n
        return t

    def mm(self, out, lhsT, rhs, start=True, stop=True):
        self.fw.add("pe", lambda e: e.matmul(out.ap, lhsT.ap, rhs.ap, start=start, stop=stop),
                    reads=[lhsT, rhs], writes=[out])

    def tr(self, out, in_):
        ident = self.ident
        n = in_.p1 - in_.p0
        idv = ident[0:n, 0:n]
        self.fw.add("pe", lambda e: e.transpose(out.ap, in_.ap, idv.ap), reads=[in_, idv], writes=[out])

    def act(self, out, in_, func, bias=None, scale=1.0):
        rd = [in_]
        kw = {}
        if isinstance(bias, View):
            rd.append(bias)
            kw["bias"] = bias.ap
        elif bias is not None:
            kw["bias"] = bias
        if isinstance(scale, View):
            rd.append(scale)
            kw["scale"] = scale.ap
        else:
            kw["scale"] = scale
        self.fw.add("act", lambda e: e.activation(out.ap, in_.ap, func, **kw), reads=rd, writes=[out])

    def tt(self, out, a, b, op, eng="dve"):
        self.fw.add(eng, lambda e: e.tensor_tensor(out.ap, a.ap, b.ap, op), reads=[a, b], writes=[out])

    def ts(self, out, a, s1, s2, op0, op1=None, eng="dve"):
        rd = [a]
        v1 = s1.ap if isinstance(s1, View) else s1
        v2 = s2.ap if isinstance(s2, View) else s2
        if isinstance(s1, View):
            rd.append(s1)
        if isinstance(s2, View):
            rd.append(s2)
        if op1 is None:
            self.fw.add(eng, lambda e: e.tensor_scalar(out.ap, a.ap, v1, None, op0), reads=rd, writes=[out])
        else:
            self.fw.add(eng, lambda e: e.tensor_scalar(out.ap, a.ap, v1, v2, op0, op1), reads=rd, writes=[out])

    def stt(self, out, a, s, b, op0, op1, eng="dve"):
        rd = [a, b]
        sv = s.ap if isinstance(s, View) else s
        if isinstance(s, View):
            rd.append(s)
        self.fw.add(eng, lambda e: e.scalar_tensor_tensor(out.ap, a.ap, sv, b.ap, op0, op1), reads=rd, writes=[out])

    def cp(self, out, in_, eng="dve"):
        if eng == "act":
            self.fw.add("act", lambda e: e.copy(out.ap, in_.ap), reads=[in_], writes=[out])
        else:
            self.fw.add(eng, lambda e: e.tensor_copy(out.ap, in_.ap), reads=[in_], writes=[out])

    def memset(self, out, val, eng="dve"):
        self.fw.add(eng, lambda e: e.memset(out.ap, val), writes=[out])

    def scan(self, out, d0, d1, init, eng="dve"):
        rd = [d0, d1]
        iv = init.ap if isinstance(init, View) else init
        if isinstance(init, View):
            rd.append(init)
        self.fw.add(eng, lambda e: e.tensor_tensor_scan(out.ap, d0.ap, d1.ap, iv, ALU.mult, ALU.add),
                    reads=rd, writes=[out])

    def wload(self, src2d, r0, nk, c0, ncols):
        t = self.ws[self.wsi % 3]
        self.wsi += 1
        v = t[:, 0:nk, 0:ncols]
        self.fw.dma("pool", v, src2d[r0:r0 + 128 * nk, c0:c0 + ncols].rearrange("(k p) m -> p k m", p=128))
        return t

    def fmv(self, l, name, j, n=1):
        o = (l * FM_L if l is not None else 0) + FM_OFF[name] + j
        return self.fm[:, o:o + n]

    def dump(self, name, view):
        if self.dbg == name:
            self.fw.dma("pool", self.dram["dbg"], view)
            return True
        return False

    def setup(self):
        fw, d = self.fw, self.dram
        fw.dma("sp", self.cst.all(), d["cst"])
        fw.dma("sp", self.fm.all(), d["fm"])
        fw.dma("sp", self.bc.all(), d["bcp"].partition_broadcast(128))
        fw.dma("sp", self.sm.all(), d["sm"])
        c = self.cst
        self.ident = Tile("ident", [128, 128], F32, c.h[:, 0:128], track=c, base=0)
        names = ["triF", "triB", "negF", "negB", "posF", "posB"]
        self.cm = {}
        for i, n in enumerate(names):
            self.cm[n] = Tile(n, [128, 128], F32, c.h[:, 128 * (i + 1):128 * (i + 2)], track=c, base=512 * (i + 1))
        o = 7 * 128
        self.iota = Tile("iota", [128, 1026], F32, c.h[:, o:o + 1026], track=c, base=4 * o)
        o += 1026
        self.rcnt = Tile("rcnt", [128, 2, 128], F32, c.h[:, o:o + 256].rearrange("p (a b) -> p a b", a=2), track=c, base=4 * o)
        o += 256
        self.lvm = Tile("lvm", [128, 7, 128], F32, c.h[:, o:o + 896].rearrange("p (a b) -> p a b", a=7), track=c, base=4 * o)
        self.memset(self.onesb.all(), 1.0 / 1024.0)
        self.memset(self.ones128b.all(), 128.0)
        self.memset(self.ones.all(), 1.0)
        self.memset(self.ones1b.all(), 1.0)
        self.cp(self.identb.all(), self.ident.all())
        self.memset(self.onesi128b.all(), 1.0 / 128.0)
        self.memset(self.epsc[:, 0:1], EPS)
        self.memset(self.epsc[:, 1:2], EPS * 128.0)
        self.memset(self.kc[:, 0:1], math.pi / 2)
        self.memset(self.kc[:, 1:2], 3.1415925)
        self.memset(self.kc[:, 2:3], 2 * 3.1415925)

    def load_x(self):
        fw, d = self.fw, self.dram
        self.aoff = 0
        stg = [self.carve(f"stg{i}", [128, 4, 1024]) for i in range(2)]
        pet = [self.carve(f"pet{i}", [128, 4, 1024]) for i in range(2)]
        for g in range(3):
            s = stg[g % 2]
            fw.dma("sp", s.all(), d["xin"][g * 512:(g + 1) * 512, :].rearrange("(b p) m -> p b m", p=128))
            if g >= 1:
                p = pet[g % 2]
                fw.dma("sp", p.all(), d["pe"][(g - 1) * 512:g * 512, :].rearrange("(b p) m -> p b m", p=128))
                self.tt(s.all(), s.all(), p.all(), ALU.add)
            for c in range(8):
                pb = self.bank()
                for b in range(4):
                    self.tr(pb[:, b * 128:(b + 1) * 128], s[:, b, c * 128:(c + 1) * 128])
                self.cp(self.X[:, c, g * 512:(g + 1) * 512], pb.all(), eng="act" if c % 2 else "dve")

    def ada(self, l):
        fw, d = self.fw, self.dram
        self.aoff = 0
        sc = self.carve("scond", [128, 8, 2])
        wsl = [self.carve(f"wada{i}", [128, 8, 1024]) for i in range(2)]
        cnd = self.carve("cond", [128, 8, 2])
        mrow = self.carve("mrow", [128, 6144])
        fw.dma("sp", cnd.all(), d["cond"])
        self.act(sc.all(), cnd.all(), AF.Silu)
        wa = d["w_ada"][l]
        for blk in range(6):
            w = wsl[blk % 2]
            for hf in range(2):
                fw.dma("sp" if hf == 0 else "act", w[:, hf * 4:(hf + 1) * 4, :],
                       wa[hf * 512:(hf + 1) * 512, blk * 1024:(blk + 1) * 1024].rearrange("(k p) m -> p k m", p=128))
            for sub in range(2):
                pb = self.bank()
                for kc in range(8):
                    self.mm(pb[0:2, :], sc[:, kc, :], w[:, kc, sub * 512:(sub + 1) * 512], start=(kc == 0), stop=(kc == 7))
                self.cp(mrow[0:2, blk * 1024 + sub * 512:blk * 1024 + (sub + 1) * 512], pb[0:2, :], eng="act")
        pt = self.bank()
        for j in range(48):
            self.fw.add("pe", lambda e, j=j: e.transpose(pt[:, 2 * j:2 * j + 2].ap, mrow[0:2, j * 128:(j + 1) * 128].ap,
                                                          self.ident[0:2, 0:2].ap),
                        reads=[mrow[0:2, j * 128:(j + 1) * 128], self.ident[0:2, 0:2]], writes=[pt[:, 2 * j:2 * j + 2]])
        bada = self.fm[:, l * FM_L + FM_OFF["bada"]: l * FM_L + FM_OFF["bada"] + 96]
        mf = self.modt.all().with_ap(self.modt.all().ap.rearrange("p a b -> p (a b)"))
        self.tt(mf, pt[:, 0:96], bada, ALU.add)
        md = self.mods
        n1 = self.fm[:, l * FM_L + FM_OFF["n1g"]: l * FM_L + FM_OFF["n1g"] + 16]
        n2 = self.fm[:, l * FM_L + FM_OFF["n2g"]: l * FM_L + FM_OFF["n2g"] + 16]

        def fl(v):
            return v.with_ap(v.ap.rearrange("p a b -> p (a b)"))
        for h, ng in ((0, n1), (1, n2)):
            self.stt(fl(md[:, 3 * h + 0]), fl(self.modt[:, 24 * h + 8:24 * h + 16]), 1.0, ng, ALU.add, ALU.mult)
            self.cp(fl(md[:, 3 * h + 1]), fl(self.modt[:, 24 * h + 0:24 * h + 8]))
            self.cp(fl(md[:, 3 * h + 2]), fl(self.modt[:, 24 * h + 16:24 * h + 24]))

    def norm_mod(self, which):
        self.aoff_save = self.aoff
        self.aoff = 0
        sq = self.carve("sq", [128, 8, 512], BF16)
        rs = self.carve("rs", [128, NT])
        tmp = [self.carve(f"nmt{i}", [128, NT]) for i in range(2)]
        for ti, (t0, tn) in enumerate(TTS):
            self.act(sq.all(), self.X[:, :, t0:t0 + tn], AF.Square)
            pb = self.bank()
            for c in range(8):
                self.mm(pb.all(), self.onesb.all(), sq[:, c, :], start=(c == 0), stop=(c == 7))
            self.act(rs[:, t0:t0 + tn], pb.all(), AF.Ln, bias=self.epsc[:, 0:1])
        self.act(rs.all(), rs.all(), AF.Exp, scale=-0.5)
        for c in range(8):
            t = tmp[c % 2]
            self.tt(t.all(), self.X[:, c, :], rs.all(), ALU.mult)
            for (j, a, b) in ((0, 0, 512), (1, 512, NT)):
                self.act(self.H[:, c, a:b], t[:, a:b], AF.Identity,
                         bias=self.mods[:, 3 * which + 1, c, j:j + 1], scale=self.mods[:, 3 * which + 0, c, j:j + 1])
        self.aoff = self.aoff_save

    def proj(self, w2d, c0, nchunks, consume, nk=8, rhs=None, r0=0):
        rhs = rhs if rhs is not None else self.H
        j = 0
        while j < nchunks:
            nb = min(2, nchunks - j)
            w = self.wload(w2d, r0, nk, c0 + j * 128, nb * 128)
            for jj in range(nb):
                for ti, (t0, tn) in enumerate(TTS):
                    pb = self.bank()
                    for kc in range(nk):
                        self.mm(pb.all(), w[:, kc, jj * 128:(jj + 1) * 128], rhs[:, kc, t0:t0 + tn],
                                start=(kc == 0), stop=(kc == nk - 1))
                    consume(j + jj, ti, pb)
            j += nb

    def dwconv(self, dst, src, w3):
        self.act(dst.all(), src.all(), AF.Identity, scale=w3(1))
        for (s0, L, _) in SEGS:
            self.stt(dst[:, s0 + 1:s0 + L], src[:, s0:s0 + L - 1], w3(0), dst[:, s0 + 1:s0 + L], ALU.mult, ALU.add)
            self.stt(dst[:, s0:s0 + L - 1], src[:, s0 + 1:s0 + L], w3(2), dst[:, s0:s0 + L - 1], ALU.mult, ALU.add)

    def mixer_layout(self):
        self.aoff = 0
        self.o_dn = self.carve("o_dn", [128, 4, NT], BF16)
        self.dn_base = self.aoff
        self.o_ssm = self.carve("o_ssm", [128, 4, NT], BF16)
        self.s5_base = self.aoff
        self.o_pool = self.carve("o_pool", [128, 4, NT], BF16)
        self.mix_base = self.aoff

    def sinred(self, out, ang, ti, tf, shift=0.0):
        if shift != 0.0:
            self.ts(tf, ang, shift, None, ALU.add)
            src = tf
        else:
            src = ang
        self.ts(ti, src, 1.0 / TWO_PI, None, ALU.mult)
        self.cp(out, ti)
        self.stt(tf, out, -TWO_PI, src, ALU.mult, ALU.add)
        self.ts(tf, tf, -3.1415925, 3.1415925, ALU.max, ALU.min)
        self.act(out, tf, AF.Sin)

    def s5(self, l):
        fw, d = self.fw, self.dram
        self.aoff = self.s5_base
        U = self.carve("U16", [128, NT], BF16)
        Yb = self.carve("Ygb", [128, 4, NT], BF16)
        CSs = [self.carve(f"CS{i}", [128, 2, 1026]) for i in range(2)]
        Z = self.carve("Z", [128, 2, NT])
        XR = self.carve("XR", [128, 2, NT], BF16)
        T = self.carve("T", [128, 2, 512])
        ANG = self.carve("ANG", [128, 1026])
        RR = self.carve("RR", [128, 1026])
        TIi = Tile("TIi", [128, 1026], I32, RR.h.bitcast(I32), track=self.arena, base=RR.base)
        sp = self.carve("s5par", [128, 2, 24, 16])
        c32 = self.carve("c32", [128, 2, 128])
        ctmp = self.carve("ctmp", [128, 128])
        ctmp2 = self.carve("ctmp2", [128, 128])
        h0, cb = self.h0t, self.cbt
        bb = self.bbt[0]
        fw.dma("sp", h0[:, 0], d["sre"][l])
        fw.dma("sp", h0[:, 1], d["sim"][l])
        PI = {"step": 0, "r": 1, "th": 2, "lbr": 3, "lbi": 4, "den": 5, "fr": 6, "fi": 7, "gr": 8, "gi": 9,
              "t0": 10, "t1": 11, "t2": 12, "ti": 13, "nr": 14, "hr": 15, "hi": 16, "nfi": 17}
        for dr in range(2):
            def P(n):
                return sp[:, dr, PI[n], :]
            base = ((l * 2 + dr) * 3) * 16
            lre = self.sm[:, base:base + 16]
            lim = self.sm[:, base + 16:base + 32]
            lst = self.sm[:, base + 32:base + 48]
            self.act(P("step"), lst, AF.Exp)
            self.tt(P("t0"), lre, P("step"), ALU.mult)
            self.act(P("r"), P("t0"), AF.Exp)
            self.tt(P("th"), lim, P("step"), ALU.mult)
            tiv = sp[:, dr, PI["ti"], :]
            tiv = tiv.with_ap(tiv.ap.bitcast(I32))
            self.sinred(P("t1"), P("th"), tiv, P("t2"))
            self.tt(P("lbi"), P("r"), P("t1"), ALU.mult)
            self.sinred(P("t1"), P("th"), tiv, P("t2"), shift=math.pi / 2)
            self.tt(P("lbr"), P("r"), P("t1"), ALU.mult)
            self.ts(P("nr"), P("lbr"), -1.0, None, ALU.add)
            self.tt(P("den"), lre, lre, ALU.mult)
            self.tt(P("t0"), lim, lim, ALU.mult)
            self.tt(P("den"), P("den"), P("t0"), ALU.add)
            self.fw.add("dve", lambda e, v=P("den"): e.reciprocal(v.ap, v.ap), reads=[P("den")], writes=[P("den")])
            self.tt(P("t0"), P("nr"), lre, ALU.mult)
            self.tt(P("t1"), P("lbi"), lim, ALU.mult)
            self.tt(P("t0"), P("t0"), P("t1"), ALU.add)
            self.tt(P("fr"), P("t0"), P("den"), ALU.mult)
            self.tt(P("t0"), P("lbi"), lre, ALU.mult)
            self.tt(P("t1"), P("nr"), lim, ALU.mult)
            self.tt(P("t0"), P("t0"), P("t1"), ALU.subtract)
            self.tt(P("fi"), P("t0"), P("den"), ALU.mult)
            self.ts(P("nfi"), P("fi"), -1.0, None, ALU.mult)
            self.tt(P("t0"), P("fr"), P("fr"), ALU.mult)
            self.tt(P("t1"), P("fi"), P("fi"), ALU.mult)
            self.tt(P("t0"), P("t0"), P("t1"), ALU.add)
            self.fw.add("dve", lambda e, v=P("t0"): e.reciprocal(v.ap, v.ap), reads=[P("t0")], writes=[P("t0")])
            self.tt(P("gr"), P("fr"), P("t0"), ALU.mult)
            self.tt(P("gi"), P("nfi"), P("t0"), ALU.mult)
            hr0, hi0 = h0[:, 0, dr, :], h0[:, 1, dr, :]
            self.tt(P("t0"), hr0, P("gr"), ALU.mult)
            self.tt(P("t1"), hi0, P("gi"), ALU.mult)
            self.tt(P("hr"), P("t0"), P("t1"), ALU.subtract)
            self.tt(P("t0"), hr0, P("gi"), ALU.mult)
            self.tt(P("t1"), hi0, P("gr"), ALU.mult)
            self.tt(P("hi"), P("t0"), P("t1"), ALU.add)
        if self.dump(f"sp{l}", sp[:, 0].with_ap(sp[:, 0].ap.rearrange("p a b -> p (a b)"))):
            raise Stop()
        ybanks = [self.ps[5], self.ps[6], self.ps[7]]
        self.pbi = 0
        self.nb = 5
        fin = self.fin
        iters = [(c, sti, dr) for c in range(4) for sti in range(4) for dr in range(2)]

        def tables_a(i):
            c, sti, dr = iters[i]
            st = 4 * c + sti
            c16 = cb[i % 2]

            def P(n):
                return sp[:, dr, PI[n], st:st + 1]
            if dr == 0:
                fw.dma("act", c32.all(), d["cblk"][l, st])
            self.ts(ctmp.all(), c32[:, 1, :], P("fi"), None, ALU.mult)
            self.ts(ctmp2.all(), c32[:, 1, :], P("fr"), None, ALU.mult)
            self.act(ANG.all(), self.iota.all(), AF.Identity, scale=P("th"))
            self.stt(c16[:, 0, :], c32[:, 0, :], P("fr"), ctmp.all(), ALU.mult, ALU.subtract)
            self.stt(c16[:, 1, :], c32[:, 0, :], P("fi"), ctmp2.all(), ALU.mult, ALU.add)
            self.ts(TIi.all(), ANG.all(), 1.0 / TWO_PI, None, ALU.mult)

        def tables_b(i):
            CS = CSs[i % 2]
            self.stt(RR.all(), TIi.all(), -TWO_PI, ANG.all(), ALU.mult, ALU.add)
            self.act(RR.all(), RR.all(), AF.Relu, bias=self.kc[:, 1:2])
            self.act(RR.all(), RR.all(), AF.Relu, bias=self.kc[:, 2:3], scale=-1.0)
            self.act(CS[:, 1, :], RR.all(), AF.Sin, bias=self.kc[:, 1:2], scale=-1.0)
            self.act(ANG.all(), RR.all(), AF.Abs, bias=self.kc[:, 1:2], scale=-1.0)
            self.act(CS[:, 0, :], ANG.all(), AF.Sin, bias=self.kc[:, 0:1], scale=-1.0)

        def compute(i):
            c, sti, dr = iters[i]
            st = 4 * c + sti
            CS, c16 = CSs[i % 2], cb[i % 2]

            def P(n):
                return sp[:, dr, PI[n], st:st + 1]
            if dr == 0:
                fw.dma("pool", bb.all(), d["bblk"][l, st])
            for ti, (t0, tn) in enumerate(TTS):
                pp, pq = self.bank(), self.bank()
                self.mm(pp.all(), bb[:, 0, :], U[:, t0:t0 + tn])
                self.mm(pq.all(), bb[:, 1, :], U[:, t0:t0 + tn])
                pieces = [(0, 256, 0), (256, 256, 0)] if ti == 0 else [(0, 512, (ti - 1) * 512)]
                for (o, n, e0) in pieces:
                    cc, ss = CS[:, 0, e0:e0 + n], CS[:, 1, e0:e0 + n]
                    z0, z1 = Z[:, 0, t0 + o:t0 + o + n], Z[:, 1, t0 + o:t0 + o + n]
                    self.tt(z0, cc, pp[:, o:o + n], ALU.mult)
                    self.tt(T[:, 0, 0:n], ss, pq[:, o:o + n], ALU.mult)
                    self.tt(z1, cc, pq[:, o:o + n], ALU.mult)
                    self.tt(T[:, 1, 0:n], ss, pp[:, o:o + n], ALU.mult)
                    self.tt(z0, z0, T[:, 0, 0:n], ALU.add if dr == 0 else ALU.subtract)
                    self.tt(z1, z1, T[:, 1, 0:n], ALU.subtract if dr == 0 else ALU.add)
            if i + 1 < len(iters):
                tables_b(i + 1)
            rb = P("r")
            col = 1 if dr == 0 else 1024
            cc1, ss1 = CS[:, 0, col:col + 1], CS[:, 1, col:col + 1]
            hr, hi = P("hr"), P("hi")
            i0, i1, i2, i3 = (sp[:, dr, 18 + q, st:st + 1] for q in range(4))
            self.tt(i0, cc1, hr, ALU.mult)
            self.tt(i2, ss1, hi, ALU.mult)
            self.tt(i1, ss1, hr, ALU.mult)
            self.tt(i3, cc1, hi, ALU.mult)
            for si, (s0, L, cj) in enumerate(SEGS):
                inits = [0.0, 0.0]
                if cj == 1:
                    self.tt(i0, i0, i2, ALU.subtract)
                    self.tt(i1, i1, i3, ALU.add)
                    inits = [i0, i1]
                for ri in range(2):
                    zv = Z[:, ri, s0:s0 + L]
                    if dr == 1:
                        zv = zv.rev()
                    self.scan(zv, rb.bcast(L), zv, inits[ri])
            for si, (s0, L, cj) in enumerate(SEGS):
                for o in range(0, L, 512):
                    n = min(512, L - o)
                    a = s0 + o
                    cc, ss = CS[:, 0, o:o + n], CS[:, 1, o:o + n]
                    zr, zi = Z[:, 0, a:a + n], Z[:, 1, a:a + n]
                    k = 255 if dr == 0 else 0
                    dofin = (cj == 0 and o <= k < o + n)
                    xr_, xi_, t5, t6 = (sp[:, dr, 18 + q, st:st + 1] for q in range(4))
                    self.tt(T[:, 0, 0:n], cc, zr, ALU.mult)
                    self.tt(T[:, 1, 0:n], ss, zi, ALU.mult)
                    self.tt(zr, ss, zr, ALU.mult)
                    self.tt(zi, cc, zi, ALU.mult)
                    self.tt(XR[:, 0, a:a + n], T[:, 0, 0:n], T[:, 1, 0:n], ALU.subtract if dr == 0 else ALU.add)
                    self.stt(XR[:, 1, a:a + n], zr, -1.0 if dr == 0 else 1.0, zi, ALU.mult, ALU.subtract)
                    if dofin:
                        col = ((si * 2 + l) * 2 + dr) * 16 + st
                        zrk, zik = Z[:, 0, a + k - o:a + k - o + 1], Z[:, 1, a + k - o:a + k - o + 1]
                        self.tt(xr_, T[:, 0, k - o:k - o + 1], T[:, 1, k - o:k - o + 1], ALU.subtract if dr == 0 else ALU.add)
                        if dr == 0:
                            self.tt(xi_, zrk, zik, ALU.add)
                        else:
                            self.tt(xi_, zik, zrk, ALU.subtract)
                        self.tt(t5, xr_, P("fr"), ALU.mult)
                        self.tt(t6, xi_, P("fi"), ALU.mult)
                        self.tt(fin[:, 0, col:col + 1], t5, t6, ALU.subtract)
                        self.tt(t5, xr_, P("fi"), ALU.mult)
                        self.tt(t6, xi_, P("fr"), ALU.mult)
                        self.tt(fin[:, 1, col:col + 1], t5, t6, ALU.add)
            for ti, (t0, tn) in enumerate(TTS):
                first = (sti == 0 and dr == 0)
                lastm = (sti == 3 and dr == 1)
                self.mm(ybanks[ti].all(), c16[:, 0, :], XR[:, 0, t0:t0 + tn], start=first, stop=False)
                self.mm(ybanks[ti].all(), c16[:, 1, :], XR[:, 1, t0:t0 + tn], start=False, stop=lastm)

        tables_a(0)
        tables_b(0)
        for i, (c, sti, dr) in enumerate(iters):
            if sti == 0 and dr == 0:
                def ev(j, ti, pb):
                    t0, tn = TTS[ti]
                    self.cp(U[:, t0:t0 + tn], pb.all(), eng="act")
                self.proj(d["w_in"][l], 2064 + c * 128, 1, ev)
            if i + 1 < len(iters):
                tables_a(i + 1)
            compute(i)
            if sti == 3 and dr == 1:
                for ti, (t0, tn) in enumerate(TTS):
                    self.stt(T[:, 0, :], U[:, t0:t0 + tn], self.fmv(l, "ssmd", c), ybanks[ti].all(), ALU.mult, ALU.add)
                    self.act(Yb[:, c, t0:t0 + tn], T[:, 0, :], AF.Gelu)
        self.nb = 8
        if self.dump(f"yb{l}", Yb[:, 0, :]):
            raise Stop()

        def evg(j, ti, pb):
            t0, tn = TTS[ti]
            self.act(T[:, 0, :], pb.all(), AF.Sigmoid, bias=self.fmv(l, "glub", j))
            self.tt(self.o_ssm[:, j, t0:t0 + tn], T[:, 0, :], Yb[:, j, t0:t0 + tn], ALU.mult)
        self.proj(d["ssm_glu_w"][l], 0, 4, evg, nk=4, rhs=Yb)

    def pnorm(self, dst, src, ones_bf, eps_col, sqs, rs):
        for ti, (t0, tn) in enumerate(TTS):
            self.act(sqs.all(), src[:, t0:t0 + tn], AF.Square)
            pb = self.bank()
            self.mm(pb.all(), ones_bf.all(), sqs.all())
            self.act(rs[:, t0:t0 + tn], pb.all(), AF.Ln, bias=eps_col)
        self.act(rs.all(), rs.all(), AF.Exp, scale=-0.5)
        self.tt(dst.all(), src.all(), rs.all(), ALU.mult)

    def dn_gates(self, l):
        d = self.dram
        w = self.wload(d["w_in"][l], 0, 8, 2048, 16)
        pb = self.bank()
        for n in range(NCH):
            for kc in range(8):
                self.mm(pb[:, n * 16:(n + 1) * 16], self.H[:, kc, n * 128:(n + 1) * 128], w[:, kc, 0:16],
                        start=(kc == 0), stop=(kc == 7))
        ABf = self.AB.all().with_ap(self.AB.all().ap.rearrange("p a b -> p (a b)"))
        self.cp(ABf, pb[:, 0:192])
        alog = self.bc[:, l * 16:l * 16 + 8]
        dtb = self.bc[:, l * 16 + 8:l * 16 + 16]
        self.aoff = self.mix_base
        nea = self.carve("nea", [128, 8])
        tmp = self.carve("gtmp", [128, 12, 8])
        self.act(nea.all(), alog, AF.Exp)
        self.ts(nea.all(), nea.all(), -1.0, None, ALU.mult)
        for n in range(NCH):
            self.tt(tmp[:, n, :], self.AB[:, n, 0:8], dtb, ALU.add)
        tf = tmp.all().with_ap(tmp.all().ap.rearrange("p a b -> p (a b)"))
        self.act(tf, tf, AF.Exp)
        self.act(tf, tf, AF.Ln, bias=self.ones[:, 0:1])
        for n in range(NCH):
            self.tt(self.GT[:, n, :], tmp[:, n, :], nea.all(), ALU.mult)
            self.act(self.BT[:, n, :], self.AB[:, n, 8:16], AF.Sigmoid)

    def dn_head(self, l, hd):
        fw, d = self.fw, self.dram
        self.aoff = self.dn_base
        TMP = self.carve("TMPd", [128, NT])
        ZS = self.carve("ZS", [128, NT], BF16)
        SCR = self.carve("SCR", [128, NT])
        SQ = self.carve("SQd", [128, 512], BF16)
        Qb = self.carve("Qb", [128, NT], BF16)
        Kb = self.carve("Kb", [128, NT], BF16)
        Vb = self.carve("Vb", [128, NT], BF16)
        Oacc = Tile("Oacc", [128, 12, 128], F32, SCR.h.rearrange("p (a b) -> p a b", a=12), track=self.arena, base=SCR.base)
        w_in = d["w_in"][l]
        SCR2 = self.carve("SCR2", [128, NT])
        TMP2 = self.carve("TMP2", [128, NT])
        SQ2 = self.carve("SQd2", [128, 512], BF16)

        def chain(which, dst, SCRx, TMPx, SQx):
            w = self.wload(w_in, 0, 8, which * 512 + hd * 128, 128)
            yield
            for ti, (t0, tn) in enumerate(TTS):
                pb = self.bank()
                for kc in range(8):
                    self.mm(pb.all(), w[:, kc, 0:128], self.H[:, kc, t0:t0 + tn], start=(kc == 0), stop=(kc == 7))
                if which == 3:
                    self.act(ZS[:, t0:t0 + tn], pb.all(), AF.Silu)
                else:
                    self.cp(SCRx[:, t0:t0 + tn], pb.all(), eng="act")
                yield
            if which == 3:
                return
            ch = which * 4 + hd
            w3 = lambda k: self.fmv(l, "dnconv", ch * 3 + k)
            self.act(TMPx.all(), SCRx.all(), AF.Identity, scale=w3(1))
            yield
            for (s0, L, _) in SEGS:
                self.stt(TMPx[:, s0 + 1:s0 + L], SCRx[:, s0:s0 + L - 1], w3(0), TMPx[:, s0 + 1:s0 + L], ALU.mult, ALU.add)
            for (s0, L, _) in SEGS:
                self.stt(TMPx[:, s0:s0 + L - 1], SCRx[:, s0 + 1:s0 + L], w3(2), TMPx[:, s0:s0 + L - 1], ALU.mult, ALU.add)
            yield
            if which == 2:
                self.act(dst.all(), TMPx.all(), AF.Silu)
                return
            self.act(TMPx.all(), TMPx.all(), AF.Silu)
            yield
            ones_bf, eps_col = (self.ones128b, self.epsc[:, 1:2]) if which == 0 else (self.ones1b, self.epsc[:, 0:1])
            for ti, (t0, tn) in enumerate(TTS):
                self.act(SQx.all(), TMPx[:, t0:t0 + tn], AF.Square)
                pb = self.bank()
                self.mm(pb.all(), ones_bf.all(), SQx.all())
                self.act(SCRx[:, t0:t0 + tn], pb.all(), AF.Ln, bias=eps_col)
                yield
            self.act(SCRx.all(), SCRx.all(), AF.Exp, scale=-0.5)
            yield
            self.tt(dst.all(), TMPx.all(), SCRx.all(), ALU.mult)

        bufs = [(SCR, TMP, SQ), (SCR2, TMP2, SQ2)]
        pending = [(0, Qb), (1, Kb), (2, Vb)]
        active = [(chain(3, None, None, None, None), None)]
        ptick = 0
        while pending or active:
            ptick += 1
            if pending and bufs and ptick % 2 == 1:
                which, dst = pending.pop(0)
                bset = bufs.pop(0)
                active.append((chain(which, dst, *bset), bset))
            for item in list(active):
                try:
                    next(item[0])
                except StopIteration:
                    active.remove(item)
                    if item[1] is not None:
                        bufs.append(item[1])
        if self.dump(f"q{l}{hd}", Qb.all()) or self.dump(f"k{l}{hd}", Kb.all()):
            return
        for si in range(3):
            for dr in range(2):
                if SEGS[si][2] == 1:
                    fw.dma("sp", self.S[si][dr].all(), d["sdn"][l, dr, hd])
                else:
                    self.memset(self.S[si][dr].all(), 0.0)
                self.cp(self.Sb[si][dr].all(), self.S[si][dr].all(), eng="act")
        self.dstop("dnS0", self.S[2][0].all())
        NS = 5
        slots = []
        for i in range(NS):
            sl = {}
            sl["R1"] = self.carve(f"R1_{i}", [128, 2, 128])
            sl["R2"] = self.carve(f"R2_{i}", [128, 2, 128])
            for nm in ("Egc", "U"):
                sl[nm] = self.carve(f"{nm}{i}", [128, 128])
            for nm in ("AB", "TT"):
                sl[nm] = self.carve(f"{nm}{i}", [128, 2, 128])
            for nm in ("LL", "YW", "TTb"):
                sl[nm] = self.carve(f"{nm}{i}", [128, 2, 128], BF16)
            for nm in ("QgT", "QKmT", "Kd", "WT", "Vn"):
                sl[nm] = self.carve(f"{nm}{i}", [128, 128], BF16)
            sl["Gb"] = Tile(f"Gb{i}", [128, 128], F32, sl["R2"].h[:, 0, :], track=self.arena, base=sl["R2"].base)
            sl["Xs"] = self.carve(f"Xs{i}", [128, 256], BF16)
            sl["col"] = self.carve(f"col{i}", [128, 16])
            slots.append(sl)
        cm = self.cm
        ident = self.ident
        seqch = [(0, 2), (2, 2), (4, 8)]
        done = {}

        def mid2(v):
            ap = v.ap
            pat = [list(p) for p in ap.ap]
            return v.with_ap(bass.AP(ap.tensor, ap.offset, [pat[0], [0, 2], pat[1]]))

        def fl2(t):
            return t.all().with_ap(t.all().ap.rearrange("p a b -> p (a b)"))

        freeb = list(range(8))

        def problem(si, n, dr, sl):
            t0 = n * 128
            gi = dr * 4 + hd
            gcol = self.GT[:, n, gi:gi + 1]
            bcol = self.BT[:, n, gi:gi + 1]
            tri = cm["triF" if dr == 0 else "triB"]
            neg = cm["negF" if dr == 0 else "negB"]
            pos = cm["posF" if dr == 0 else "posB"]
            last = 127 if dr == 0 else 0
            col = sl["col"]
            gc, ngc, egc, bg, kd, gl, egl = (col[:, i:i + 1] for i in range(7))
            kbc, qbc, vbc = Kb[:, t0:t0 + 128], Qb[:, t0:t0 + 128], Vb[:, t0:t0 + 128]
            Kt, Vt = sl["R1"][:, 0, :], sl["R1"][:, 1, :]
            DcT, DcS = sl["R2"][:, 0, :], sl["R2"][:, 1, :]
            while not freeb:
                yield
            pg = self.ps[freeb.pop(0)]
            self.mm(pg[:, 0:128], kbc, self.identb.all())
            self.mm(pg[:, 128:256], vbc, self.identb.all())
            self.mm(pg[:, 256:384], kbc, kbc)
            self.mm(pg[:, 384:512], kbc, qbc)
            Gb = sl["Gb"]
            self.act(Gb.all(), self.ones.all(), AF.Identity, scale=gcol)
            while not freeb:
                yield
            pb = self.ps[freeb.pop(0)]
            self.mm(pb[:, 0:128], Gb.all(), tri.all())
            self.mm(pb[:, 384:385], tri.all(), gcol)
            yield
            self.cp(sl["R1"].all().with_ap(sl["R1"].all().ap.rearrange("p a b -> p (a b)")), pg[:, 0:256], eng="act")
            self.cp(gc, pb[:, 384:385])
            self.ts(ngc, pb[:, 384:385], -1.0, None, ALU.mult)
            self.cp(gl, pb[:, last:last + 1])
            self.act(egl, pb[:, last:last + 1], AF.Exp)
            self.act(sl["Egc"].all(), pb[:, 0:128], AF.Exp)
            self.tt(DcT, pb[:, 0:128], neg.all(), ALU.add)
            self.tt(DcS, pb[:, 0:128], pos.all(), ALU.add)
            self.act(DcT, DcT, AF.Exp, bias=ngc)
            self.act(DcS, DcS, AF.Exp, bias=gc, scale=-1.0)
            freeb.append(self.ps.index(pb))
            self.act(egc, gc, AF.Exp)
            self.tt(bg, bcol, egc, ALU.mult)
            self.act(kd, gc, AF.Exp, bias=gl, scale=-1.0)
            yield
            AB = sl["AB"]
            self.stt(AB[:, 0, :], DcS, bcol, pg[:, 256:384], ALU.mult, ALU.mult)
            self.tt(sl["QKmT"].all(), DcT, pg[:, 384:512], ALU.mult)
            freeb.append(self.ps.index(pg))
            while not freeb:
                yield
            pb = self.ps[freeb.pop(0)]
            self.tr(pb[:, 0:128], AB[:, 0, :])
            X = sl["Xs"]
            self.act(X[:, 0:128], Vt, AF.Identity, scale=bcol)
            self.act(X[:, 128:256], Kt, AF.Identity, scale=bg)
            self.tt(sl["QgT"].all(), qbc, sl["Egc"].all(), ALU.mult, eng="pool")
            self.act(sl["Kd"].all(), Kt, AF.Identity, scale=kd)
            yield
            self.cp(AB[:, 1, :], pb[:, 0:128], eng="act")
            freeb.append(self.ps.index(pb))
            TT, LL, YW, TTb = sl["TT"], sl["LL"], sl["YW"], sl["TTb"]
            self.tt(TT.all(), AB.all(), mid2(self.lvm[:, 0, :]), ALU.mult, eng="pool")
            self.tt(TTb.all(), mid2(ident.all()), TT.all(), ALU.subtract, eng="pool")
            self.tt(TT.all(), mid2(ident.all()), TT.all(), ALU.subtract, eng="pool")
            for lev in range(1, 7):
                self.tt(LL.all(), AB.all(), mid2(self.lvm[:, lev, :]), ALU.mult, eng="pool")
                yield
                while not freeb:
                    yield
                p1 = self.ps[freeb.pop(0)]
                self.mm(p1[:, 0:128], LL[:, 1, :], TTb[:, 0, :])
                self.mm(p1[:, 128:256], LL[:, 0, :], TTb[:, 1, :])
                yield
                self.cp(fl2(YW), p1[:, 0:256], eng="act")
                yield
                p2 = p1
                self.mm(p2[:, 256:384], TTb[:, 1, :], YW[:, 0, :])
                self.mm(p2[:, 384:512], TTb[:, 0, :], YW[:, 1, :])
                yield
                self.tt(fl2(TTb), fl2(TT), p2[:, 256:512], ALU.subtract)
                if lev < 6:
                    self.tt(fl2(TT), fl2(TT), p2[:, 256:512], ALU.subtract)
                freeb.append(self.ps.index(p2))
            yield
            while not freeb:
                yield
            pu = self.ps[freeb.pop(0)]
            self.mm(pu[:, 0:128], TTb[:, 1, :], X[:, 0:128])
            self.mm(pu[:, 128:256], X[:, 128:256], TTb[:, 1, :])
            yield
            self.cp(sl["U"].all(), pu[:, 0:128], eng="act")
            self.cp(sl["WT"].all(), pu[:, 128:256], eng="act")
            freeb.append(self.ps.index(pu))
            nloc = seqch[si][1]
            m = n - seqch[si][0]
            k = m if dr == 0 else nloc - 1 - m
            while done.get((si, dr), 0) < k:
                yield
            S = self.S[si][dr]
            Sb = self.Sb[si][dr]
            while not freeb:
                yield
            pa = self.ps[freeb.pop(0)]
            self.mm(pa[:, 0:128], sl["WT"].all(), Sb.all())
            yield
            self.tt(sl["Vn"].all(), sl["U"].all(), pa[:, 0:128], ALU.subtract)
            yield
            while not freeb:
                yield
            po = self.ps[freeb.pop(0)]
            self.mm(po[:, 0:128], sl["QgT"].all(), Sb.all(), start=True, stop=False)
            self.mm(po[:, 0:128], sl["QKmT"].all(), sl["Vn"].all(), start=False, stop=True)
            self.mm(pa[:, 256:384], sl["Kd"].all(), sl["Vn"].all())
            yield
            first = (m < nloc - 1 - m) if dr == 0 else (nloc - 1 - m < m)
            if first:
                self.cp(Oacc[:, n, :], po[:, 0:128], eng="act")
            else:
                self.tt(Oacc[:, n, :], Oacc[:, n, :], po[:, 0:128], ALU.add)
            Sn = self.S2[si][dr]
            self.stt(Sn.all(), S.all(), egl, pa[:, 256:384], ALU.mult, ALU.add)
            self.cp(Sb.all(), Sn.all(), eng="act")
            freeb.append(self.ps.index(pa))
            freeb.append(self.ps.index(po))
            self.S[si][dr], self.S2[si][dr] = Sn, S
            done[(si, dr)] = k + 1

        queue = []
        for step in range(8):
            for si, (c0, nc_) in enumerate(seqch):
                if step < nc_:
                    queue.append((si, c0 + step, 0))
                    queue.append((si, c0 + nc_ - 1 - step, 1))
        active = []
        free = list(range(NS))
        tick = 0
        STG = 2
        while queue or active:
            tick += 1
            if queue and free and tick % STG == 0:
                si, n, dr = queue.pop(0)
                i = free.pop(0)
                active.append((problem(si, n, dr, slots[i]), i))
            for item in list(active):
                try:
                    next(item[0])
                except StopIteration:
                    active.remove(item)
                    free.append(item[1])
        for si in range(2):
            for dr in range(2):
                fw.dma("sp", d["dn_out"][si, l, dr, hd], self.S[si][dr].all())
        for n in range(NCH):
            if n % 4 == 0:
                pb = self.bank()
            self.tr(pb[:, (n % 4) * 128:(n % 4 + 1) * 128], Oacc[:, n, :])
            if n % 4 == 3:
                self.cp(TMP[:, (n - 3) * 128:(n + 1) * 128], pb.all(), eng="act")
        self.pnorm(TMP, TMP, self.onesi128b, self.epsc[:, 0:1], SQ, SCR)
        self.stt(self.o_dn[:, hd, :], TMP.all(), self.fmv(l, "dnng", 0), ZS.all(), ALU.mult, ALU.mult)

    def pool(self, l):
        d = self.dram
        self.aoff = self.mix_base
        UP = self.carve("UP", [128, NT])
        PA = self.carve("PA", [128, 1024 + 32])
        PB_ = self.carve("PB", [128, 1024 + 32])
        PL = self.carve("PL", [128, NT], BF16)
        pw = self.pwt
        for gi in range(4):
            w = 2 << gi

            def ev(j, ti, pb):
                t0, tn = TTS[ti]
                self.cp(UP[:, t0:t0 + tn], pb.all(), eng="act")
            self.proj(d["w_in"][l], 2576 + gi * 128, 1, ev)
            for si, (s0, L, cj) in enumerate(SEGS):
                self.memset(PA.all(), 0.0)
                self.memset(PB_.all(), 0.0)
                self.cp(PA[:, 16:16 + L], UP[:, s0:s0 + L])
                cur, oth = PA, PB_
                m = 1
                while m < w:
                    self.tt(oth[:, 16:32 + L], cur[:, 16:32 + L], cur[:, 16 - m:32 + L - m], ALU.add)
                    cur, oth = oth, cur
                    m *= 2
                sh = w // 2 - 1
                hw = w // 2
                self.ts(oth[:, 16:16 + L], cur[:, 16 + sh:16 + sh + L], 1.0 / w, None, ALU.mult)
                self.tt(oth[:, 16:16 + hw], cur[:, 16 + sh:16 + sh + hw], self.rcnt[:, 0, gi * 16:gi * 16 + hw], ALU.mult)
                if hw > 1:
                    a = L - hw + 1
                    self.tt(oth[:, 16 + a:16 + L], cur[:, 16 + sh + a:16 + sh + L], self.rcnt[:, 1, gi * 16:gi * 16 + hw - 1], ALU.mult)
                self.tt(PL[:, s0:s0 + L], oth[:, 16:16 + L], UP[:, s0:s0 + L], ALU.subtract)
            self.fw.dma("pool", pw.all(), d["pool_w"][l, gi])
            for ti, (t0, tn) in enumerate(TTS):
                pb = self.bank()
                self.mm(pb.all(), pw.all(), PL[:, t0:t0 + tn])
                self.act(self.o_pool[:, gi, t0:t0 + tn], pb.all(), AF.Identity, scale=self.fmv(l, "pscale", gi))

    def merge(self, l):
        d = self.dram
        self.aoff = self.mix_base
        MG = self.carve("MG", [128, 8, NT], BF16)
        SG = self.carve("SG", [128, 512])
        TM = self.carve("TM", [128, 512])
        srcs = [(self.o_dn, d["w_branch_dn"][l]), (self.o_ssm, d["w_branch_ssm"][l]), (self.o_pool, d["w_branch_pool"][l])]
        for b, (osrc, wb) in enumerate(srcs):
            for j in range(8):
                wg = self.wload(d["w_in"][l], 0, 8, 3088 + b * 1024 + j * 128, 128)
                ww = self.wload(wb, 0, 4, j * 128, 128)
                for ti, (t0, tn) in enumerate(TTS):
                    pg, pv = self.bank(), self.bank()
                    for kc in range(8):
                        self.mm(pg.all(), wg[:, kc, 0:128], self.H[:, kc, t0:t0 + tn], start=(kc == 0), stop=(kc == 7))
                    for kc in range(4):
                        self.mm(pv.all(), ww[:, kc, 0:128], osrc[:, kc, t0:t0 + tn], start=(kc == 0), stop=(kc == 3))
                    self.act(SG.all(), pg.all(), AF.Sigmoid)
                    if b == 0:
                        self.tt(MG[:, j, t0:t0 + tn], SG.all(), pv.all(), ALU.mult)
                    else:
                        self.tt(TM.all(), SG.all(), pv.all(), ALU.mult)
                        self.tt(MG[:, j, t0:t0 + tn], MG[:, j, t0:t0 + tn], TM.all(), ALU.add)

        def ev(j, ti, pb):
            t0, tn = TTS[ti]
            for (cj, a, b_) in ((0, 0, 512), (1, 512, NT)):
                lo, hi = max(a, t0), min(b_, t0 + tn)
                if lo < hi:
                    self.stt(self.X[:, j, lo:hi], pb[:, lo - t0:hi - t0], self.mods[:, 2, j, cj:cj + 1], self.X[:, j, lo:hi],
                             ALU.mult, ALU.add)
        self.proj(d["w_out"][l], 0, 8, ev, rhs=MG)

    def ffn(self, l):
        d = self.dram
        self.aoff = 0
        ACTT = self.carve("ACTT", [128, 22, NT], BF16)
        G0s = [self.carve(f"G0{i}", [128, NT]) for i in range(2)]
        G1 = self.carve("G1", [128, NT])
        V1 = self.carve("V1", [128, NT])
        wup = d["ffn_w_up"][l]

        def half(j, isval, G0):
            col = (D_FF if isval else 0) + j * 128
            w = self.wload(wup, 0, 8, col, 128)
            yield
            for ti, (t0, tn) in enumerate(TTS):
                pb = self.bank()
                for kc in range(8):
                    self.mm(pb.all(), w[:, kc, 0:128], self.H[:, kc, t0:t0 + tn], start=(kc == 0), stop=(kc == 7))
                self.cp(G0[:, t0:t0 + tn], pb.all(), eng="act")
                yield
            dst = V1 if isval else G1
            ch = (22 if isval else 0) + j
            w3 = lambda k: self.fmv(l, "fconv", ch * 3 + k)
            self.act(dst.all(), G0.all(), AF.Identity, scale=w3(1))
            yield
            for (s0, L, _) in SEGS:
                self.stt(dst[:, s0 + 1:s0 + L], G0[:, s0:s0 + L - 1], w3(0), dst[:, s0 + 1:s0 + L], ALU.mult, ALU.add)
            yield
            for (s0, L, _) in SEGS:
                self.stt(dst[:, s0:s0 + L - 1], G0[:, s0 + 1:s0 + L], w3(2), dst[:, s0:s0 + L - 1], ALU.mult, ALU.add)
            yield
            if not isval:
                self.act(G1.all(), G1.all(), AF.Silu)
            else:
                self.tt(ACTT[:, j, :], G1.all(), V1.all(), ALU.mult)

        work = []
        for j in range(22):
            work.append((j, False))
            work.append((j, True))
        bufs = list(G0s)
        active = []
        ftick = 0
        while work or active:
            ftick += 1
            if work and bufs and ftick % 2 == 1:
                j, isval = work.pop(0)
                g0 = bufs.pop(0)
                active.append((half(j, isval, g0), g0))
            for item in list(active):
                try:
                    next(item[0])
                except StopIteration:
                    active.remove(item)
                    bufs.append(item[1])
        wd = d["ffn_w_down"][l]
        for j in range(8):
            ws_ = [self.wload(wd, 0, 8, j * 128, 128), self.wload(wd, 1024, 8, j * 128, 128), self.wload(wd, 2048, 6, j * 128, 128)]
            for ti, (t0, tn) in enumerate(TTS):
                pb = self.bank()
                for kc in range(22):
                    w = ws_[kc // 8]
                    self.mm(pb.all(), w[:, kc % 8, 0:128], ACTT[:, kc, t0:t0 + tn], start=(kc == 0), stop=(kc == 21))
                for (cj, a, b_) in ((0, 0, 512), (1, 512, NT)):
                    lo, hi = max(a, t0), min(b_, t0 + tn)
                    if lo < hi:
                        self.stt(self.X[:, j, lo:hi], pb[:, lo - t0:hi - t0], self.mods[:, 5, j, cj:cj + 1], self.X[:, j, lo:hi],
                                 ALU.mult, ALU.add)

    def final(self):
        fw, d = self.fw, self.dram
        self.aoff = 0
        sq = self.carve("fsq", [128, 8, 512], BF16)
        rs = self.carve("frs", [128, NT])
        Y = self.carve("Y", [128, 8, 512])
        stg = [self.carve(f"ostg{i}", [128, 1024]) for i in range(2)]
        for ti, (t0, tn) in enumerate(TTS):
            self.act(sq.all(), self.X[:, :, t0:t0 + tn], AF.Square)
            pb = self.bank()
            for c in range(8):
                self.mm(pb.all(), self.onesb.all(), sq[:, c, :], start=(c == 0), stop=(c == 7))
            self.act(rs[:, t0:t0 + tn], pb.all(), AF.Ln, bias=self.epsc[:, 0:1])
        self.act(rs.all(), rs.all(), AF.Exp, scale=-0.5)
        fo = FM_L * DEPTH
        for ti, (t0, tn) in enumerate(TTS):
            for c in range(8):
                self.stt(Y[:, c, :], self.X[:, c, t0:t0 + tn], self.fm[:, fo + c:fo + c + 1], rs[:, t0:t0 + tn], ALU.mult, ALU.mult)
            for b in range(4):
                s = stg[b % 2]
                for half in range(2):
                    pb = self.bank()
                    for cc in range(4):
                        c = half * 4 + cc
                        self.tr(pb[:, cc * 128:(cc + 1) * 128], Y[:, c, b * 128:(b + 1) * 128])
                    self.cp(s[:, half * 512:(half + 1) * 512], pb.all(), eng="act" if half else "dve")
                r0 = t0 + b * 128
                fw.dma("sp", d["y"][r0:r0 + 128, :], s.all())
        for ri, nm in ((0, "ssm_re_out"), (1, "ssm_im_out")):
            pb = self.bank()
            self.tr(pb[:, 0:128], self.fin[:, ri, :])
            s = stg[ri]
            self.cp(s[:, 0:128], pb[:, 0:128])
            fw.dma("sp", d[nm], s[:, 0:128])

    def build(self):
        try:
            self.build_()
        except Stop:
            pass

    def dstop(self, name, view):
        if self.dump(name, view):
            raise Stop()

    def build_(self):
        self.setup()
        self.load_x()
        if self.dump("x0", self.X[:, 0, :]):
            return
        for l in range(DEPTH):
            self.ada(l)
            if self.dump(f"mod{l}", self.modt.all().with_ap(self.modt.all().ap.rearrange("p a b -> p (a b)"))):
                return
            self.norm_mod(0)
            if self.dump(f"h{l}", self.H[:, 0, :]):
                return
            self.mixer_layout()
            self.dn_gates(l)
            for hd in range(4):
                self.dn_head(l, hd)
                if self.dbg in (f"q{l}{hd}", f"k{l}{hd}"):
                    return
                if self.dump(f"odn{l}{hd}", self.o_dn[:, hd, :]):
                    return
            if SKIP_MIX:
                self.memset(self.o_ssm.all(), 0.0)
                self.memset(self.o_pool.all(), 0.0)
            else:
                self.s5(l)
                if self.dump(f"ossm{l}", self.o_ssm[:, 0, :]):
                    return
                self.pool(l)
                if self.dump(f"opool{l}", self.o_pool[:, 0, :]):
                    return
            self.merge(l)
            if self.dump(f"xm{l}", self.X[:, 0, :]):
                return
            self.norm_mod(1)
            self.ffn(l)
            if self.dump(f"xf{l}", self.X[:, 0, :]):
                return
        self.final()


def _pos_embed():
    rows, dim, gw = 16, D, 64
    q = dim // 4
    omega = (1.0 / (10000.0 ** (np.arange(q, dtype=np.float32) / np.float32(q)))).astype(np.float32)
    r = np.repeat(np.arange(rows, dtype=np.float32), gw)
    col = np.tile(np.arange(gw, dtype=np.float32), rows)

    def sc(p):
        ang = (p[:, None] * omega[None, :]).astype(np.float32)
        return np.concatenate([np.sin(ang), np.cos(ang)], axis=-1)
    return np.concatenate([sc(r), sc(col)], axis=-1).astype(np.float32)


def _consts():
    c = np.zeros((128, CST_N), np.float32)
    i = np.arange(128)
    P, Fr = i[:, None], i[None, :]
    c[:, 0:128] = np.eye(128)
    c[:, 128:256] = (P <= Fr)
    c[:, 256:384] = (P >= Fr)
    c[:, 384:512] = np.where(Fr >= P, 0.0, -BIG)
    c[:, 512:640] = np.where(Fr <= P, 0.0, -BIG)
    c[:, 640:768] = np.where(Fr < P, 0.0, BIG)
    c[:, 768:896] = np.where(Fr > P, 0.0, BIG)
    c[:, 896:896 + 1026] = np.arange(1026)[None, :]
    o = 896 + 1026
    for gi in range(4):
        w = 2 << gi
        hw = w // 2
        for k in range(hw):
            c[:, o + gi * 16 + k] = 1.0 / (k + hw)
        for k in range(hw - 1):
            c[:, o + 128 + gi * 16 + k] = 1.0 / (w - 1 - k)
    o += 256
    for k in range(7):
        c[:, o + k * 128:o + (k + 1) * 128] = ((P >> (k + 1)) == (Fr >> (k + 1))) & ((P >> k) != (Fr >> k))
    return c


def _fm(a):
    return np.ascontiguousarray(a.reshape(-1, 128).T)


def _build(dbg=None, dbg_shape=None):
    nc = bass.Bass("TRN2", target_bir_lowering=False)
    dram = {}

    def inp(name, shape):
        dram[name] = nc.dram_tensor(name, list(shape), F32, kind="ExternalInput").ap()

    def outp(name, shape):
        dram[name] = nc.dram_tensor(name, list(shape), F32, kind="ExternalOutput").ap()
    inp("xin", [NT, D]); inp("pe", [1024, D]); inp("cond", [128, 8, 2]); inp("cst", [128, CST_N])
    inp("fm", [128, FM_TOT]); inp("bcp", [1, 32]); inp("sm", [128, 192]); inp("sdn", [2, 2, 4, 128, 128])
    inp("sre", [2, 128, 2, 16]); inp("sim", [2, 128, 2, 16])
    inp("w_ada", [2, D, 6 * D]); inp("w_in", [2, D, IN_COLS]); inp("w_branch_dn", [2, 512, D])
    inp("w_branch_ssm", [2, 512, D]); inp("w_branch_pool", [2, 512, D]); inp("w_out", [2, D, D])
    inp("ffn_w_up", [2, D, 2 * D_FF]); inp("ffn_w_down", [2, D_FF, D]); inp("ssm_glu_w", [2, 512, 512])
    inp("pool_w", [2, 4, 128, 128]); inp("bblk", [2, 16, 128, 2, 128]); inp("cblk", [2, 16, 128, 2, 128])
    outp("y", [NT, D]); outp("dn_out", [2, 2, 2, 4, 128, 128]); outp("ssm_re_out", [128, 128]); outp("ssm_im_out", [128, 128])
    if dbg:
        outp("dbg", list(dbg_shape))
    st = ExitStack()
    kb = KB(nc, st, dram, dbg=dbg)
    kb.build()
    kb.fw.emit()
    return nc, st, kb


def _prep(inputs):
    g = {k: np.asarray(v) for k, v in inputs.items()}
    f32 = np.float32
    shared = {}
    for k in ("w_ada", "w_in", "w_branch_dn", "w_branch_ssm", "w_branch_pool", "w_out", "ffn_w_up", "ffn_w_down",
              "ssm_glu_w", "pool_w"):
        shared[k] = np.ascontiguousarray(g[k], dtype=f32)
    shared["pe"] = _pos_embed()
    shared["cst"] = _consts()
    fm = np.zeros((128, FM_TOT), f32)
    for l in range(DEPTH):
        b = l * FM_L
        fm[:, b + FM_OFF["n1g"]:b + FM_OFF["n1g"] + 16] = np.repeat(_fm(g["norm1_g"][l]), 2, axis=1)
        fm[:, b + FM_OFF["n2g"]:b + FM_OFF["n2g"] + 16] = np.repeat(_fm(g["norm2_g"][l]), 2, axis=1)
        fm[:, b + FM_OFF["bada"]:b + FM_OFF["bada"] + 96] = np.repeat(_fm(g["b_ada"][l]), 2, axis=1)
        dc = g["dn_conv"][l]
        fm[:, b + FM_OFF["dnconv"]:b + FM_OFF["dnconv"] + 36] = dc.reshape(3, 12, 128).transpose(2, 1, 0).reshape(128, 36)
        fm[:, b + FM_OFF["dnng"]] = g["dn_norm_g"][l]
        fm[:, b + FM_OFF["ssmd"]:b + FM_OFF["ssmd"] + 4] = _fm(g["ssm_d"][l])
        fm[:, b + FM_OFF["glub"]:b + FM_OFF["glub"] + 4] = _fm(g["ssm_glu_b"][l])
        fm[:, b + FM_OFF["pscale"]:b + FM_OFF["pscale"] + 4] = _fm(g["pool_scale"][l])
        fc = g["ffn_conv"][l]
        fm[:, b + FM_OFF["fconv"]:b + FM_OFF["fconv"] + 132] = fc.reshape(3, 44, 128).transpose(2, 1, 0).reshape(128, 132)
    fm[:, FM_L * DEPTH:] = _fm(g["final_norm_g"])
    shared["fm"] = fm
    bcp = np.zeros((1, 32), f32)
    sm = np.zeros((128, 192), f32)
    for l in range(DEPTH):
        bcp[0, l * 16:l * 16 + 8] = g["dn_a_log"][l].reshape(8)
        bcp[0, l * 16 + 8:l * 16 + 16] = g["dn_dt_bias"][l].reshape(8)
        for dr in range(2):
            base = ((l * 2 + dr) * 3) * 16
            sm[:, base:base + 16] = g["ssm_lambda_re"][l, dr].reshape(16, 128).T
            sm[:, base + 16:base + 32] = g["ssm_lambda_im"][l, dr].reshape(16, 128).T
            sm[:, base + 32:base + 48] = np.repeat(g["ssm_log_step"][l, dr], 64).reshape(16, 128).T
    shared["bcp"], shared["sm"] = bcp, sm
    bblk = np.zeros((2, 16, 128, 2, 128), f32)
    cblk = np.zeros((2, 16, 128, 2, 128), f32)
    for l in range(DEPTH):
        for st_ in range(16):
            for gl in range(2):
                gg = 2 * st_ + gl
                k0 = 32 * (st_ % 4) + 16 * gl
                for ri, (bn, cn) in enumerate((("ssm_b_re", "ssm_c_re"), ("ssm_b_im", "ssm_c_im"))):
                    bblk[l, st_, k0:k0 + 16, ri, 64 * gl:64 * gl + 64] = g[bn][l, gg].T
                    cblk[l, st_, 64 * gl:64 * gl + 64, ri, k0:k0 + 16] = g[cn][l, gg].T
    shared["bblk"], shared["cblk"] = bblk, cblk
    maps = []
    for c in range(8):
        m = dict(shared)
        m["xin"] = np.ascontiguousarray(np.concatenate([g["x_prompt"][2 * c], g["x_prompt"][2 * c + 1], g["x_sample"][c]], axis=0), dtype=f32)
        cond = np.zeros((128, 8, 2), f32)
        cond[:, :, 0] = _fm(g["c_ctx"])
        cond[:, :, 1] = _fm(g["c"][c])
        m["cond"] = cond
        m["sdn"] = np.ascontiguousarray(g["state_dn"][c], dtype=f32)
        for nm, src in (("sre", "state_ssm_re"), ("sim", "state_ssm_im")):
            a = g[src][c].reshape(2, 2, 16, 128)
            m[nm] = np.ascontiguousarray(a.transpose(0, 3, 1, 2), dtype=f32)
        maps.append(m)
    return maps


_CACHE = {}


def kernel(**inputs):
    if "nc" not in _CACHE:
        _CACHE["nc"] = _build()
    nc, st, kb = _CACHE["nc"]
    maps = _prep(inputs)
    res = run_bass_kernel_spmd(nc, maps, core_ids=list(range(8)))
    R = res.results
    y_prompt = np.zeros((16, 256, D), np.float32)
    y_sample = np.zeros((8, 1024, D), np.float32)
    ndn = np.zeros((16, 2, 2, 4, 128, 128), np.float32)
    nre = np.zeros((16, 2, 2, 32, 64), np.float32)
    nim = np.zeros((16, 2, 2, 32, 64), np.float32)
    for c in range(8):
        y = R[c]["y"]
        y_prompt[2 * c] = y[0:256]
        y_prompt[2 * c + 1] = y[256:512]
        y_sample[c] = y[512:]
        ndn[2 * c:2 * c + 2] = R[c]["dn_out"]
        nre[2 * c:2 * c + 2] = R[c]["ssm_re_out"].reshape(2, 2, 2, 16, 128).reshape(2, 2, 2, 32, 64)
        nim[2 * c:2 * c + 2] = R[c]["ssm_im_out"].reshape(2, 2, 2, 16, 128).reshape(2, 2, 2, 32, 64)
    return (y_prompt, y_sample, ndn, nre, nim)
```

```python
import math
import os
from contextlib import ExitStack
import numpy as np
import concourse.bass as bass
import concourse.mybir as mybir
from concourse.bass_utils import run_bass_kernel_spmd

F32 = mybir.dt.float32
BF16 = mybir.dt.bfloat16
I32 = mybir.dt.int32
AF = mybir.ActivationFunctionType
ALU = mybir.AluOpType
SEM_ROLL = 30000
DSZ = {F32: 4, BF16: 2, I32: 4}


class View:
    __slots__ = ("ap", "tile", "p0", "p1", "lo", "hi")

    def __init__(self, ap, tile, p0, p1, lo, hi):
        self.ap, self.tile, self.p0, self.p1, self.lo, self.hi = ap, tile, p0, p1, lo, hi

    def with_ap(self, ap):
        return View(ap, self.tile, self.p0, self.p1, self.lo, self.hi)

    def rev(self):
        ap = self.ap
        pat = [list(p) for p in ap.ap]
        assert pat[-1][0] == 1
        off = ap.offset + (pat[-1][1] - 1)
        pat[-1][0] = -1
        return self.with_ap(bass.AP(ap.tensor, off, pat))

    def bcast(self, n):
        ap = self.ap
        pat = [list(p) for p in ap.ap]
        pat[-1] = [0, n]
        return self.with_ap(bass.AP(ap.tensor, ap.offset, pat))


class Tile:
    def __init__(self, name, shape, dtype, handle, track=None, base=0):
        self.name, self.shape, self.dtype, self.h = name, list(shape), dtype, handle
        self.esz = DSZ[dtype]
        st = [1] * len(shape)
        for i in range(len(shape) - 2, 0, -1):
            st[i] = st[i + 1] * shape[i + 1]
        self.strides = st
        self.track = track if track is not None else self
        self.base = base
        self.recs = []
        self.dma_in = 0
        self.dma_in_sem = None
        self.dma_out = 0
        self.dma_out_sem = None
        self.whole = False

    def __getitem__(self, key):
        if not isinstance(key, tuple):
            key = (key,)
        key = tuple(key) + (slice(None),) * (len(self.shape) - len(key))
        rng = []
        for k, n in zip(key, self.shape):
            if isinstance(k, slice):
                a, b, s = k.indices(n)
                assert s == 1 and b > a, (self.name, key)
                rng.append((a, b))
            else:
                assert 0 <= k < n, (self.name, key)
                rng.append((k, k + 1))
        p0, p1 = rng[0]
        lo = sum(r[0] * s for r, s in zip(rng[1:], self.strides[1:]))
        hi = sum((r[1] - 1) * s for r, s in zip(rng[1:], self.strides[1:])) + 1
        if self.whole:
            return View(self.h[key], self.track, 0, 128, 0, 1 << 20)
        return View(self.h[key], self.track, p0, p1, self.base + lo * self.esz, self.base + hi * self.esz)

    def all(self):
        return self[tuple(slice(None) for _ in self.shape)]


class Op:
    __slots__ = ("eng", "fn", "deps", "is_dma", "dma_sem_tile", "dma_kind", "need_inc", "val", "idx", "dma_waits",
                 "deps_need", "clk")


class FW:
    ENGS = ("pe", "act", "dve", "pool", "sp")

    def __init__(self, nc, stack):
        self.nc, self.stack = nc, stack
        self.ops, self.tiles = [], []

    def sbuf(self, name, shape, dtype=F32):
        h = self.stack.enter_context(self.nc.sbuf_tensor("t_" + name, list(shape), dtype))
        t = Tile(name, shape, dtype, h)
        self.tiles.append(t)
        return t

    def psum(self, name, shape, dtype=F32):
        h = self.stack.enter_context(self.nc.psum_tensor("t_" + name, list(shape), dtype))
        t = Tile(name, shape, dtype, h)
        t.whole = True
        self.tiles.append(t)
        return t

    def carve(self, arena, off_bytes, name, shape, dtype=F32):
        n = int(np.prod(shape[1:])) * DSZ[dtype]
        assert off_bytes % 4 == 0 and n % 4 == 0
        assert off_bytes + n <= arena.shape[1] * 4, (name, off_bytes, n)
        ap = arena.h[:, off_bytes // 4:(off_bytes + n) // 4]
        if dtype != F32:
            ap = ap.bitcast(dtype)
        if len(shape) == 3:
            ap = ap.rearrange("p (a b) -> p a b", a=shape[1])
        elif len(shape) == 4:
            ap = ap.rearrange("p (a b c) -> p a b c", a=shape[1], b=shape[2])
        return Tile(name, shape, dtype, ap, track=arena, base=off_bytes)

    @staticmethod
    def _ov(r, v):
        return r[0] < v.p1 and v.p0 < r[1] and r[2] < v.hi and v.lo < r[3]

    def _track(self, op, reads, writes, whole_tile_war=False):
        deps = []
        for v in reads:
            for r in v.tile.recs:
                if r[5] and self._ov(r, v):
                    deps.append(r[4])
                elif v.tile.whole and (not r[5]) and r[4].eng != op.eng:
                    deps.append(r[4])
        for v in writes:
            for r in v.tile.recs:
                if self._ov(r, v) or (whole_tile_war and not (r[5] and r[4].is_dma)):
                    deps.append(r[4])
        for v in reads:
            t = v.tile
            key = (v.p0, v.p1, v.lo, v.hi)
            new = [r for r in t.recs
                   if not ((not r[5]) and (not op.is_dma) and (not r[4].is_dma) and r[4].eng == op.eng and r[:4] == key)]
            new.append((v.p0, v.p1, v.lo, v.hi, op, False))
            t.recs = new
        for v in writes:
            t = v.tile
            new = [r for r in t.recs
                   if not (v.p0 <= r[0] and r[1] <= v.p1 and v.lo <= r[2] and r[3] <= v.hi)]
            new.append((v.p0, v.p1, v.lo, v.hi, op, True))
            t.recs = new
        return deps

    def _newop(self, eng, is_dma):
        op = Op()
        op.eng, op.is_dma, op.need_inc, op.dma_waits, op.idx = eng, is_dma, False, [], len(self.ops)
        op.val = 0
        return op

    def _finish(self, op, deps):
        cd = {}
        for d in deps:
            if d is op:
                continue
            if d.is_dma:
                t = d.dma_sem_tile
                op.dma_waits.append((t, d.dma_kind, t.dma_in if d.dma_kind == "in" else t.dma_out))
            else:
                if d.eng == "pe" and op.eng == "pe":
                    continue
                cd[d.idx] = d
        op.deps = list(cd.values())
        for d in op.deps:
            d.need_inc = True
        self.ops.append(op)

    def add(self, eng, fn, reads=(), writes=()):
        op = self._newop(eng, False)
        op.fn = fn
        self._finish(op, self._track(op, [r for r in reads if r is not None], writes))
        return op

    def dma(self, queue, out, in_, **kw):
        op = self._newop(queue, True)
        if isinstance(out, View):
            t = out.tile
            deps = self._track(op, [], [out], whole_tile_war=True)
            op.dma_sem_tile, op.dma_kind = t, "in"
            oap, iap = out.ap, in_
        else:
            t = in_.tile
            deps = [r[4] for r in t.recs if r[5]]
            t.recs.append((in_.p0, in_.p1, in_.lo, in_.hi, op, False))
            op.dma_sem_tile, op.dma_kind = t, "out"
            oap, iap = out, in_.ap
        self._finish(op, deps)
        if op.dma_kind == "in":
            t.dma_in += 1
        else:
            t.dma_out += 1
        op.fn = lambda eng: eng.dma_start(out=oap, in_=iap, **kw)
        return op

    def emit(self):
        nc = self.nc
        counts = {e: 0 for e in self.ENGS}
        for op in self.ops:
            if op.is_dma or not op.need_inc:
                continue
            counts[op.eng] += 1
            op.val = counts[op.eng]
        esems = {e: [self.stack.enter_context(nc.semaphore(f"s_{e}_{i}")) for i in range(counts[e] // SEM_ROLL + 1)]
                 for e in self.ENGS}
        for t in self.tiles:
            if t.dma_in:
                t.dma_in_sem = self.stack.enter_context(nc.semaphore(f"di_{t.name}"))
            if t.dma_out:
                t.dma_out_sem = self.stack.enter_context(nc.semaphore(f"do_{t.name}"))
        per_eng = {e: [] for e in self.ENGS}
        for op in self.ops:
            per_eng[op.eng].append(op)
        finals = [(t.dma_out_sem, 16 * t.dma_out) for t in self.tiles if t.dma_out]
        finals += [(t.dma_in_sem, 16 * t.dma_in) for t in self.tiles if t.dma_in]
        last = {e: counts[e] for e in self.ENGS}

        clock = {e: {} for e in self.ENGS}
        for op in self.ops:
            ck = clock[op.eng]
            need = {}
            for d in op.deps:
                si, v = divmod(d.val - 1, SEM_ROLL)
                key = ("e", d.eng, si)
                if need.get(key, (None, 0))[1] < v + 1:
                    need[key] = (esems[d.eng][si], v + 1)
            for (t, kind, cnt) in op.dma_waits:
                s_ = t.dma_in_sem if kind == "in" else t.dma_out_sem
                key = ("d", t.name, kind)
                if need.get(key, (None, 0))[1] < 16 * cnt:
                    need[key] = (s_, 16 * cnt)
            op.deps_need = [(key, s_, v) for key, (s_, v) in need.items() if ck.get(key, 0) < v]
            for key, (s_, v) in need.items():
                if ck.get(key, 0) < v:
                    ck[key] = v
            for d in op.deps:
                for k2, v2 in d.clk.items():
                    if ck.get(k2, 0) < v2:
                        ck[k2] = v2
            if op.is_dma:
                op.clk = {}
            else:
                op.clk = dict(ck)
                if op.need_inc:
                    si, v = divmod(op.val - 1, SEM_ROLL)
                    op.clk[("e", op.eng, si)] = v + 1

        def run(engname, eng):
            for op in per_eng[engname]:
                todo = list(op.deps_need)
                embed = None
                if todo and not op.is_dma:
                    embed = todo.pop()
                for key, s, v in todo:
                    eng.wait_ge(s, v)
                ins = op.fn(eng)
                if embed is not None:
                    ins._wait_ge(embed[1], embed[2])
                if op.is_dma:
                    t = op.dma_sem_tile
                    ins.then_inc(t.dma_in_sem if op.dma_kind == "in" else t.dma_out_sem, 16)
                elif op.need_inc:
                    ins.then_inc(esems[engname][(op.val - 1) // SEM_ROLL], 1)
            if engname == "sp":
                for s, v in finals:
                    eng.wait_ge(s, v)

        with nc.Block() as block:
            @block.sync
            def _(eng):
                run("sp", eng)

            @block.tensor
            def _(eng):
                run("pe", eng)

            @block.scalar
            def _(eng):
                run("act", eng)

            @block.vector
            def _(eng):
                run("dve", eng)

            @block.gpsimd
            def _(eng):
                run("pool", eng)


D = 1024
DEPTH = 2
NT = 1536
SEGS = [(0, 256, 0), (256, 256, 0), (512, 1024, 1)]
TTS = [(0, 512), (512, 512), (1024, 512)]
NCH = NT // 128
IN_COLS = 6160
D_FF = 2816
EPS = 1e-6
BIG = 1.0e9
TWO_PI = 2.0 * math.pi
CST_N = 7 * 128 + 1026 + 256 + 7 * 128

FM_LAYOUT = [("n1g", 16), ("n2g", 16), ("bada", 96), ("dnconv", 36), ("dnng", 1), ("ssmd", 4), ("glub", 4),
             ("pscale", 4), ("fconv", 132)]
FM_OFF = {}
_o = 0
for _n, _s in FM_LAYOUT:
    FM_OFF[_n] = _o
    _o += _s
FM_L = _o
FM_TOT = FM_L * DEPTH + 8


SKIP_MIX = False
DN_DBG_P = 1


class Stop(Exception):
    pass


class KB:
    def __init__(self, nc, stack, dram, dbg=None):
        self.nc, self.dram, self.dbg = nc, dram, dbg
        self.fw = FW(nc, stack)
        fw = self.fw
        self.ps = [fw.psum(f"ps{i}", [128, 512]) for i in range(8)]
        self.pbi = 0
        self.X = fw.sbuf("X", [128, 8, NT])
        self.H = fw.sbuf("H", [128, 8, NT], BF16)
        self.ws = [fw.sbuf(f"ws{i}", [128, 8, 256], BF16) for i in range(3)]
        self.wsi = 0
        self.arena = fw.sbuf("arena", [128, 23500])
        self.cst = fw.sbuf("cst", [128, CST_N])
        self.fm = fw.sbuf("fm", [128, FM_TOT])
        self.bc = fw.sbuf("bc", [128, 32])
        self.sm = fw.sbuf("sm", [128, DEPTH * 2 * 3 * 16])
        self.modt = fw.sbuf("modt", [128, 48, 2])
        self.mods = fw.sbuf("mods", [128, 6, 8, 2])
        self.onesb = fw.sbuf("onesb", [128, 128], BF16)
        self.ones128b = fw.sbuf("ones128b", [128, 128], BF16)
        self.ones = fw.sbuf("ones", [128, 128])
        self.epsc = fw.sbuf("epsc", [128, 2])
        self.kc = fw.sbuf("kc", [128, 4])
        self.ones1b = fw.sbuf("ones1b", [128, 128], BF16)
        self.identb = fw.sbuf("identb", [128, 128], BF16)
        self.onesi128b = fw.sbuf("onesi128b", [128, 128], BF16)
        self.fin = fw.sbuf("fin", [128, 2, 128])
        self.bbt = [fw.sbuf("bbt0", [128, 2, 128], BF16)] * 2
        self.cbt = [fw.sbuf(f"cbt{i}", [128, 2, 128], BF16) for i in range(2)]
        self.h0t = fw.sbuf("h0t", [128, 2, 2, 16])
        self.pwt = fw.sbuf("pwt", [128, 128], BF16)
        self.S = [[fw.sbuf(f"S{a}{b}", [128, 128]) for b in range(2)] for a in range(3)]
        self.S2 = [[fw.sbuf(f"Sx{a}{b}", [128, 128]) for b in range(2)] for a in range(3)]
        self.Sb = [[fw.sbuf(f"Sb{a}{b}", [128, 128], BF16) for b in range(2)] for a in range(3)]
        self.AB = fw.sbuf("AB", [128, 12, 16])
        self.GT = fw.sbuf("GT", [128, 12, 8])
        self.BT = fw.sbuf("BT", [128, 12, 8])
        self.nb = 8
        self.aoff = 0

    def bank(self):
        b = self.ps[self.pbi % self.nb]
        self.pbi += 1
        return b

    def carve(self, name, shape, dtype=F32):
        n = int(np.prod(shape[1:])) * DSZ[dtype]
        n = (n + 63) // 64 * 64
        t = self.fw.carve(self.arena, self.aoff, name, shape, dtype)
        self.aoff += n
        return t

    def mm(self, out, lhsT, rhs, start=True, stop=True):
        self.fw.add("pe", lambda e: e.matmul(out.ap, lhsT.ap, rhs.ap, start=start, stop=stop),
                    reads=[lhsT, rhs], writes=[out])

    def tr(self, out, in_):
        ident = self.ident
        n = in_.p1 - in_.p0
        idv = ident[0:n, 0:n]
        self.fw.add("pe", lambda e: e.transpose(out.ap, in_.ap, idv.ap), reads=[in_, idv], writes=[out])

    def act(self, out, in_, func, bias=None, scale=1.0):
        rd = [in_]
        kw = {}
        if isinstance(bias, View):
            rd.append(bias)
            kw["bias"] = bias.ap
        elif bias is not None:
            kw["bias"] = bias
        if isinstance(scale, View):
            rd.append(scale)
            kw["scale"] = scale.ap
        else:
            kw["scale"] = scale
        self.fw.add("act", lambda e: e.activation(out.ap, in_.ap, func, **kw), reads=rd, writes=[out])

    def tt(self, out, a, b, op, eng="dve"):
        self.fw.add(eng, lambda e: e.tensor_tensor(out.ap, a.ap, b.ap, op), reads=[a, b], writes=[out])

    def ts(self, out, a, s1, s2, op0, op1=None, eng="dve"):
        rd = [a]
        v1 = s1.ap if isinstance(s1, View) else s1
        v2 = s2.ap if isinstance(s2, View) else s2
        if isinstance(s1, View):
            rd.append(s1)
        if isinstance(s2, View):
            rd.append(s2)
        if op1 is None:
            self.fw.add(eng, lambda e: e.tensor_scalar(out.ap, a.ap, v1, None, op0), reads=rd, writes=[out])
        else:
            self.fw.add(eng, lambda e: e.tensor_scalar(out.ap, a.ap, v1, v2, op0, op1), reads=rd, writes=[out])

    def stt(self, out, a, s, b, op0, op1, eng="dve"):
        rd = [a, b]
        sv = s.ap if isinstance(s, View) else s
        if isinstance(s, View):
            rd.append(s)
        self.fw.add(eng, lambda e: e.scalar_tensor_tensor(out.ap, a.ap, sv, b.ap, op0, op1), reads=rd, writes=[out])

    def cp(self, out, in_, eng="dve"):
        if eng == "act":
            self.fw.add("act", lambda e: e.copy(out.ap, in_.ap), reads=[in_], writes=[out])
        else:
            self.fw.add(eng, lambda e: e.tensor_copy(out.ap, in_.ap), reads=[in_], writes=[out])

    def memset(self, out, val, eng="dve"):
        self.fw.add(eng, lambda e: e.memset(out.ap, val), writes=[out])

    def scan(self, out, d0, d1, init, eng="dve"):
        rd = [d0, d1]
        iv = init.ap if isinstance(init, View) else init
        if isinstance(init, View):
            rd.append(init)
        self.fw.add(eng, lambda e: e.tensor_tensor_scan(out.ap, d0.ap, d1.ap, iv, ALU.mult, ALU.add),
                    reads=rd, writes=[out])

    def wload(self, src2d, r0, nk, c0, ncols):
        t = self.ws[self.wsi % 3]
        self.wsi += 1
        v = t[:, 0:nk, 0:ncols]
        self.fw.dma("pool", v, src2d[r0:r0 + 128 * nk, c0:c0 + ncols].rearrange("(k p) m -> p k m", p=128))
        return t

    def fmv(self, l, name, j, n=1):
        o = (l * FM_L if l is not None else 0) + FM_OFF[name] + j
        return self.fm[:, o:o + n]

    def dump(self, name, view):
        if self.dbg == name:
            self.fw.dma("pool", self.dram["dbg"], view)
            return True
        return False

    def setup(self):
        fw, d = self.fw, self.dram
        fw.dma("sp", self.cst.all(), d["cst"])
        fw.dma("sp", self.fm.all(), d["fm"])
        fw.dma("sp", self.bc.all(), d["bcp"].partition_broadcast(128))
        fw.dma("sp", self.sm.all(), d["sm"])
        c = self.cst
        self.ident = Tile("ident", [128, 128], F32, c.h[:, 0:128], track=c, base=0)
        names = ["triF", "triB", "negF", "negB", "posF", "posB"]
        self.cm = {}
        for i, n in enumerate(names):
            self.cm[n] = Tile(n, [128, 128], F32, c.h[:, 128 * (i + 1):128 * (i + 2)], track=c, base=512 * (i + 1))
        o = 7 * 128
        self.iota = Tile("iota", [128, 1026], F32, c.h[:, o:o + 1026], track=c, base=4 * o)
        o += 1026
        self.rcnt = Tile("rcnt", [128, 2, 128], F32, c.h[:, o:o + 256].rearrange("p (a b) -> p a b", a=2), track=c, base=4 * o)
        o += 256
        self.lvm = Tile("lvm", [128, 7, 128], F32, c.h[:, o:o + 896].rearrange("p (a b) -> p a b", a=7), track=c, base=4 * o)
        self.memset(self.onesb.all(), 1.0 / 1024.0)
        self.memset(self.ones128b.all(), 128.0)
        self.memset(self.ones.all(), 1.0)
        self.memset(self.ones1b.all(), 1.0)
        self.cp(self.identb.all(), self.ident.all())
        self.memset(self.onesi128b.all(), 1.0 / 128.0)
        self.memset(self.epsc[:, 0:1], EPS)
        self.memset(self.epsc[:, 1:2], EPS * 128.0)
        self.memset(self.kc[:, 0:1], math.pi / 2)
        self.memset(self.kc[:, 1:2], 3.1415925)
        self.memset(self.kc[:, 2:3], 2 * 3.1415925)

    def load_x(self):
        fw, d = self.fw, self.dram
        self.aoff = 0
        stg = [self.carve(f"stg{i}", [128, 4, 1024]) for i in range(2)]
        pet = [self.carve(f"pet{i}", [128, 4, 1024]) for i in range(2)]
        for g in range(3):
            s = stg[g % 2]
            fw.dma("sp", s.all(), d["xin"][g * 512:(g + 1) * 512, :].rearrange("(b p) m -> p b m", p=128))
            if g >= 1:
                p = pet[g % 2]
                fw.dma("sp", p.all(), d["pe"][(g - 1) * 512:g * 512, :].rearrange("(b p) m -> p b m", p=128))
                self.tt(s.all(), s.all(), p.all(), ALU.add)
            for c in range(8):
                pb = self.bank()
                for b in range(4):
                    self.tr(pb[:, b * 128:(b + 1) * 128], s[:, b, c * 128:(c + 1) * 128])
                self.cp(self.X[:, c, g * 512:(g + 1) * 512], pb.all(), eng="act" if c % 2 else "dve")

    def ada(self, l):
        fw, d = self.fw, self.dram
        self.aoff = 0
        sc = self.carve("scond", [128, 8, 2])
        wsl = [self.carve(f"wada{i}", [128, 8, 1024]) for i in range(2)]
        cnd = self.carve("cond", [128, 8, 2])
        mrow = self.carve("mrow", [128, 6144])
        fw.dma("sp", cnd.all(), d["cond"])
        self.act(sc.all(), cnd.all(), AF.Silu)
        wa = d["w_ada"][l]
        for blk in range(6):
            w = wsl[blk % 2]
            for hf in range(2):
                fw.dma("sp" if hf == 0 else "act", w[:, hf * 4:(hf + 1) * 4, :],
                       wa[hf * 512:(hf + 1) * 512, blk * 1024:(blk + 1) * 1024].rearrange("(k p) m -> p k m", p=128))
            for sub in range(2):
                pb = self.bank()
                for kc in range(8):
                    self.mm(pb[0:2, :], sc[:, kc, :], w[:, kc, sub * 512:(sub + 1) * 512], start=(kc == 0), stop=(kc == 7))
                self.cp(mrow[0:2, blk * 1024 + sub * 512:blk * 1024 + (sub + 1) * 512], pb[0:2, :], eng="act")
        pt = self.bank()
        for j in range(48):
            self.fw.add("pe", lambda e, j=j: e.transpose(pt[:, 2 * j:2 * j + 2].ap, mrow[0:2, j * 128:(j + 1) * 128].ap,
                                                          self.ident[0:2, 0:2].ap),
                        reads=[mrow[0:2, j * 128:(j + 1) * 128], self.ident[0:2, 0:2]], writes=[pt[:, 2 * j:2 * j + 2]])
        bada = self.fm[:, l * FM_L + FM_OFF["bada"]: l * FM_L + FM_OFF["bada"] + 96]
        mf = self.modt.all().with_ap(self.modt.all().ap.rearrange("p a b -> p (a b)"))
        self.tt(mf, pt[:, 0:96], bada, ALU.add)
        md = self.mods
        n1 = self.fm[:, l * FM_L + FM_OFF["n1g"]: l * FM_L + FM_OFF["n1g"] + 16]
        n2 = self.fm[:, l * FM_L + FM_OFF["n2g"]: l * FM_L + FM_OFF["n2g"] + 16]

        def fl(v):
            return v.with_ap(v.ap.rearrange("p a b -> p (a b)"))
        for h, ng in ((0, n1), (1, n2)):
            self.stt(fl(md[:, 3 * h + 0]), fl(self.modt[:, 24 * h + 8:24 * h + 16]), 1.0, ng, ALU.add, ALU.mult)
            self.cp(fl(md[:, 3 * h + 1]), fl(self.modt[:, 24 * h + 0:24 * h + 8]))
            self.cp(fl(md[:, 3 * h + 2]), fl(self.modt[:, 24 * h + 16:24 * h + 24]))

    def norm_mod(self, which):
        self.aoff_save = self.aoff
        self.aoff = 0
        sq = self.carve("sq", [128, 8, 512], BF16)
        rs = self.carve("rs", [128, NT])
        tmp = [self.carve(f"nmt{i}", [128, NT]) for i in range(2)]
        for ti, (t0, tn) in enumerate(TTS):
            self.act(sq.all(), self.X[:, :, t0:t0 + tn], AF.Square)
            pb = self.bank()
            for c in range(8):
                self.mm(pb.all(), self.onesb.all(), sq[:, c, :], start=(c == 0), stop=(c == 7))
            self.act(rs[:, t0:t0 + tn], pb.all(), AF.Ln, bias=self.epsc[:, 0:1])
        self.act(rs.all(), rs.all(), AF.Exp, scale=-0.5)
        for c in range(8):
            t = tmp[c % 2]
            self.tt(t.all(), self.X[:, c, :], rs.all(), ALU.mult)
            for (j, a, b) in ((0, 0, 512), (1, 512, NT)):
                self.act(self.H[:, c, a:b], t[:, a:b], AF.Identity,
                         bias=self.mods[:, 3 * which + 1, c, j:j + 1], scale=self.mods[:, 3 * which + 0, c, j:j + 1])
        self.aoff = self.aoff_save

    def proj(self, w2d, c0, nchunks, consume, nk=8, rhs=None, r0=0):
        rhs = rhs if rhs is not None else self.H
        j = 0
        while j < nchunks:
            nb = min(2, nchunks - j)
            w = self.wload(w2d, r0, nk, c0 + j * 128, nb * 128)
            for jj in range(nb):
                for ti, (t0, tn) in enumerate(TTS):
                    pb = self.bank()
                    for kc in range(nk):
                        self.mm(pb.all(), w[:, kc, jj * 128:(jj + 1) * 128], rhs[:, kc, t0:t0 + tn],
                                start=(kc == 0), stop=(kc == nk - 1))
                    consume(j + jj, ti, pb)
            j += nb

    def dwconv(self, dst, src, w3):
        self.act(dst.all(), src.all(), AF.Identity, scale=w3(1))
        for (s0, L, _) in SEGS:
            self.stt(dst[:, s0 + 1:s0 + L], src[:, s0:s0 + L - 1], w3(0), dst[:, s0 + 1:s0 + L], ALU.mult, ALU.add)
            self.stt(dst[:, s0:s0 + L - 1], src[:, s0 + 1:s0 + L], w3(2), dst[:, s0:s0 + L - 1], ALU.mult, ALU.add)

    def mixer_layout(self):
        self.aoff = 0
        self.o_dn = self.carve("o_dn", [128, 4, NT], BF16)
        self.dn_base = self.aoff
        self.o_ssm = self.carve("o_ssm", [128, 4, NT], BF16)
        self.s5_base = self.aoff
        self.o_pool = self.carve("o_pool", [128, 4, NT], BF16)
        self.mix_base = self.aoff

    def sinred(self, out, ang, ti, tf, shift=0.0):
        if shift != 0.0:
            self.ts(tf, ang, shift, None, ALU.add)
            src = tf
        else:
            src = ang
        self.ts(ti, src, 1.0 / TWO_PI, None, ALU.mult)
        self.cp(out, ti)
        self.stt(tf, out, -TWO_PI, src, ALU.mult, ALU.add)
        self.ts(tf, tf, -3.1415925, 3.1415925, ALU.max, ALU.min)
        self.act(out, tf, AF.Sin)

    def s5(self, l):
        fw, d = self.fw, self.dram
        self.aoff = self.s5_base
        U = self.carve("U16", [128, NT], BF16)
        Yb = self.carve("Ygb", [128, 4, NT], BF16)
        CSs = [self.carve(f"CS{i}", [128, 2, 1026]) for i in range(2)]
        Z = self.carve("Z", [128, 2, NT])
        XR = self.carve("XR", [128, 2, NT], BF16)
        T = self.carve("T", [128, 2, 512])
        ANG = self.carve("ANG", [128, 1026])
        RR = self.carve("RR", [128, 1026])
        TIi = Tile("TIi", [128, 1026], I32, RR.h.bitcast(I32), track=self.arena, base=RR.base)
        sp = self.carve("s5par", [128, 2, 24, 16])
        c32 = self.carve("c32", [128, 2, 128])
        ctmp = self.carve("ctmp", [128, 128])
        ctmp2 = self.carve("ctmp2", [128, 128])
        h0, cb = self.h0t, self.cbt
        bb = self.bbt[0]
        fw.dma("sp", h0[:, 0], d["sre"][l])
        fw.dma("sp", h0[:, 1], d["sim"][l])
        PI = {"step": 0, "r": 1, "th": 2, "lbr": 3, "lbi": 4, "den": 5, "fr": 6, "fi": 7, "gr": 8, "gi": 9,
              "t0": 10, "t1": 11, "t2": 12, "ti": 13, "nr": 14, "hr": 15, "hi": 16, "nfi": 17}
        for dr in range(2):
            def P(n):
                return sp[:, dr, PI[n], :]
            base = ((l * 2 + dr) * 3) * 16
            lre = self.sm[:, base:base + 16]
            lim = self.sm[:, base + 16:base + 32]
            lst = self.sm[:, base + 32:base + 48]
            self.act(P("step"), lst, AF.Exp)
            self.tt(P("t0"), lre, P("step"), ALU.mult)
            self.act(P("r"), P("t0"), AF.Exp)
            self.tt(P("th"), lim, P("step"), ALU.mult)
            tiv = sp[:, dr, PI["ti"], :]
            tiv = tiv.with_ap(tiv.ap.bitcast(I32))
            self.sinred(P("t1"), P("th"), tiv, P("t2"))
            self.tt(P("lbi"), P("r"), P("t1"), ALU.mult)
            self.sinred(P("t1"), P("th"), tiv, P("t2"), shift=math.pi / 2)
            self.tt(P("lbr"), P("r"), P("t1"), ALU.mult)
            self.ts(P("nr"), P("lbr"), -1.0, None, ALU.add)
            self.tt(P("den"), lre, lre, ALU.mult)
            self.tt(P("t0"), lim, lim, ALU.mult)
            self.tt(P("den"), P("den"), P("t0"), ALU.add)
            self.fw.add("dve", lambda e, v=P("den"): e.reciprocal(v.ap, v.ap), reads=[P("den")], writes=[P("den")])
            self.tt(P("t0"), P("nr"), lre, ALU.mult)
            self.tt(P("t1"), P("lbi"), lim, ALU.mult)
            self.tt(P("t0"), P("t0"), P("t1"), ALU.add)
            self.tt(P("fr"), P("t0"), P("den"), ALU.mult)
            self.tt(P("t0"), P("lbi"), lre, ALU.mult)
            self.tt(P("t1"), P("nr"), lim, ALU.mult)
            self.tt(P("t0"), P("t0"), P("t1"), ALU.subtract)
            self.tt(P("fi"), P("t0"), P("den"), ALU.mult)
            self.ts(P("nfi"), P("fi"), -1.0, None, ALU.mult)
            self.tt(P("t0"), P("fr"), P("fr"), ALU.mult)
            self.tt(P("t1"), P("fi"), P("fi"), ALU.mult)
            self.tt(P("t0"), P("t0"), P("t1"), ALU.add)
            self.fw.add("dve", lambda e, v=P("t0"): e.reciprocal(v.ap, v.ap), reads=[P("t0")], writes=[P("t0")])
            self.tt(P("gr"), P("fr"), P("t0"), ALU.mult)
            self.tt(P("gi"), P("nfi"), P("t0"), ALU.mult)
            hr0, hi0 = h0[:, 0, dr, :], h0[:, 1, dr, :]
            self.tt(P("t0"), hr0, P("gr"), ALU.mult)
            self.tt(P("t1"), hi0, P("gi"), ALU.mult)
            self.tt(P("hr"), P("t0"), P("t1"), ALU.subtract)
            self.tt(P("t0"), hr0, P("gi"), ALU.mult)
            self.tt(P("t1"), hi0, P("gr"), ALU.mult)
            self.tt(P("hi"), P("t0"), P("t1"), ALU.add)
        if self.dump(f"sp{l}", sp[:, 0].with_ap(sp[:, 0].ap.rearrange("p a b -> p (a b)"))):
            raise Stop()
        ybanks = [self.ps[5], self.ps[6], self.ps[7]]
        self.pbi = 0
        self.nb = 5
        fin = self.fin
        iters = [(c, sti, dr) for c in range(4) for sti in range(4) for dr in range(2)]

        def tables_a(i):
            c, sti, dr = iters[i]
            st = 4 * c + sti
            c16 = cb[i % 2]

            def P(n):
                return sp[:, dr, PI[n], st:st + 1]
            if dr == 0:
                fw.dma("act", c32.all(), d["cblk"][l, st])
            self.ts(ctmp.all(), c32[:, 1, :], P("fi"), None, ALU.mult)
            self.ts(ctmp2.all(), c32[:, 1, :], P("fr"), None, ALU.mult)
            self.act(ANG.all(), self.iota.all(), AF.Identity, scale=P("th"))
            self.stt(c16[:, 0, :], c32[:, 0, :], P("fr"), ctmp.all(), ALU.mult, ALU.subtract)
            self.stt(c16[:, 1, :], c32[:, 0, :], P("fi"), ctmp2.all(), ALU.mult, ALU.add)
            self.ts(TIi.all(), ANG.all(), 1.0 / TWO_PI, None, ALU.mult)

        def tables_b(i):
            CS = CSs[i % 2]
            self.stt(RR.all(), TIi.all(), -TWO_PI, ANG.all(), ALU.mult, ALU.add)
            self.act(RR.all(), RR.all(), AF.Relu, bias=self.kc[:, 1:2])
            self.act(RR.all(), RR.all(), AF.Relu, bias=self.kc[:, 2:3], scale=-1.0)
            self.act(CS[:, 1, :], RR.all(), AF.Sin, bias=self.kc[:, 1:2], scale=-1.0)
            self.act(ANG.all(), RR.all(), AF.Abs, bias=self.kc[:, 1:2], scale=-1.0)
            self.act(CS[:, 0, :], ANG.all(), AF.Sin, bias=self.kc[:, 0:1], scale=-1.0)

        def compute(i):
            c, sti, dr = iters[i]
            st = 4 * c + sti
            CS, c16 = CSs[i % 2], cb[i % 2]

            def P(n):
                return sp[:, dr, PI[n], st:st + 1]
            if dr == 0:
                fw.dma("pool", bb.all(), d["bblk"][l, st])
            for ti, (t0, tn) in enumerate(TTS):
                pp, pq = self.bank(), self.bank()
                self.mm(pp.all(), bb[:, 0, :], U[:, t0:t0 + tn])
                self.mm(pq.all(), bb[:, 1, :], U[:, t0:t0 + tn])
                pieces = [(0, 256, 0), (256, 256, 0)] if ti == 0 else [(0, 512, (ti - 1) * 512)]
                for (o, n, e0) in pieces:
                    cc, ss = CS[:, 0, e0:e0 + n], CS[:, 1, e0:e0 + n]
                    z0, z1 = Z[:, 0, t0 + o:t0 + o + n], Z[:, 1, t0 + o:t0 + o + n]
                    self.tt(z0, cc, pp[:, o:o + n], ALU.mult)
                    self.tt(T[:, 0, 0:n], ss, pq[:, o:o + n], ALU.mult)
                    self.tt(z1, cc, pq[:, o:o + n], ALU.mult)
                    self.tt(T[:, 1, 0:n], ss, pp[:, o:o + n], ALU.mult)
                    self.tt(z0, z0, T[:, 0, 0:n], ALU.add if dr == 0 else ALU.subtract)
                    self.tt(z1, z1, T[:, 1, 0:n], ALU.subtract if dr == 0 else ALU.add)
            if i + 1 < len(iters):
                tables_b(i + 1)
            rb = P("r")
            col = 1 if dr == 0 else 1024
            cc1, ss1 = CS[:, 0, col:col + 1], CS[:, 1, col:col + 1]
            hr, hi = P("hr"), P("hi")
            i0, i1, i2, i3 = (sp[:, dr, 18 + q, st:st + 1] for q in range(4))
            self.tt(i0, cc1, hr, ALU.mult)
            self.tt(i2, ss1, hi, ALU.mult)
            self.tt(i1, ss1, hr, ALU.mult)
            self.tt(i3, cc1, hi, ALU.mult)
            for si, (s0, L, cj) in enumerate(SEGS):
                inits = [0.0, 0.0]
                if cj == 1:
                    self.tt(i0, i0, i2, ALU.subtract)
                    self.tt(i1, i1, i3, ALU.add)
                    inits = [i0, i1]
                for ri in range(2):
                    zv = Z[:, ri, s0:s0 + L]
                    if dr == 1:
                        zv = zv.rev()
                    self.scan(zv, rb.bcast(L), zv, inits[ri])
            for si, (s0, L, cj) in enumerate(SEGS):
                for o in range(0, L, 512):
                    n = min(512, L - o)
                    a = s0 + o
                    cc, ss = CS[:, 0, o:o + n], CS[:, 1, o:o + n]
                    zr, zi = Z[:, 0, a:a + n], Z[:, 1, a:a + n]
                    k = 255 if dr == 0 else 0
                    dofin = (cj == 0 and o <= k < o + n)
                    xr_, xi_, t5, t6 = (sp[:, dr, 18 + q, st:st + 1] for q in range(4))
                    self.tt(T[:, 0, 0:n], cc, zr, ALU.mult)
                    self.tt(T[:, 1, 0:n], ss, zi, ALU.mult)
                    self.tt(zr, ss, zr, ALU.mult)
                    self.tt(zi, cc, zi, ALU.mult)
                    self.tt(XR[:, 0, a:a + n], T[:, 0, 0:n], T[:, 1, 0:n], ALU.subtract if dr == 0 else ALU.add)
                    self.stt(XR[:, 1, a:a + n], zr, -1.0 if dr == 0 else 1.0, zi, ALU.mult, ALU.subtract)
                    if dofin:
                        col = ((si * 2 + l) * 2 + dr) * 16 + st
                        zrk, zik = Z[:, 0, a + k - o:a + k - o + 1], Z[:, 1, a + k - o:a + k - o + 1]
                        self.tt(xr_, T[:, 0, k - o:k - o + 1], T[:, 1, k - o:k - o + 1], ALU.subtract if dr == 0 else ALU.add)
                        if dr == 0:
                            self.tt(xi_, zrk, zik, ALU.add)
                        else:
                            self.tt(xi_, zik, zrk, ALU.subtract)
                        self.tt(t5, xr_, P("fr"), ALU.mult)
                        self.tt(t6, xi_, P("fi"), ALU.mult)
                        self.tt(fin[:, 0, col:col + 1], t5, t6, ALU.subtract)
                        self.tt(t5, xr_, P("fi"), ALU.mult)
                        self.tt(t6, xi_, P("fr"), ALU.mult)
                        self.tt(fin[:, 1, col:col + 1], t5, t6, ALU.add)
            for ti, (t0, tn) in enumerate(TTS):
                first = (sti == 0 and dr == 0)
                lastm = (sti == 3 and dr == 1)
                self.mm(ybanks[ti].all(), c16[:, 0, :], XR[:, 0, t0:t0 + tn], start=first, stop=False)
                self.mm(ybanks[ti].all(), c16[:, 1, :], XR[:, 1, t0:t0 + tn], start=False, stop=lastm)

        tables_a(0)
        tables_b(0)
        for i, (c, sti, dr) in enumerate(iters):
            if sti == 0 and dr == 0:
                def ev(j, ti, pb):
                    t0, tn = TTS[ti]
                    self.cp(U[:, t0:t0 + tn], pb.all(), eng="act")
                self.proj(d["w_in"][l], 2064 + c * 128, 1, ev)
            if i + 1 < len(iters):
                tables_a(i + 1)
            compute(i)
            if sti == 3 and dr == 1:
                for ti, (t0, tn) in enumerate(TTS):
                    self.stt(T[:, 0, :], U[:, t0:t0 + tn], self.fmv(l, "ssmd", c), ybanks[ti].all(), ALU.mult, ALU.add)
                    self.act(Yb[:, c, t0:t0 + tn], T[:, 0, :], AF.Gelu)
        self.nb = 8
        if self.dump(f"yb{l}", Yb[:, 0, :]):
            raise Stop()

        def evg(j, ti, pb):
            t0, tn = TTS[ti]
            self.act(T[:, 0, :], pb.all(), AF.Sigmoid, bias=self.fmv(l, "glub", j))
            self.tt(self.o_ssm[:, j, t0:t0 + tn], T[:, 0, :], Yb[:, j, t0:t0 + tn], ALU.mult)
        self.proj(d["ssm_glu_w"][l], 0, 4, evg, nk=4, rhs=Yb)

    def pnorm(self, dst, src, ones_bf, eps_col, sqs, rs):
        for ti, (t0, tn) in enumerate(TTS):
            self.act(sqs.all(), src[:, t0:t0 + tn], AF.Square)
            pb = self.bank()
            self.mm(pb.all(), ones_bf.all(), sqs.all())
            self.act(rs[:, t0:t0 + tn], pb.all(), AF.Ln, bias=eps_col)
        self.act(rs.all(), rs.all(), AF.Exp, scale=-0.5)
        self.tt(dst.all(), src.all(), rs.all(), ALU.mult)

    def dn_gates(self, l):
        d = self.dram
        w = self.wload(d["w_in"][l], 0, 8, 2048, 16)
        pb = self.bank()
        for n in range(NCH):
            for kc in range(8):
                self.mm(pb[:, n * 16:(n + 1) * 16], self.H[:, kc, n * 128:(n + 1) * 128], w[:, kc, 0:16],
                        start=(kc == 0), stop=(kc == 7))
        ABf = self.AB.all().with_ap(self.AB.all().ap.rearrange("p a b -> p (a b)"))
        self.cp(ABf, pb[:, 0:192])
        alog = self.bc[:, l * 16:l * 16 + 8]
        dtb = self.bc[:, l * 16 + 8:l * 16 + 16]
        self.aoff = self.mix_base
        nea = self.carve("nea", [128, 8])
        tmp = self.carve("gtmp", [128, 12, 8])
        self.act(nea.all(), alog, AF.Exp)
        self.ts(nea.all(), nea.all(), -1.0, None, ALU.mult)
        for n in range(NCH):
            self.tt(tmp[:, n, :], self.AB[:, n, 0:8], dtb, ALU.add)
        tf = tmp.all().with_ap(tmp.all().ap.rearrange("p a b -> p (a b)"))
        self.act(tf, tf, AF.Exp)
        self.act(tf, tf, AF.Ln, bias=self.ones[:, 0:1])
        for n in range(NCH):
            self.tt(self.GT[:, n, :], tmp[:, n, :], nea.all(), ALU.mult)
            self.act(self.BT[:, n, :], self.AB[:, n, 8:16], AF.Sigmoid)

    def dn_head(self, l, hd):
        fw, d = self.fw, self.dram
        self.aoff = self.dn_base
        TMP = self.carve("TMPd", [128, NT])
        ZS = self.carve("ZS", [128, NT], BF16)
        SCR = self.carve("SCR", [128, NT])
        SQ = self.carve("SQd", [128, 512], BF16)
        Qb = self.carve("Qb", [128, NT], BF16)
        Kb = self.carve("Kb", [128, NT], BF16)
        Vb = self.carve("Vb", [128, NT], BF16)
        Oacc = Tile("Oacc", [128, 12, 128], F32, SCR.h.rearrange("p (a b) -> p a b", a=12), track=self.arena, base=SCR.base)
        w_in = d["w_in"][l]
        SCR2 = self.carve("SCR2", [128, NT])
        TMP2 = self.carve("TMP2", [128, NT])
        SQ2 = self.carve("SQd2", [128, 512], BF16)

        def chain(which, dst, SCRx, TMPx, SQx):
            w = self.wload(w_in, 0, 8, which * 512 + hd * 128, 128)
            yield
            for ti, (t0, tn) in enumerate(TTS):
                pb = self.bank()
                for kc in range(8):
                    self.mm(pb.all(), w[:, kc, 0:128], self.H[:, kc, t0:t0 + tn], start=(kc == 0), stop=(kc == 7))
                if which == 3:
                    self.act(ZS[:, t0:t0 + tn], pb.all(), AF.Silu)
                else:
                    self.cp(SCRx[:, t0:t0 + tn], pb.all(), eng="act")
                yield
            if which == 3:
                return
            ch = which * 4 + hd
            w3 = lambda k: self.fmv(l, "dnconv", ch * 3 + k)
            self.act(TMPx.all(), SCRx.all(), AF.Identity, scale=w3(1))
            yield
            for (s0, L, _) in SEGS:
                self.stt(TMPx[:, s0 + 1:s0 + L], SCRx[:, s0:s0 + L - 1], w3(0), TMPx[:, s0 + 1:s0 + L], ALU.mult, ALU.add)
            for (s0, L, _) in SEGS:
                self.stt(TMPx[:, s0:s0 + L - 1], SCRx[:, s0 + 1:s0 + L], w3(2), TMPx[:, s0:s0 + L - 1], ALU.mult, ALU.add)
            yield
            if which == 2:
                self.act(dst.all(), TMPx.all(), AF.Silu)
                return
            self.act(TMPx.all(), TMPx.all(), AF.Silu)
            yield
            ones_bf, eps_col = (self.ones128b, self.epsc[:, 1:2]) if which == 0 else (self.ones1b, self.epsc[:, 0:1])
            for ti, (t0, tn) in enumerate(TTS):
                self.act(SQx.all(), TMPx[:, t0:t0 + tn], AF.Square)
                pb = self.bank()
                self.mm(pb.all(), ones_bf.all(), SQx.all())
                self.act(SCRx[:, t0:t0 + tn], pb.all(), AF.Ln, bias=eps_col)
                yield
            self.act(SCRx.all(), SCRx.all(), AF.Exp, scale=-0.5)
            yield
            self.tt(dst.all(), TMPx.all(), SCRx.all(), ALU.mult)

        bufs = [(SCR, TMP, SQ), (SCR2, TMP2, SQ2)]
        pending = [(0, Qb), (1, Kb), (2, Vb)]
        active = [(chain(3, None, None, None, None), None)]
        ptick = 0
        while pending or active:
            ptick += 1
            if pending and bufs and ptick % 2 == 1:
                which, dst = pending.pop(0)
                bset = bufs.pop(0)
                active.append((chain(which, dst, *bset), bset))
            for item in list(active):
                try:
                    next(item[0])
                except StopIteration:
                    active.remove(item)
                    if item[1] is not None:
                        bufs.append(item[1])
        if self.dump(f"q{l}{hd}", Qb.all()) or self.dump(f"k{l}{hd}", Kb.all()):
            return
        for si in range(3):
            for dr in range(2):
                if SEGS[si][2] == 1:
                    fw.dma("sp", self.S[si][dr].all(), d["sdn"][l, dr, hd])
                else:
                    self.memset(self.S[si][dr].all(), 0.0)
                self.cp(self.Sb[si][dr].all(), self.S[si][dr].all(), eng="act")
        self.dstop("dnS0", self.S[2][0].all())
        NS = 5
        slots = []
        for i in range(NS):
            sl = {}
            sl["R1"] = self.carve(f"R1_{i}", [128, 2, 128])
            sl["R2"] = self.carve(f"R2_{i}", [128, 2, 128])
            for nm in ("Egc", "U"):
                sl[nm] = self.carve(f"{nm}{i}", [128, 128])
            for nm in ("AB", "TT"):
                sl[nm] = self.carve(f"{nm}{i}", [128, 2, 128])
            for nm in ("LL", "YW", "TTb"):
                sl[nm] = self.carve(f"{nm}{i}", [128, 2, 128], BF16)
            for nm in ("QgT", "QKmT", "Kd", "WT", "Vn"):
                sl[nm] = self.carve(f"{nm}{i}", [128, 128], BF16)
            sl["Gb"] = Tile(f"Gb{i}", [128, 128], F32, sl["R2"].h[:, 0, :], track=self.arena, base=sl["R2"].base)
            sl["Xs"] = self.carve(f"Xs{i}", [128, 256], BF16)
            sl["col"] = self.carve(f"col{i}", [128, 16])
            slots.append(sl)
        cm = self.cm
        ident = self.ident
        seqch = [(0, 2), (2, 2), (4, 8)]
        done = {}

        def mid2(v):
            ap = v.ap
            pat = [list(p) for p in ap.ap]
            return v.with_ap(bass.AP(ap.tensor, ap.offset, [pat[0], [0, 2], pat[1]]))

        def fl2(t):
            return t.all().with_ap(t.all().ap.rearrange("p a b -> p (a b)"))

        freeb = list(range(8))

        def problem(si, n, dr, sl):
            t0 = n * 128
            gi = dr * 4 + hd
            gcol = self.GT[:, n, gi:gi + 1]
            bcol = self.BT[:, n, gi:gi + 1]
            tri = cm["triF" if dr == 0 else "triB"]
            neg = cm["negF" if dr == 0 else "negB"]
            pos = cm["posF" if dr == 0 else "posB"]
            last = 127 if dr == 0 else 0
            col = sl["col"]
            gc, ngc, egc, bg, kd, gl, egl = (col[:, i:i + 1] for i in range(7))
            kbc, qbc, vbc = Kb[:, t0:t0 + 128], Qb[:, t0:t0 + 128], Vb[:, t0:t0 + 128]
            Kt, Vt = sl["R1"][:, 0, :], sl["R1"][:, 1, :]
            DcT, DcS = sl["R2"][:, 0, :], sl["R2"][:, 1, :]
            while not freeb:
                yield
            pg = self.ps[freeb.pop(0)]
            self.mm(pg[:, 0:128], kbc, self.identb.all())
            self.mm(pg[:, 128:256], vbc, self.identb.all())
            self.mm(pg[:, 256:384], kbc, kbc)
            self.mm(pg[:, 384:512], kbc, qbc)
            Gb = sl["Gb"]
            self.act(Gb.all(), self.ones.all(), AF.Identity, scale=gcol)
            while not freeb:
                yield
            pb = self.ps[freeb.pop(0)]
            self.mm(pb[:, 0:128], Gb.all(), tri.all())
            self.mm(pb[:, 384:385], tri.all(), gcol)
            yield
            self.cp(sl["R1"].all().with_ap(sl["R1"].all().ap.rearrange("p a b -> p (a b)")), pg[:, 0:256], eng="act")
            self.cp(gc, pb[:, 384:385])
            self.ts(ngc, pb[:, 384:385], -1.0, None, ALU.mult)
            self.cp(gl, pb[:, last:last + 1])
            self.act(egl, pb[:, last:last + 1], AF.Exp)
            self.act(sl["Egc"].all(), pb[:, 0:128], AF.Exp)
            self.tt(DcT, pb[:, 0:128], neg.all(), ALU.add)
            self.tt(DcS, pb[:, 0:128], pos.all(), ALU.add)
            self.act(DcT, DcT, AF.Exp, bias=ngc)
            self.act(DcS, DcS, AF.Exp, bias=gc, scale=-1.0)
            freeb.append(self.ps.index(pb))
            self.act(egc, gc, AF.Exp)
            self.tt(bg, bcol, egc, ALU.mult)
            self.act(kd, gc, AF.Exp, bias=gl, scale=-1.0)
            yield
            AB = sl["AB"]
            self.stt(AB[:, 0, :], DcS, bcol, pg[:, 256:384], ALU.mult, ALU.mult)
            self.tt(sl["QKmT"].all(), DcT, pg[:, 384:512], ALU.mult)
            freeb.append(self.ps.index(pg))
            while not freeb:
                yield
            pb = self.ps[freeb.pop(0)]
            self.tr(pb[:, 0:128], AB[:, 0, :])
            X = sl["Xs"]
            self.act(X[:, 0:128], Vt, AF.Identity, scale=bcol)
            self.act(X[:, 128:256], Kt, AF.Identity, scale=bg)
            self.tt(sl["QgT"].all(), qbc, sl["Egc"].all(), ALU.mult, eng="pool")
            self.act(sl["Kd"].all(), Kt, AF.Identity, scale=kd)
            yield
            self.cp(AB[:, 1, :], pb[:, 0:128], eng="act")
            freeb.append(self.ps.index(pb))
            TT, LL, YW, TTb = sl["TT"], sl["LL"], sl["YW"], sl["TTb"]
            self.tt(TT.all(), AB.all(), mid2(self.lvm[:, 0, :]), ALU.mult, eng="pool")
            self.tt(TTb.all(), mid2(ident.all()), TT.all(), ALU.subtract, eng="pool")
            self.tt(TT.all(), mid2(ident.all()), TT.all(), ALU.subtract, eng="pool")
            for lev in range(1, 7):
                self.tt(LL.all(), AB.all(), mid2(self.lvm[:, lev, :]), ALU.mult, eng="pool")
                yield
                while not freeb:
                    yield
                p1 = self.ps[freeb.pop(0)]
                self.mm(p1[:, 0:128], LL[:, 1, :], TTb[:, 0, :])
                self.mm(p1[:, 128:256], LL[:, 0, :], TTb[:, 1, :])
                yield
                self.cp(fl2(YW), p1[:, 0:256], eng="act")
                yield
                p2 = p1
                self.mm(p2[:, 256:384], TTb[:, 1, :], YW[:, 0, :])
                self.mm(p2[:, 384:512], TTb[:, 0, :], YW[:, 1, :])
                yield
                self.tt(fl2(TTb), fl2(TT), p2[:, 256:512], ALU.subtract)
                if lev < 6:
                    self.tt(fl2(TT), fl2(TT), p2[:, 256:512], ALU.subtract)
                freeb.append(self.ps.index(p2))
            yield
            while not freeb:
                yield
            pu = self.ps[freeb.pop(0)]
            self.mm(pu[:, 0:128], TTb[:, 1, :], X[:, 0:128])
            self.mm(pu[:, 128:256], X[:, 128:256], TTb[:, 1, :])
            yield
            self.cp(sl["U"].all(), pu[:, 0:128], eng="act")
            self.cp(sl["WT"].all(), pu[:, 128:256], eng="act")
            freeb.append(self.ps.index(pu))
            nloc = seqch[si][1]
            m = n - seqch[si][0]
            k = m if dr == 0 else nloc - 1 - m
            while done.get((si, dr), 0) < k:
                yield
            S = self.S[si][dr]
            Sb = self.Sb[si][dr]
            while not freeb:
                yield
            pa = self.ps[freeb.pop(0)]
            self.mm(pa[:, 0:128], sl["WT"].all(), Sb.all())
            yield
            self.tt(sl["Vn"].all(), sl["U"].all(), pa[:, 0:128], ALU.subtract)
            yield
            while not freeb:
                yield
            po = self.ps[freeb.pop(0)]
            self.mm(po[:, 0:128], sl["QgT"].all(), Sb.all(), start=True, stop=False)
            self.mm(po[:, 0:128], sl["QKmT"].all(), sl["Vn"].all(), start=False, stop=True)
            self.mm(pa[:, 256:384], sl["Kd"].all(), sl["Vn"].all())
            yield
            first = (m < nloc - 1 - m) if dr == 0 else (nloc - 1 - m < m)
            if first:
                self.cp(Oacc[:, n, :], po[:, 0:128], eng="act")
            else:
                self.tt(Oacc[:, n, :], Oacc[:, n, :], po[:, 0:128], ALU.add)
            Sn = self.S2[si][dr]
            self.stt(Sn.all(), S.all(), egl, pa[:, 256:384], ALU.mult, ALU.add)
            self.cp(Sb.all(), Sn.all(), eng="act")
            freeb.append(self.ps.index(pa))
            freeb.append(self.ps.index(po))
            self.S[si][dr], self.S2[si][dr] = Sn, S
            done[(si, dr)] = k + 1

        queue = []
        for step in range(8):
            for si, (c0, nc_) in enumerate(seqch):
                if step < nc_:
                    queue.append((si, c0 + step, 0))
                    queue.append((si, c0 + nc_ - 1 - step, 1))
        active = []
        free = list(range(NS))
        tick = 0
        STG = 2
        while queue or active:
            tick += 1
            if queue and free and tick % STG == 0:
                si, n, dr = queue.pop(0)
                i = free.pop(0)
                active.append((problem(si, n, dr, slots[i]), i))
            for item in list(active):
                try:
                    next(item[0])
                except StopIteration:
                    active.remove(item)
                    free.append(item[1])
        for si in range(2):
            for dr in range(2):
                fw.dma("sp", d["dn_out"][si, l, dr, hd], self.S[si][dr].all())
        for n in range(NCH):
            if n % 4 == 0:
                pb = self.bank()
            self.tr(pb[:, (n % 4) * 128:(n % 4 + 1) * 128], Oacc[:, n, :])
            if n % 4 == 3:
                self.cp(TMP[:, (n - 3) * 128:(n + 1) * 128], pb.all(), eng="act")
        self.pnorm(TMP, TMP, self.onesi128b, self.epsc[:, 0:1], SQ, SCR)
        self.stt(self.o_dn[:, hd, :], TMP.all(), self.fmv(l, "dnng", 0), ZS.all(), ALU.mult, ALU.mult)

    def pool(self, l):
        d = self.dram
        self.aoff = self.mix_base
        UP = self.carve("UP", [128, NT])
        PA = self.carve("PA", [128, 1024 + 32])
        PB_ = self.carve("PB", [128, 1024 + 32])
        PL = self.carve("PL", [128, NT], BF16)
        pw = self.pwt
        for gi in range(4):
            w = 2 << gi

            def ev(j, ti, pb):
                t0, tn = TTS[ti]
                self.cp(UP[:, t0:t0 + tn], pb.all(), eng="act")
            self.proj(d["w_in"][l], 2576 + gi * 128, 1, ev)
            for si, (s0, L, cj) in enumerate(SEGS):
                self.memset(PA.all(), 0.0)
                self.memset(PB_.all(), 0.0)
                self.cp(PA[:, 16:16 + L], UP[:, s0:s0 + L])
                cur, oth = PA, PB_
                m = 1
                while m < w:
                    self.tt(oth[:, 16:32 + L], cur[:, 16:32 + L], cur[:, 16 - m:32 + L - m], ALU.add)
                    cur, oth = oth, cur
                    m *= 2
                sh = w // 2 - 1
                hw = w // 2
                self.ts(oth[:, 16:16 + L], cur[:, 16 + sh:16 + sh + L], 1.0 / w, None, ALU.mult)
                self.tt(oth[:, 16:16 + hw], cur[:, 16 + sh:16 + sh + hw], self.rcnt[:, 0, gi * 16:gi * 16 + hw], ALU.mult)
                if hw > 1:
                    a = L - hw + 1
                    self.tt(oth[:, 16 + a:16 + L], cur[:, 16 + sh + a:16 + sh + L], self.rcnt[:, 1, gi * 16:gi * 16 + hw - 1], ALU.mult)
                self.tt(PL[:, s0:s0 + L], oth[:, 16:16 + L], UP[:, s0:s0 + L], ALU.subtract)
            self.fw.dma("pool", pw.all(), d["pool_w"][l, gi])
            for ti, (t0, tn) in enumerate(TTS):
                pb = self.bank()
                self.mm(pb.all(), pw.all(), PL[:, t0:t0 + tn])
                self.act(self.o_pool[:, gi, t0:t0 + tn], pb.all(), AF.Identity, scale=self.fmv(l, "pscale", gi))

    def merge(self, l):
        d = self.dram
        self.aoff = self.mix_base
        MG = self.carve("MG", [128, 8, NT], BF16)
        SG = self.carve("SG", [128, 512])
        TM = self.carve("TM", [128, 512])
        srcs = [(self.o_dn, d["w_branch_dn"][l]), (self.o_ssm, d["w_branch_ssm"][l]), (self.o_pool, d["w_branch_pool"][l])]
        for b, (osrc, wb) in enumerate(srcs):
            for j in range(8):
                wg = self.wload(d["w_in"][l], 0, 8, 3088 + b * 1024 + j * 128, 128)
                ww = self.wload(wb, 0, 4, j * 128, 128)
                for ti, (t0, tn) in enumerate(TTS):
                    pg, pv = self.bank(), self.bank()
                    for kc in range(8):
                        self.mm(pg.all(), wg[:, kc, 0:128], self.H[:, kc, t0:t0 + tn], start=(kc == 0), stop=(kc == 7))
                    for kc in range(4):
                        self.mm(pv.all(), ww[:, kc, 0:128], osrc[:, kc, t0:t0 + tn], start=(kc == 0), stop=(kc == 3))
                    self.act(SG.all(), pg.all(), AF.Sigmoid)
                    if b == 0:
                        self.tt(MG[:, j, t0:t0 + tn], SG.all(), pv.all(), ALU.mult)
                    else:
                        self.tt(TM.all(), SG.all(), pv.all(), ALU.mult)
                        self.tt(MG[:, j, t0:t0 + tn], MG[:, j, t0:t0 + tn], TM.all(), ALU.add)

        def ev(j, ti, pb):
            t0, tn = TTS[ti]
            for (cj, a, b_) in ((0, 0, 512), (1, 512, NT)):
                lo, hi = max(a, t0), min(b_, t0 + tn)
                if lo < hi:
                    self.stt(self.X[:, j, lo:hi], pb[:, lo - t0:hi - t0], self.mods[:, 2, j, cj:cj + 1], self.X[:, j, lo:hi],
                             ALU.mult, ALU.add)
        self.proj(d["w_out"][l], 0, 8, ev, rhs=MG)

    def ffn(self, l):
        d = self.dram
        self.aoff = 0
        ACTT = self.carve("ACTT", [128, 22, NT], BF16)
        G0s = [self.carve(f"G0{i}", [128, NT]) for i in range(2)]
        G1 = self.carve("G1", [128, NT])
        V1 = self.carve("V1", [128, NT])
        wup = d["ffn_w_up"][l]

        def half(j, isval, G0):
            col = (D_FF if isval else 0) + j * 128
            w = self.wload(wup, 0, 8, col, 128)
            yield
            for ti, (t0, tn) in enumerate(TTS):
                pb = self.bank()
                for kc in range(8):
                    self.mm(pb.all(), w[:, kc, 0:128], self.H[:, kc, t0:t0 + tn], start=(kc == 0), stop=(kc == 7))
                self.cp(G0[:, t0:t0 + tn], pb.all(), eng="act")
                yield
            dst = V1 if isval else G1
            ch = (22 if isval else 0) + j
            w3 = lambda k: self.fmv(l, "fconv", ch * 3 + k)
            self.act(dst.all(), G0.all(), AF.Identity, scale=w3(1))
            yield
            for (s0, L, _) in SEGS:
                self.stt(dst[:, s0 + 1:s0 + L], G0[:, s0:s0 + L - 1], w3(0), dst[:, s0 + 1:s0 + L], ALU.mult, ALU.add)
            yield
            for (s0, L, _) in SEGS:
                self.stt(dst[:, s0:s0 + L - 1], G0[:, s0 + 1:s0 + L], w3(2), dst[:, s0:s0 + L - 1], ALU.mult, ALU.add)
            yield
            if not isval:
                self.act(G1.all(), G1.all(), AF.Silu)
            else:
                self.tt(ACTT[:, j, :], G1.all(), V1.all(), ALU.mult)

        work = []
        for j in range(22):
            work.append((j, False))
            work.append((j, True))
        bufs = list(G0s)
        active = []
        ftick = 0
        while work or active:
            ftick += 1
            if work and bufs and ftick % 2 == 1:
                j, isval = work.pop(0)
                g0 = bufs.pop(0)
                active.append((half(j, isval, g0), g0))
            for item in list(active):
                try:
                    next(item[0])
                except StopIteration:
                    active.remove(item)
                    bufs.append(item[1])
        wd = d["ffn_w_down"][l]
        for j in range(8):
            ws_ = [self.wload(wd, 0, 8, j * 128, 128), self.wload(wd, 1024, 8, j * 128, 128), self.wload(wd, 2048, 6, j * 128, 128)]
            pbs = [self.bank() for _ in TTS]
            for blk in range(3):
                for ti, (t0, tn) in enumerate(TTS):
                    for kk in range(8 if blk < 2 else 6):
                        kc = blk * 8 + kk
                        self.mm(pbs[ti].all(), ws_[blk][:, kk, 0:128], ACTT[:, kc, t0:t0 + tn], start=(kc == 0), stop=(kc == 21))
            for ti, (t0, tn) in enumerate(TTS):
                pb = pbs[ti]
                for (cj, a_, b_) in ((0, 0, 512), (1, 512, NT)):
                    lo, hi = max(a_, t0), min(b_, t0 + tn)
                    if lo < hi:
                        self.stt(self.X[:, j, lo:hi], pb[:, lo - t0:hi - t0], self.mods[:, 5, j, cj:cj + 1], self.X[:, j, lo:hi],
                                 ALU.mult, ALU.add)

    def final(self):
        fw, d = self.fw, self.dram
        self.aoff = 0
        sq = self.carve("fsq", [128, 8, 512], BF16)
        rs = self.carve("frs", [128, NT])
        Y = self.carve("Y", [128, 8, 512])
        stg = [self.carve(f"ostg{i}", [128, 1024]) for i in range(2)]
        for ti, (t0, tn) in enumerate(TTS):
            self.act(sq.all(), self.X[:, :, t0:t0 + tn], AF.Square)
            pb = self.bank()
            for c in range(8):
                self.mm(pb.all(), self.onesb.all(), sq[:, c, :], start=(c == 0), stop=(c == 7))
            self.act(rs[:, t0:t0 + tn], pb.all(), AF.Ln, bias=self.epsc[:, 0:1])
        self.act(rs.all(), rs.all(), AF.Exp, scale=-0.5)
        fo = FM_L * DEPTH
        for ti, (t0, tn) in enumerate(TTS):
            for c in range(8):
                self.stt(Y[:, c, :], self.X[:, c, t0:t0 + tn], self.fm[:, fo + c:fo + c + 1], rs[:, t0:t0 + tn], ALU.mult, ALU.mult)
            for b in range(4):
                s = stg[b % 2]
                for half in range(2):
                    pb = self.bank()
                    for cc in range(4):
                        c = half * 4 + cc
                        self.tr(pb[:, cc * 128:(cc + 1) * 128], Y[:, c, b * 128:(b + 1) * 128])
                    self.cp(s[:, half * 512:(half + 1) * 512], pb.all(), eng="act" if half else "dve")
                r0 = t0 + b * 128
                fw.dma("sp", d["y"][r0:r0 + 128, :], s.all())
        for ri, nm in ((0, "ssm_re_out"), (1, "ssm_im_out")):
            pb = self.bank()
            self.tr(pb[:, 0:128], self.fin[:, ri, :])
            s = stg[ri]
            self.cp(s[:, 0:128], pb[:, 0:128])
            fw.dma("sp", d[nm], s[:, 0:128])

    def build(self):
        try:
            self.build_()
        except Stop:
            pass

    def dstop(self, name, view):
        if self.dump(name, view):
            raise Stop()

    def build_(self):
        self.setup()
        self.load_x()
        if self.dump("x0", self.X[:, 0, :]):
            return
        for l in range(DEPTH):
            self.ada(l)
            if self.dump(f"mod{l}", self.modt.all().with_ap(self.modt.all().ap.rearrange("p a b -> p (a b)"))):
                return
            self.norm_mod(0)
            if self.dump(f"h{l}", self.H[:, 0, :]):
                return
            self.mixer_layout()
            self.dn_gates(l)
            for hd in range(4):
                self.dn_head(l, hd)
                if self.dbg in (f"q{l}{hd}", f"k{l}{hd}"):
                    return
                if self.dump(f"odn{l}{hd}", self.o_dn[:, hd, :]):
                    return
            if SKIP_MIX:
                self.memset(self.o_ssm.all(), 0.0)
                self.memset(self.o_pool.all(), 0.0)
            else:
                self.s5(l)
                if self.dump(f"ossm{l}", self.o_ssm[:, 0, :]):
                    return
                self.pool(l)
                if self.dump(f"opool{l}", self.o_pool[:, 0, :]):
                    return
            self.merge(l)
            if self.dump(f"xm{l}", self.X[:, 0, :]):
                return
            self.norm_mod(1)
            self.ffn(l)
            if self.dump(f"xf{l}", self.X[:, 0, :]):
                return
        self.final()


def _pos_embed():
    rows, dim, gw = 16, D, 64
    q = dim // 4
    omega = (1.0 / (10000.0 ** (np.arange(q, dtype=np.float32) / np.float32(q)))).astype(np.float32)
    r = np.repeat(np.arange(rows, dtype=np.float32), gw)
    col = np.tile(np.arange(gw, dtype=np.float32), rows)

    def sc(p):
        ang = (p[:, None] * omega[None, :]).astype(np.float32)
        return np.concatenate([np.sin(ang), np.cos(ang)], axis=-1)
    return np.concatenate([sc(r), sc(col)], axis=-1).astype(np.float32)


def _consts():
    c = np.zeros((128, CST_N), np.float32)
    i = np.arange(128)
    P, Fr = i[:, None], i[None, :]
    c[:, 0:128] = np.eye(128)
    c[:, 128:256] = (P <= Fr)
    c[:, 256:384] = (P >= Fr)
    c[:, 384:512] = np.where(Fr >= P, 0.0, -BIG)
    c[:, 512:640] = np.where(Fr <= P, 0.0, -BIG)
    c[:, 640:768] = np.where(Fr < P, 0.0, BIG)
    c[:, 768:896] = np.where(Fr > P, 0.0, BIG)
    c[:, 896:896 + 1026] = np.arange(1026)[None, :]
    o = 896 + 1026
    for gi in range(4):
        w = 2 << gi
        hw = w // 2
        for k in range(hw):
            c[:, o + gi * 16 + k] = 1.0 / (k + hw)
        for k in range(hw - 1):
            c[:, o + 128 + gi * 16 + k] = 1.0 / (w - 1 - k)
    o += 256
    for k in range(7):
        c[:, o + k * 128:o + (k + 1) * 128] = ((P >> (k + 1)) == (Fr >> (k + 1))) & ((P >> k) != (Fr >> k))
    return c


def _fm(a):
    return np.ascontiguousarray(a.reshape(-1, 128).T)


def _build(dbg=None, dbg_shape=None):
    nc = bass.Bass("TRN2", target_bir_lowering=False)
    dram = {}

    def inp(name, shape):
        dram[name] = nc.dram_tensor(name, list(shape), F32, kind="ExternalInput").ap()

    def outp(name, shape):
        dram[name] = nc.dram_tensor(name, list(shape), F32, kind="ExternalOutput").ap()
    inp("xin", [NT, D]); inp("pe", [1024, D]); inp("cond", [128, 8, 2]); inp("cst", [128, CST_N])
    inp("fm", [128, FM_TOT]); inp("bcp", [1, 32]); inp("sm", [128, 192]); inp("sdn", [2, 2, 4, 128, 128])
    inp("sre", [2, 128, 2, 16]); inp("sim", [2, 128, 2, 16])
    inp("w_ada", [2, D, 6 * D]); inp("w_in", [2, D, IN_COLS]); inp("w_branch_dn", [2, 512, D])
    inp("w_branch_ssm", [2, 512, D]); inp("w_branch_pool", [2, 512, D]); inp("w_out", [2, D, D])
    inp("ffn_w_up", [2, D, 2 * D_FF]); inp("ffn_w_down", [2, D_FF, D]); inp("ssm_glu_w", [2, 512, 512])
    inp("pool_w", [2, 4, 128, 128]); inp("bblk", [2, 16, 128, 2, 128]); inp("cblk", [2, 16, 128, 2, 128])
    outp("y", [NT, D]); outp("dn_out", [2, 2, 2, 4, 128, 128]); outp("ssm_re_out", [128, 128]); outp("ssm_im_out", [128, 128])
    if dbg:
        outp("dbg", list(dbg_shape))
    st = ExitStack()
    kb = KB(nc, st, dram, dbg=dbg)
    kb.build()
    kb.fw.emit()
    return nc, st, kb


def _prep(inputs):
    g = {k: np.asarray(v) for k, v in inputs.items()}
    f32 = np.float32
    shared = {}
    for k in ("w_ada", "w_in", "w_branch_dn", "w_branch_ssm", "w_branch_pool", "w_out", "ffn_w_up", "ffn_w_down",
              "ssm_glu_w", "pool_w"):
        shared[k] = np.ascontiguousarray(g[k], dtype=f32)
    shared["pe"] = _pos_embed()
    shared["cst"] = _consts()
    fm = np.zeros((128, FM_TOT), f32)
    for l in range(DEPTH):
        b = l * FM_L
        fm[:, b + FM_OFF["n1g"]:b + FM_OFF["n1g"] + 16] = np.repeat(_fm(g["norm1_g"][l]), 2, axis=1)
        fm[:, b + FM_OFF["n2g"]:b + FM_OFF["n2g"] + 16] = np.repeat(_fm(g["norm2_g"][l]), 2, axis=1)
        fm[:, b + FM_OFF["bada"]:b + FM_OFF["bada"] + 96] = np.repeat(_fm(g["b_ada"][l]), 2, axis=1)
        dc = g["dn_conv"][l]
        fm[:, b + FM_OFF["dnconv"]:b + FM_OFF["dnconv"] + 36] = dc.reshape(3, 12, 128).transpose(2, 1, 0).reshape(128, 36)
        fm[:, b + FM_OFF["dnng"]] = g["dn_norm_g"][l]
        fm[:, b + FM_OFF["ssmd"]:b + FM_OFF["ssmd"] + 4] = _fm(g["ssm_d"][l])
        fm[:, b + FM_OFF["glub"]:b + FM_OFF["glub"] + 4] = _fm(g["ssm_glu_b"][l])
        fm[:, b + FM_OFF["pscale"]:b + FM_OFF["pscale"] + 4] = _fm(g["pool_scale"][l])
        fc = g["ffn_conv"][l]
        fm[:, b + FM_OFF["fconv"]:b + FM_OFF["fconv"] + 132] = fc.reshape(3, 44, 128).transpose(2, 1, 0).reshape(128, 132)
    fm[:, FM_L * DEPTH:] = _fm(g["final_norm_g"])
    shared["fm"] = fm
    bcp = np.zeros((1, 32), f32)
    sm = np.zeros((128, 192), f32)
    for l in range(DEPTH):
        bcp[0, l * 16:l * 16 + 8] = g["dn_a_log"][l].reshape(8)
        bcp[0, l * 16 + 8:l * 16 + 16] = g["dn_dt_bias"][l].reshape(8)
        for dr in range(2):
            base = ((l * 2 + dr) * 3) * 16
            sm[:, base:base + 16] = g["ssm_lambda_re"][l, dr].reshape(16, 128).T
            sm[:, base + 16:base + 32] = g["ssm_lambda_im"][l, dr].reshape(16, 128).T
            sm[:, base + 32:base + 48] = np.repeat(g["ssm_log_step"][l, dr], 64).reshape(16, 128).T
    shared["bcp"], shared["sm"] = bcp, sm
    bblk = np.zeros((2, 16, 128, 2, 128), f32)
    cblk = np.zeros((2, 16, 128, 2, 128), f32)
    for l in range(DEPTH):
        for st_ in range(16):
            for gl in range(2):
                gg = 2 * st_ + gl
                k0 = 32 * (st_ % 4) + 16 * gl
                for ri, (bn, cn) in enumerate((("ssm_b_re", "ssm_c_re"), ("ssm_b_im", "ssm_c_im"))):
                    bblk[l, st_, k0:k0 + 16, ri, 64 * gl:64 * gl + 64] = g[bn][l, gg].T
                    cblk[l, st_, 64 * gl:64 * gl + 64, ri, k0:k0 + 16] = g[cn][l, gg].T
    shared["bblk"], shared["cblk"] = bblk, cblk
    maps = []
    for c in range(8):
        m = dict(shared)
        m["xin"] = np.ascontiguousarray(np.concatenate([g["x_prompt"][2 * c], g["x_prompt"][2 * c + 1], g["x_sample"][c]], axis=0), dtype=f32)
        cond = np.zeros((128, 8, 2), f32)
        cond[:, :, 0] = _fm(g["c_ctx"])
        cond[:, :, 1] = _fm(g["c"][c])
        m["cond"] = cond
        m["sdn"] = np.ascontiguousarray(g["state_dn"][c], dtype=f32)
        for nm, src in (("sre", "state_ssm_re"), ("sim", "state_ssm_im")):
            a = g[src][c].reshape(2, 2, 16, 128)
            m[nm] = np.ascontiguousarray(a.transpose(0, 3, 1, 2), dtype=f32)
        maps.append(m)
    return maps


_CACHE = {}


def kernel(**inputs):
    if "nc" not in _CACHE:
        _CACHE["nc"] = _build()
    nc, st, kb = _CACHE["nc"]
    maps = _prep(inputs)
    res = run_bass_kernel_spmd(nc, maps, core_ids=list(range(8)))
    R = res.results
    y_prompt = np.zeros((16, 256, D), np.float32)
    y_sample = np.zeros((8, 1024, D), np.float32)
    ndn = np.zeros((16, 2, 2, 4, 128, 128), np.float32)
    nre = np.zeros((16, 2, 2, 32, 64), np.float32)
    nim = np.zeros((16, 2, 2, 32, 64), np.float32)
    for c in range(8):
        y = R[c]["y"]
        y_prompt[2 * c] = y[0:256]
        y_prompt[2 * c + 1] = y[256:512]
        y_sample[c] = y[512:]
        ndn[2 * c:2 * c + 2] = R[c]["dn_out"]
        nre[2 * c:2 * c + 2] = R[c]["ssm_re_out"].reshape(2, 2, 2, 16, 128).reshape(2, 2, 2, 32, 64)
        nim[2 * c:2 * c + 2] = R[c]["ssm_im_out"].reshape(2, 2, 2, 16, 128).reshape(2, 2, 2, 32, 64)
    return (y_prompt, y_sample, ndn, nre, nim)
```

```python
import math
import os
from contextlib import ExitStack
import numpy as np
import concourse.bass as bass
import concourse.mybir as mybir
from concourse.bass_utils import run_bass_kernel_spmd

F32 = mybir.dt.float32
BF16 = mybir.dt.bfloat16
I32 = mybir.dt.int32
AF = mybir.ActivationFunctionType
ALU = mybir.AluOpType
SEM_ROLL = 30000
DSZ = {F32: 4, BF16: 2, I32: 4}


class View:
    __slots__ = ("ap", "tile", "p0", "p1", "lo", "hi")

    def __init__(self, ap, tile, p0, p1, lo, hi):
        self.ap, self.tile, self.p0, self.p1, self.lo, self.hi = ap, tile, p0, p1, lo, hi

    def with_ap(self, ap):
        return View(ap, self.tile, self.p0, self.p1, self.lo, self.hi)

    def rev(self):
        ap = self.ap
        pat = [list(p) for p in ap.ap]
        assert pat[-1][0] == 1
        off = ap.offset + (pat[-1][1] - 1)
        pat[-1][0] = -1
        return self.with_ap(bass.AP(ap.tensor, off, pat))

    def bcast(self, n):
        ap = self.ap
        pat = [list(p) for p in ap.ap]
        pat[-1] = [0, n]
        return self.with_ap(bass.AP(ap.tensor, ap.offset, pat))


class Tile:
    def __init__(self, name, shape, dtype, handle, track=None, base=0):
        self.name, self.shape, self.dtype, self.h = name, list(shape), dtype, handle
        self.esz = DSZ[dtype]
        st = [1] * len(shape)
        for i in range(len(shape) - 2, 0, -1):
            st[i] = st[i + 1] * shape[i + 1]
        self.strides = st
        self.track = track if track is not None else self
        self.base = base
        self.recs = []
        self.dma_in = 0
        self.dma_in_sem = None
        self.dma_out = 0
        self.dma_out_sem = None
        self.whole = False

    def __getitem__(self, key):
        if not isinstance(key, tuple):
            key = (key,)
        key = tuple(key) + (slice(None),) * (len(self.shape) - len(key))
        rng = []
        for k, n in zip(key, self.shape):
            if isinstance(k, slice):
                a, b, s = k.indices(n)
                assert s == 1 and b > a, (self.name, key)
                rng.append((a, b))
            else:
                assert 0 <= k < n, (self.name, key)
                rng.append((k, k + 1))
        p0, p1 = rng[0]
        lo = sum(r[0] * s for r, s in zip(rng[1:], self.strides[1:]))
        hi = sum((r[1] - 1) * s for r, s in zip(rng[1:], self.strides[1:])) + 1
        if self.whole:
            return View(self.h[key], self.track, 0, 128, 0, 1 << 20)
        return View(self.h[key], self.track, p0, p1, self.base + lo * self.esz, self.base + hi * self.esz)

    def all(self):
        return self[tuple(slice(None) for _ in self.shape)]


class Op:
    __slots__ = ("eng", "fn", "deps", "is_dma", "dma_sem_tile", "dma_kind", "need_inc", "val", "idx", "dma_waits",
                 "deps_need", "clk")


class FW:
    ENGS = ("pe", "act", "dve", "pool", "sp")

    def __init__(self, nc, stack):
        self.nc, self.stack = nc, stack
        self.ops, self.tiles = [], []

    def sbuf(self, name, shape, dtype=F32):
        h = self.stack.enter_context(self.nc.sbuf_tensor("t_" + name, list(shape), dtype))
        t = Tile(name, shape, dtype, h)
        self.tiles.append(t)
        return t

    def psum(self, name, shape, dtype=F32):
        h = self.stack.enter_context(self.nc.psum_tensor("t_" + name, list(shape), dtype))
        t = Tile(name, shape, dtype, h)
        t.whole = True
        self.tiles.append(t)
        return t

    def carve(self, arena, off_bytes, name, shape, dtype=F32):
        n = int(np.prod(shape[1:])) * DSZ[dtype]
        assert off_bytes % 4 == 0 and n % 4 == 0
        assert off_bytes + n <= arena.shape[1] * 4, (name, off_bytes, n)
        ap = arena.h[:, off_bytes // 4:(off_bytes + n) // 4]
        if dtype != F32:
            ap = ap.bitcast(dtype)
        if len(shape) == 3:
            ap = ap.rearrange("p (a b) -> p a b", a=shape[1])
        elif len(shape) == 4:
            ap = ap.rearrange("p (a b c) -> p a b c", a=shape[1], b=shape[2])
        return Tile(name, shape, dtype, ap, track=arena, base=off_bytes)

    @staticmethod
    def _ov(r, v):
        return r[0] < v.p1 and v.p0 < r[1] and r[2] < v.hi and v.lo < r[3]

    def _track(self, op, reads, writes, whole_tile_war=False):
        deps = []
        for v in reads:
            for r in v.tile.recs:
                if r[5] and self._ov(r, v):
                    deps.append(r[4])
                elif v.tile.whole and (not r[5]) and r[4].eng != op.eng:
                    deps.append(r[4])
        for v in writes:
            for r in v.tile.recs:
                if self._ov(r, v) or (whole_tile_war and not (r[5] and r[4].is_dma)):
                    deps.append(r[4])
        for v in reads:
            t = v.tile
            key = (v.p0, v.p1, v.lo, v.hi)
            new = [r for r in t.recs
                   if not ((not r[5]) and (not op.is_dma) and (not r[4].is_dma) and r[4].eng == op.eng and r[:4] == key)]
            new.append((v.p0, v.p1, v.lo, v.hi, op, False))
            t.recs = new
        for v in writes:
            t = v.tile
            new = [r for r in t.recs
                   if not (v.p0 <= r[0] and r[1] <= v.p1 and v.lo <= r[2] and r[3] <= v.hi)]
            new.append((v.p0, v.p1, v.lo, v.hi, op, True))
            t.recs = new
        return deps

    def _newop(self, eng, is_dma):
        op = Op()
        op.eng, op.is_dma, op.need_inc, op.dma_waits, op.idx = eng, is_dma, False, [], len(self.ops)
        op.val = 0
        return op

    def _finish(self, op, deps):
        cd = {}
        for d in deps:
            if d is op:
                continue
            if d.is_dma:
                t = d.dma_sem_tile
                op.dma_waits.append((t, d.dma_kind, t.dma_in if d.dma_kind == "in" else t.dma_out))
            else:
                if d.eng == "pe" and op.eng == "pe":
                    continue
                cd[d.idx] = d
        op.deps = list(cd.values())
        for d in op.deps:
            d.need_inc = True
        self.ops.append(op)

    def add(self, eng, fn, reads=(), writes=()):
        op = self._newop(eng, False)
        op.fn = fn
        self._finish(op, self._track(op, [r for r in reads if r is not None], writes))
        return op

    def dma(self, queue, out, in_, **kw):
        op = self._newop(queue, True)
        if isinstance(out, View):
            t = out.tile
            deps = self._track(op, [], [out], whole_tile_war=True)
            op.dma_sem_tile, op.dma_kind = t, "in"
            oap, iap = out.ap, in_
        else:
            t = in_.tile
            deps = [r[4] for r in t.recs if r[5]]
            t.recs.append((in_.p0, in_.p1, in_.lo, in_.hi, op, False))
            op.dma_sem_tile, op.dma_kind = t, "out"
            oap, iap = out, in_.ap
        self._finish(op, deps)
        if op.dma_kind == "in":
            t.dma_in += 1
        else:
            t.dma_out += 1
        op.fn = lambda eng: eng.dma_start(out=oap, in_=iap, **kw)
        return op

    def emit(self):
        nc = self.nc
        counts = {e: 0 for e in self.ENGS}
        for op in self.ops:
            if op.is_dma or not op.need_inc:
                continue
            counts[op.eng] += 1
            op.val = counts[op.eng]
        esems = {e: [self.stack.enter_context(nc.semaphore(f"s_{e}_{i}")) for i in range(counts[e] // SEM_ROLL + 1)]
                 for e in self.ENGS}
        for t in self.tiles:
            if t.dma_in:
                t.dma_in_sem = self.stack.enter_context(nc.semaphore(f"di_{t.name}"))
            if t.dma_out:
                t.dma_out_sem = self.stack.enter_context(nc.semaphore(f"do_{t.name}"))
        per_eng = {e: [] for e in self.ENGS}
        for op in self.ops:
            per_eng[op.eng].append(op)
        finals = [(t.dma_out_sem, 16 * t.dma_out) for t in self.tiles if t.dma_out]
        finals += [(t.dma_in_sem, 16 * t.dma_in) for t in self.tiles if t.dma_in]
        last = {e: counts[e] for e in self.ENGS}

        clock = {e: {} for e in self.ENGS}
        for op in self.ops:
            ck = clock[op.eng]
            need = {}
            for d in op.deps:
                si, v = divmod(d.val - 1, SEM_ROLL)
                key = ("e", d.eng, si)
                if need.get(key, (None, 0))[1] < v + 1:
                    need[key] = (esems[d.eng][si], v + 1)
            for (t, kind, cnt) in op.dma_waits:
                s_ = t.dma_in_sem if kind == "in" else t.dma_out_sem
                key = ("d", t.name, kind)
                if need.get(key, (None, 0))[1] < 16 * cnt:
                    need[key] = (s_, 16 * cnt)
            op.deps_need = [(key, s_, v) for key, (s_, v) in need.items() if ck.get(key, 0) < v]
            for key, (s_, v) in need.items():
                if ck.get(key, 0) < v:
                    ck[key] = v
            for d in op.deps:
                for k2, v2 in d.clk.items():
                    if ck.get(k2, 0) < v2:
                        ck[k2] = v2
            if op.is_dma:
                op.clk = {}
            else:
                op.clk = dict(ck)
                if op.need_inc:
                    si, v = divmod(op.val - 1, SEM_ROLL)
                    op.clk[("e", op.eng, si)] = v + 1

        def run(engname, eng):
            for op in per_eng[engname]:
                todo = list(op.deps_need)
                embed = None
                if todo and not op.is_dma:
                    embed = todo.pop()
                for key, s, v in todo:
                    eng.wait_ge(s, v)
                ins = op.fn(eng)
                if embed is not None:
                    ins._wait_ge(embed[1], embed[2])
                if op.is_dma:
                    t = op.dma_sem_tile
                    ins.then_inc(t.dma_in_sem if op.dma_kind == "in" else t.dma_out_sem, 16)
                elif op.need_inc:
                    ins.then_inc(esems[engname][(op.val - 1) // SEM_ROLL], 1)
            if engname == "sp":
                for s, v in finals:
                    eng.wait_ge(s, v)

        with nc.Block() as block:
            @block.sync
            def _(eng):
                run("sp", eng)

            @block.tensor
            def _(eng):
                run("pe", eng)

            @block.scalar
            def _(eng):
                run("act", eng)

            @block.vector
            def _(eng):
                run("dve", eng)

            @block.gpsimd
            def _(eng):
                run("pool", eng)


D = 1024
DEPTH = 2
NT = 1536
SEGS = [(0, 256, 0), (256, 256, 0), (512, 1024, 1)]
TTS = [(0, 512), (512, 512), (1024, 512)]
NCH = NT // 128
IN_COLS = 6160
D_FF = 2816
EPS = 1e-6
BIG = 1.0e9
TWO_PI = 2.0 * math.pi
CST_N = 7 * 128 + 1026 + 256 + 7 * 128

FM_LAYOUT = [("n1g", 16), ("n2g", 16), ("bada", 96), ("dnconv", 36), ("dnng", 1), ("ssmd", 4), ("glub", 4),
             ("pscale", 4), ("fconv", 132)]
FM_OFF = {}
_o = 0
for _n, _s in FM_LAYOUT:
    FM_OFF[_n] = _o
    _o += _s
FM_L = _o
FM_TOT = FM_L * DEPTH + 8


SKIP_MIX = False
DN_DBG_P = 1


class Stop(Exception):
    pass


class KB:
    def __init__(self, nc, stack, dram, dbg=None):
        self.nc, self.dram, self.dbg = nc, dram, dbg
        self.fw = FW(nc, stack)
        fw = self.fw
        self.ps = [fw.psum(f"ps{i}", [128, 512]) for i in range(8)]
        self.pbi = 0
        self.X = fw.sbuf("X", [128, 8, NT])
        self.H = fw.sbuf("H", [128, 8, NT], BF16)
        self.ws = [fw.sbuf(f"ws{i}", [128, 8, 256], BF16) for i in range(3)]
        self.wsi = 0
        self.arena = fw.sbuf("arena", [128, 23500])
        self.cst = fw.sbuf("cst", [128, CST_N])
        self.fm = fw.sbuf("fm", [128, FM_TOT])
        self.bc = fw.sbuf("bc", [128, 32])
        self.sm = fw.sbuf("sm", [128, DEPTH * 2 * 3 * 16])
        self.modt = fw.sbuf("modt", [128, 48, 2])
        self.mods = fw.sbuf("mods", [128, 6, 8, 2])
        self.onesb = fw.sbuf("onesb", [128, 128], BF16)
        self.ones128b = fw.sbuf("ones128b", [128, 128], BF16)
        self.ones = fw.sbuf("ones", [128, 128])
        self.epsc = fw.sbuf("epsc", [128, 2])
        self.kc = fw.sbuf("kc", [128, 4])
        self.ones1b = fw.sbuf("ones1b", [128, 128], BF16)
        self.identb = fw.sbuf("identb", [128, 128], BF16)
        self.onesi128b = fw.sbuf("onesi128b", [128, 128], BF16)
        self.fin = fw.sbuf("fin", [128, 2, 128])
        self.bbt = [fw.sbuf("bbt0", [128, 2, 128], BF16)] * 2
        self.cbt = [fw.sbuf(f"cbt{i}", [128, 2, 128], BF16) for i in range(2)]
        self.h0t = fw.sbuf("h0t", [128, 2, 2, 16])
        self.pwt = fw.sbuf("pwt", [128, 128], BF16)
        self.S = [[fw.sbuf(f"S{a}{b}", [128, 128]) for b in range(2)] for a in range(3)]
        self.S2 = [[fw.sbuf(f"Sx{a}{b}", [128, 128]) for b in range(2)] for a in range(3)]
        self.Sb = [[fw.sbuf(f"Sb{a}{b}", [128, 128], BF16) for b in range(2)] for a in range(3)]
        self.AB = fw.sbuf("AB", [128, 12, 16])
        self.GT = fw.sbuf("GT", [128, 12, 8])
        self.BT = fw.sbuf("BT", [128, 12, 8])
        self.nb = 8
        self.aoff = 0

    def bank(self):
        b = self.ps[self.pbi % self.nb]
        self.pbi += 1
        return b

    def carve(self, name, shape, dtype=F32):
        n = int(np.prod(shape[1:])) * DSZ[dtype]
        n = (n + 63) // 64 * 64
        t = self.fw.carve(self.arena, self.aoff, name, shape, dtype)
        self.aoff += n
        return t

    def mm(self, out, lhsT, rhs, start=True, stop=True):
        self.fw.add("pe", lambda e: e.matmul(out.ap, lhsT.ap, rhs.ap, start=start, stop=stop),
                    reads=[lhsT, rhs], writes=[out])

    def tr(self, out, in_):
        ident = self.ident
        n = in_.p1 - in_.p0
        idv = ident[0:n, 0:n]
        self.fw.add("pe", lambda e: e.transpose(out.ap, in_.ap, idv.ap), reads=[in_, idv], writes=[out])

    def act(self, out, in_, func, bias=None, scale=1.0):
        rd = [in_]
        kw = {}
        if isinstance(bias, View):
            rd.append(bias)
            kw["bias"] = bias.ap
        elif bias is not None:
            kw["bias"] = bias
        if isinstance(scale, View):
            rd.append(scale)
            kw["scale"] = scale.ap
        else:
            kw["scale"] = scale
        self.fw.add("act", lambda e: e.activation(out.ap, in_.ap, func, **kw), reads=rd, writes=[out])

    def tt(self, out, a, b, op, eng="dve"):
        self.fw.add(eng, lambda e: e.tensor_tensor(out.ap, a.ap, b.ap, op), reads=[a, b], writes=[out])

    def ts(self, out, a, s1, s2, op0, op1=None, eng="dve"):
        rd = [a]
        v1 = s1.ap if isinstance(s1, View) else s1
        v2 = s2.ap if isinstance(s2, View) else s2
        if isinstance(s1, View):
            rd.append(s1)
        if isinstance(s2, View):
            rd.append(s2)
        if op1 is None:
            self.fw.add(eng, lambda e: e.tensor_scalar(out.ap, a.ap, v1, None, op0), reads=rd, writes=[out])
        else:
            self.fw.add(eng, lambda e: e.tensor_scalar(out.ap, a.ap, v1, v2, op0, op1), reads=rd, writes=[out])

    def stt(self, out, a, s, b, op0, op1, eng="dve"):
        rd = [a, b]
        sv = s.ap if isinstance(s, View) else s
        if isinstance(s, View):
            rd.append(s)
        self.fw.add(eng, lambda e: e.scalar_tensor_tensor(out.ap, a.ap, sv, b.ap, op0, op1), reads=rd, writes=[out])

    def cp(self, out, in_, eng="dve"):
        if eng == "act":
            self.fw.add("act", lambda e: e.copy(out.ap, in_.ap), reads=[in_], writes=[out])
        else:
            self.fw.add(eng, lambda e: e.tensor_copy(out.ap, in_.ap), reads=[in_], writes=[out])

    def memset(self, out, val, eng="dve"):
        self.fw.add(eng, lambda e: e.memset(out.ap, val), writes=[out])

    def scan(self, out, d0, d1, init, eng="dve"):
        rd = [d0, d1]
        iv = init.ap if isinstance(init, View) else init
        if isinstance(init, View):
            rd.append(init)
        self.fw.add(eng, lambda e: e.tensor_tensor_scan(out.ap, d0.ap, d1.ap, iv, ALU.mult, ALU.add),
                    reads=rd, writes=[out])

    def wload(self, src2d, r0, nk, c0, ncols):
        t = self.ws[self.wsi % 3]
        self.wsi += 1
        v = t[:, 0:nk, 0:ncols]
        self.fw.dma("pool", v, src2d[r0:r0 + 128 * nk, c0:c0 + ncols].rearrange("(k p) m -> p k m", p=128))
        return t

    def fmv(self, l, name, j, n=1):
        o = (l * FM_L if l is not None else 0) + FM_OFF[name] + j
        return self.fm[:, o:o + n]

    def dump(self, name, view):
        if self.dbg == name:
            self.fw.dma("pool", self.dram["dbg"], view)
            return True
        return False

    def setup(self):
        fw, d = self.fw, self.dram
        fw.dma("sp", self.cst.all(), d["cst"])
        fw.dma("sp", self.fm.all(), d["fm"])
        fw.dma("sp", self.bc.all(), d["bcp"].partition_broadcast(128))
        fw.dma("sp", self.sm.all(), d["sm"])
        c = self.cst
        self.ident = Tile("ident", [128, 128], F32, c.h[:, 0:128], track=c, base=0)
        names = ["triF", "triB", "negF", "negB", "posF", "posB"]
        self.cm = {}
        for i, n in enumerate(names):
            self.cm[n] = Tile(n, [128, 128], F32, c.h[:, 128 * (i + 1):128 * (i + 2)], track=c, base=512 * (i + 1))
        o = 7 * 128
        self.iota = Tile("iota", [128, 1026], F32, c.h[:, o:o + 1026], track=c, base=4 * o)
        o += 1026
        self.rcnt = Tile("rcnt", [128, 2, 128], F32, c.h[:, o:o + 256].rearrange("p (a b) -> p a b", a=2), track=c, base=4 * o)
        o += 256
        self.lvm = Tile("lvm", [128, 7, 128], F32, c.h[:, o:o + 896].rearrange("p (a b) -> p a b", a=7), track=c, base=4 * o)
        self.memset(self.onesb.all(), 1.0 / 1024.0)
        self.memset(self.ones128b.all(), 128.0)
        self.memset(self.ones.all(), 1.0)
        self.memset(self.ones1b.all(), 1.0)
        self.cp(self.identb.all(), self.ident.all())
        self.memset(self.onesi128b.all(), 1.0 / 128.0)
        self.memset(self.epsc[:, 0:1], EPS)
        self.memset(self.epsc[:, 1:2], EPS * 128.0)
        self.memset(self.kc[:, 0:1], math.pi / 2)
        self.memset(self.kc[:, 1:2], 3.1415925)
        self.memset(self.kc[:, 2:3], 2 * 3.1415925)

    def load_x(self):
        fw, d = self.fw, self.dram
        self.aoff = 0
        stg = [self.carve(f"stg{i}", [128, 4, 1024]) for i in range(2)]
        pet = [self.carve(f"pet{i}", [128, 4, 1024]) for i in range(2)]
        for g in range(3):
            s = stg[g % 2]
            fw.dma("sp", s.all(), d["xin"][g * 512:(g + 1) * 512, :].rearrange("(b p) m -> p b m", p=128))
            if g >= 1:
                p = pet[g % 2]
                fw.dma("sp", p.all(), d["pe"][(g - 1) * 512:g * 512, :].rearrange("(b p) m -> p b m", p=128))
                self.tt(s.all(), s.all(), p.all(), ALU.add)
            for c in range(8):
                pb = self.bank()
                for b in range(4):
                    self.tr(pb[:, b * 128:(b + 1) * 128], s[:, b, c * 128:(c + 1) * 128])
                self.cp(self.X[:, c, g * 512:(g + 1) * 512], pb.all(), eng="act" if c % 2 else "dve")

    def ada(self, l):
        fw, d = self.fw, self.dram
        self.aoff = 0
        sc = self.carve("scond", [128, 8, 2])
        wsl = [self.carve(f"wada{i}", [128, 8, 1024]) for i in range(2)]
        cnd = self.carve("cond", [128, 8, 2])
        mrow = self.carve("mrow", [128, 6144])
        fw.dma("sp", cnd.all(), d["cond"])
        self.act(sc.all(), cnd.all(), AF.Silu)
        wa = d["w_ada"][l]
        for blk in range(6):
            w = wsl[blk % 2]
            for hf in range(2):
                fw.dma("sp" if hf == 0 else "act", w[:, hf * 4:(hf + 1) * 4, :],
                       wa[hf * 512:(hf + 1) * 512, blk * 1024:(blk + 1) * 1024].rearrange("(k p) m -> p k m", p=128))
            for sub in range(2):
                pb = self.bank()
                for kc in range(8):
                    self.mm(pb[0:2, :], sc[:, kc, :], w[:, kc, sub * 512:(sub + 1) * 512], start=(kc == 0), stop=(kc == 7))
                self.cp(mrow[0:2, blk * 1024 + sub * 512:blk * 1024 + (sub + 1) * 512], pb[0:2, :], eng="act")
        pt = self.bank()
        for j in range(48):
            self.fw.add("pe", lambda e, j=j: e.transpose(pt[:, 2 * j:2 * j + 2].ap, mrow[0:2, j * 128:(j + 1) * 128].ap,
                                                          self.ident[0:2, 0:2].ap),
                        reads=[mrow[0:2, j * 128:(j + 1) * 128], self.ident[0:2, 0:2]], writes=[pt[:, 2 * j:2 * j + 2]])
        bada = self.fm[:, l * FM_L + FM_OFF["bada"]: l * FM_L + FM_OFF["bada"] + 96]
        mf = self.modt.all().with_ap(self.modt.all().ap.rearrange("p a b -> p (a b)"))
        self.tt(mf, pt[:, 0:96], bada, ALU.add)
        md = self.mods
        n1 = self.fm[:, l * FM_L + FM_OFF["n1g"]: l * FM_L + FM_OFF["n1g"] + 16]
        n2 = self.fm[:, l * FM_L + FM_OFF["n2g"]: l * FM_L + FM_OFF["n2g"] + 16]

        def fl(v):
            return v.with_ap(v.ap.rearrange("p a b -> p (a b)"))
        for h, ng in ((0, n1), (1, n2)):
            self.stt(fl(md[:, 3 * h + 0]), fl(self.modt[:, 24 * h + 8:24 * h + 16]), 1.0, ng, ALU.add, ALU.mult)
            self.cp(fl(md[:, 3 * h + 1]), fl(self.modt[:, 24 * h + 0:24 * h + 8]))
            self.cp(fl(md[:, 3 * h + 2]), fl(self.modt[:, 24 * h + 16:24 * h + 24]))

    def norm_mod(self, which):
        self.aoff_save = self.aoff
        self.aoff = 0
        sq = self.carve("sq", [128, 8, 512], BF16)
        rs = self.carve("rs", [128, NT])
        tmp = [self.carve(f"nmt{i}", [128, NT]) for i in range(2)]
        for ti, (t0, tn) in enumerate(TTS):
            self.act(sq.all(), self.X[:, :, t0:t0 + tn], AF.Square)
            pb = self.bank()
            for c in range(8):
                self.mm(pb.all(), self.onesb.all(), sq[:, c, :], start=(c == 0), stop=(c == 7))
            self.act(rs[:, t0:t0 + tn], pb.all(), AF.Ln, bias=self.epsc[:, 0:1])
        self.act(rs.all(), rs.all(), AF.Exp, scale=-0.5)
        for c in range(8):
            t = tmp[c % 2]
            self.tt(t.all(), self.X[:, c, :], rs.all(), ALU.mult)
            for (j, a, b) in ((0, 0, 512), (1, 512, NT)):
                self.act(self.H[:, c, a:b], t[:, a:b], AF.Identity,
                         bias=self.mods[:, 3 * which + 1, c, j:j + 1], scale=self.mods[:, 3 * which + 0, c, j:j + 1])
        self.aoff = self.aoff_save

    def proj(self, w2d, c0, nchunks, consume, nk=8, rhs=None, r0=0):
        rhs = rhs if rhs is not None else self.H
        j = 0
        while j < nchunks:
            nb = min(2, nchunks - j)
            w = self.wload(w2d, r0, nk, c0 + j * 128, nb * 128)
            for jj in range(nb):
                for ti, (t0, tn) in enumerate(TTS):
                    pb = self.bank()
                    for kc in range(nk):
                        self.mm(pb.all(), w[:, kc, jj * 128:(jj + 1) * 128], rhs[:, kc, t0:t0 + tn],
                                start=(kc == 0), stop=(kc == nk - 1))
                    consume(j + jj, ti, pb)
            j += nb

    def dwconv(self, dst, src, w3):
        self.act(dst.all(), src.all(), AF.Identity, scale=w3(1))
        for (s0, L, _) in SEGS:
            self.stt(dst[:, s0 + 1:s0 + L], src[:, s0:s0 + L - 1], w3(0), dst[:, s0 + 1:s0 + L], ALU.mult, ALU.add)
            self.stt(dst[:, s0:s0 + L - 1], src[:, s0 + 1:s0 + L], w3(2), dst[:, s0:s0 + L - 1], ALU.mult, ALU.add)

    def mixer_layout(self):
        self.aoff = 0
        self.o_dn = self.carve("o_dn", [128, 4, NT], BF16)
        self.dn_base = self.aoff
        self.o_ssm = self.carve("o_ssm", [128, 4, NT], BF16)
        self.s5_base = self.aoff
        self.o_pool = self.carve("o_pool", [128, 4, NT], BF16)
        self.mix_base = self.aoff

    def sinred(self, out, ang, ti, tf, shift=0.0):
        if shift != 0.0:
            self.ts(tf, ang, shift, None, ALU.add)
            src = tf
        else:
            src = ang
        self.ts(ti, src, 1.0 / TWO_PI, None, ALU.mult)
        self.cp(out, ti)
        self.stt(tf, out, -TWO_PI, src, ALU.mult, ALU.add)
        self.ts(tf, tf, -3.1415925, 3.1415925, ALU.max, ALU.min)
        self.act(out, tf, AF.Sin)

    def s5(self, l):
        fw, d = self.fw, self.dram
        self.aoff = self.s5_base
        U = self.carve("U16", [128, NT], BF16)
        Yb = self.carve("Ygb", [128, 4, NT], BF16)
        CSs = [self.carve(f"CS{i}", [128, 2, 1026]) for i in range(2)]
        Z = self.carve("Z", [128, 2, NT])
        XR = self.carve("XR", [128, 2, NT], BF16)
        T = self.carve("T", [128, 2, 512])
        ANG = self.carve("ANG", [128, 1026])
        RR = self.carve("RR", [128, 1026])
        TIi = Tile("TIi", [128, 1026], I32, RR.h.bitcast(I32), track=self.arena, base=RR.base)
        sp = self.carve("s5par", [128, 2, 24, 16])
        c32 = self.carve("c32", [128, 2, 128])
        ctmp = self.carve("ctmp", [128, 128])
        ctmp2 = self.carve("ctmp2", [128, 128])
        h0, cb = self.h0t, self.cbt
        bb = self.bbt[0]
        fw.dma("sp", h0[:, 0], d["sre"][l])
        fw.dma("sp", h0[:, 1], d["sim"][l])
        PI = {"step": 0, "r": 1, "th": 2, "lbr": 3, "lbi": 4, "den": 5, "fr": 6, "fi": 7, "gr": 8, "gi": 9,
              "t0": 10, "t1": 11, "t2": 12, "ti": 13, "nr": 14, "hr": 15, "hi": 16, "nfi": 17}
        for dr in range(2):
            def P(n):
                return sp[:, dr, PI[n], :]
            base = ((l * 2 + dr) * 3) * 16
            lre = self.sm[:, base:base + 16]
            lim = self.sm[:, base + 16:base + 32]
            lst = self.sm[:, base + 32:base + 48]
            self.act(P("step"), lst, AF.Exp)
            self.tt(P("t0"), lre, P("step"), ALU.mult)
            self.act(P("r"), P("t0"), AF.Exp)
            self.tt(P("th"), lim, P("step"), ALU.mult)
            tiv = sp[:, dr, PI["ti"], :]
            tiv = tiv.with_ap(tiv.ap.bitcast(I32))
            self.sinred(P("t1"), P("th"), tiv, P("t2"))
            self.tt(P("lbi"), P("r"), P("t1"), ALU.mult)
            self.sinred(P("t1"), P("th"), tiv, P("t2"), shift=math.pi / 2)
            self.tt(P("lbr"), P("r"), P("t1"), ALU.mult)
            self.ts(P("nr"), P("lbr"), -1.0, None, ALU.add)
            self.tt(P("den"), lre, lre, ALU.mult)
            self.tt(P("t0"), lim, lim, ALU.mult)
            self.tt(P("den"), P("den"), P("t0"), ALU.add)
            self.fw.add("dve", lambda e, v=P("den"): e.reciprocal(v.ap, v.ap), reads=[P("den")], writes=[P("den")])
            self.tt(P("t0"), P("nr"), lre, ALU.mult)
            self.tt(P("t1"), P("lbi"), lim, ALU.mult)
            self.tt(P("t0"), P("t0"), P("t1"), ALU.add)
            self.tt(P("fr"), P("t0"), P("den"), ALU.mult)
            self.tt(P("t0"), P("lbi"), lre, ALU.mult)
            self.tt(P("t1"), P("nr"), lim, ALU.mult)
            self.tt(P("t0"), P("t0"), P("t1"), ALU.subtract)
            self.tt(P("fi"), P("t0"), P("den"), ALU.mult)
            self.ts(P("nfi"), P("fi"), -1.0, None, ALU.mult)
            self.tt(P("t0"), P("fr"), P("fr"), ALU.mult)
            self.tt(P("t1"), P("fi"), P("fi"), ALU.mult)
            self.tt(P("t0"), P("t0"), P("t1"), ALU.add)
            self.fw.add("dve", lambda e, v=P("t0"): e.reciprocal(v.ap, v.ap), reads=[P("t0")], writes=[P("t0")])
            self.tt(P("gr"), P("fr"), P("t0"), ALU.mult)
            self.tt(P("gi"), P("nfi"), P("t0"), ALU.mult)
            hr0, hi0 = h0[:, 0, dr, :], h0[:, 1, dr, :]
            self.tt(P("t0"), hr0, P("gr"), ALU.mult)
            self.tt(P("t1"), hi0, P("gi"), ALU.mult)
            self.tt(P("hr"), P("t0"), P("t1"), ALU.subtract)
            self.tt(P("t0"), hr0, P("gi"), ALU.mult)
            self.tt(P("t1"), hi0, P("gr"), ALU.mult)
            self.tt(P("hi"), P("t0"), P("t1"), ALU.add)
        if self.dump(f"sp{l}", sp[:, 0].with_ap(sp[:, 0].ap.rearrange("p a b -> p (a b)"))):
            raise Stop()
        ybanks = [self.ps[5], self.ps[6], self.ps[7]]
        self.pbi = 0
        self.nb = 5
        fin = self.fin
        iters = [(c, sti, dr) for c in range(4) for sti in range(4) for dr in range(2)]

        def tables_a(i):
            c, sti, dr = iters[i]
            st = 4 * c + sti
            c16 = cb[i % 2]

            def P(n):
                return sp[:, dr, PI[n], st:st + 1]
            if dr == 0:
                fw.dma("act", c32.all(), d["cblk"][l, st])
            self.ts(ctmp.all(), c32[:, 1, :], P("fi"), None, ALU.mult)
            self.ts(ctmp2.all(), c32[:, 1, :], P("fr"), None, ALU.mult)
            self.act(ANG.all(), self.iota.all(), AF.Identity, scale=P("th"))
            self.stt(c16[:, 0, :], c32[:, 0, :], P("fr"), ctmp.all(), ALU.mult, ALU.subtract)
            self.stt(c16[:, 1, :], c32[:, 0, :], P("fi"), ctmp2.all(), ALU.mult, ALU.add)
            self.ts(TIi.all(), ANG.all(), 1.0 / TWO_PI, None, ALU.mult)

        def tables_b(i):
            CS = CSs[i % 2]
            self.stt(RR.all(), TIi.all(), -TWO_PI, ANG.all(), ALU.mult, ALU.add)
            self.act(RR.all(), RR.all(), AF.Relu, bias=self.kc[:, 1:2])
            self.act(RR.all(), RR.all(), AF.Relu, bias=self.kc[:, 2:3], scale=-1.0)
            self.act(CS[:, 1, :], RR.all(), AF.Sin, bias=self.kc[:, 1:2], scale=-1.0)
            self.act(ANG.all(), RR.all(), AF.Abs, bias=self.kc[:, 1:2], scale=-1.0)
            self.act(CS[:, 0, :], ANG.all(), AF.Sin, bias=self.kc[:, 0:1], scale=-1.0)

        def compute(i):
            c, sti, dr = iters[i]
            st = 4 * c + sti
            CS, c16 = CSs[i % 2], cb[i % 2]

            def P(n):
                return sp[:, dr, PI[n], st:st + 1]
            if dr == 0:
                fw.dma("pool", bb.all(), d["bblk"][l, st])
            for ti, (t0, tn) in enumerate(TTS):
                pp, pq = self.bank(), self.bank()
                self.mm(pp.all(), bb[:, 0, :], U[:, t0:t0 + tn])
                self.mm(pq.all(), bb[:, 1, :], U[:, t0:t0 + tn])
                pieces = [(0, 256, 0), (256, 256, 0)] if ti == 0 else [(0, 512, (ti - 1) * 512)]
                for (o, n, e0) in pieces:
                    cc, ss = CS[:, 0, e0:e0 + n], CS[:, 1, e0:e0 + n]
                    z0, z1 = Z[:, 0, t0 + o:t0 + o + n], Z[:, 1, t0 + o:t0 + o + n]
                    self.tt(z0, cc, pp[:, o:o + n], ALU.mult)
                    self.tt(T[:, 0, 0:n], ss, pq[:, o:o + n], ALU.mult)
                    self.tt(z1, cc, pq[:, o:o + n], ALU.mult)
                    self.tt(T[:, 1, 0:n], ss, pp[:, o:o + n], ALU.mult)
                    self.tt(z0, z0, T[:, 0, 0:n], ALU.add if dr == 0 else ALU.subtract)
                    self.tt(z1, z1, T[:, 1, 0:n], ALU.subtract if dr == 0 else ALU.add)
            if i + 1 < len(iters):
                tables_b(i + 1)
            rb = P("r")
            col = 1 if dr == 0 else 1024
            cc1, ss1 = CS[:, 0, col:col + 1], CS[:, 1, col:col + 1]
            hr, hi = P("hr"), P("hi")
            i0, i1, i2, i3 = (sp[:, dr, 18 + q, st:st + 1] for q in range(4))
            self.tt(i0, cc1, hr, ALU.mult)
            self.tt(i2, ss1, hi, ALU.mult)
            self.tt(i1, ss1, hr, ALU.mult)
            self.tt(i3, cc1, hi, ALU.mult)
            for si, (s0, L, cj) in enumerate(SEGS):
                inits = [0.0, 0.0]
                if cj == 1:
                    self.tt(i0, i0, i2, ALU.subtract)
                    self.tt(i1, i1, i3, ALU.add)
                    inits = [i0, i1]
                for ri in range(2):
                    zv = Z[:, ri, s0:s0 + L]
                    if dr == 1:
                        zv = zv.rev()
                    self.scan(zv, rb.bcast(L), zv, inits[ri])
            for si, (s0, L, cj) in enumerate(SEGS):
                for o in range(0, L, 512):
                    n = min(512, L - o)
                    a = s0 + o
                    cc, ss = CS[:, 0, o:o + n], CS[:, 1, o:o + n]
                    zr, zi = Z[:, 0, a:a + n], Z[:, 1, a:a + n]
                    k = 255 if dr == 0 else 0
                    dofin = (cj == 0 and o <= k < o + n)
                    xr_, xi_, t5, t6 = (sp[:, dr, 18 + q, st:st + 1] for q in range(4))
                    self.tt(T[:, 0, 0:n], cc, zr, ALU.mult)
                    self.tt(T[:, 1, 0:n], ss, zi, ALU.mult)
                    self.tt(zr, ss, zr, ALU.mult)
                    self.tt(zi, cc, zi, ALU.mult)
                    self.tt(XR[:, 0, a:a + n], T[:, 0, 0:n], T[:, 1, 0:n], ALU.subtract if dr == 0 else ALU.add)
                    self.stt(XR[:, 1, a:a + n], zr, -1.0 if dr == 0 else 1.0, zi, ALU.mult, ALU.subtract)
                    if dofin:
                        col = ((si * 2 + l) * 2 + dr) * 16 + st
                        zrk, zik = Z[:, 0, a + k - o:a + k - o + 1], Z[:, 1, a + k - o:a + k - o + 1]
                        self.tt(xr_, T[:, 0, k - o:k - o + 1], T[:, 1, k - o:k - o + 1], ALU.subtract if dr == 0 else ALU.add)
                        if dr == 0:
                            self.tt(xi_, zrk, zik, ALU.add)
                        else:
                            self.tt(xi_, zik, zrk, ALU.subtract)
                        self.tt(t5, xr_, P("fr"), ALU.mult)
                        self.tt(t6, xi_, P("fi"), ALU.mult)
                        self.tt(fin[:, 0, col:col + 1], t5, t6, ALU.subtract)
                        self.tt(t5, xr_, P("fi"), ALU.mult)
                        self.tt(t6, xi_, P("fr"), ALU.mult)
                        self.tt(fin[:, 1, col:col + 1], t5, t6, ALU.add)
            for ti, (t0, tn) in enumerate(TTS):
                first = (sti == 0 and dr == 0)
                lastm = (sti == 3 and dr == 1)
                self.mm(ybanks[ti].all(), c16[:, 0, :], XR[:, 0, t0:t0 + tn], start=first, stop=False)
                self.mm(ybanks[ti].all(), c16[:, 1, :], XR[:, 1, t0:t0 + tn], start=False, stop=lastm)

        tables_a(0)
        tables_b(0)
        for i, (c, sti, dr) in enumerate(iters):
            if sti == 0 and dr == 0:
                def ev(j, ti, pb):
                    t0, tn = TTS[ti]
                    self.cp(U[:, t0:t0 + tn], pb.all(), eng="act")
                self.proj(d["w_in"][l], 2064 + c * 128, 1, ev)
            if i + 1 < len(iters):
                tables_a(i + 1)
            compute(i)
            if sti == 3 and dr == 1:
                for ti, (t0, tn) in enumerate(TTS):
                    self.stt(T[:, 0, :], U[:, t0:t0 + tn], self.fmv(l, "ssmd", c), ybanks[ti].all(), ALU.mult, ALU.add)
                    self.act(Yb[:, c, t0:t0 + tn], T[:, 0, :], AF.Gelu)
        self.nb = 8
        if self.dump(f"yb{l}", Yb[:, 0, :]):
            raise Stop()

        def evg(j, ti, pb):
            t0, tn = TTS[ti]
            self.act(T[:, 0, :], pb.all(), AF.Sigmoid, bias=self.fmv(l, "glub", j))
            self.tt(self.o_ssm[:, j, t0:t0 + tn], T[:, 0, :], Yb[:, j, t0:t0 + tn], ALU.mult)
        self.proj(d["ssm_glu_w"][l], 0, 4, evg, nk=4, rhs=Yb)

    def pnorm(self, dst, src, ones_bf, eps_col, sqs, rs):
        for ti, (t0, tn) in enumerate(TTS):
            self.act(sqs.all(), src[:, t0:t0 + tn], AF.Square)
            pb = self.bank()
            self.mm(pb.all(), ones_bf.all(), sqs.all())
            self.act(rs[:, t0:t0 + tn], pb.all(), AF.Ln, bias=eps_col)
        self.act(rs.all(), rs.all(), AF.Exp, scale=-0.5)
        self.tt(dst.all(), src.all(), rs.all(), ALU.mult)

    def dn_gates(self, l):
        d = self.dram
        w = self.wload(d["w_in"][l], 0, 8, 2048, 16)
        pb = self.bank()
        for n in range(NCH):
            for kc in range(8):
                self.mm(pb[:, n * 16:(n + 1) * 16], self.H[:, kc, n * 128:(n + 1) * 128], w[:, kc, 0:16],
                        start=(kc == 0), stop=(kc == 7))
        ABf = self.AB.all().with_ap(self.AB.all().ap.rearrange("p a b -> p (a b)"))
        self.cp(ABf, pb[:, 0:192])
        alog = self.bc[:, l * 16:l * 16 + 8]
        dtb = self.bc[:, l * 16 + 8:l * 16 + 16]
        self.aoff = self.mix_base
        nea = self.carve("nea", [128, 8])
        tmp = self.carve("gtmp", [128, 12, 8])
        self.act(nea.all(), alog, AF.Exp)
        self.ts(nea.all(), nea.all(), -1.0, None, ALU.mult)
        for n in range(NCH):
            self.tt(tmp[:, n, :], self.AB[:, n, 0:8], dtb, ALU.add)
        tf = tmp.all().with_ap(tmp.all().ap.rearrange("p a b -> p (a b)"))
        self.act(tf, tf, AF.Exp)
        self.act(tf, tf, AF.Ln, bias=self.ones[:, 0:1])
        for n in range(NCH):
            self.tt(self.GT[:, n, :], tmp[:, n, :], nea.all(), ALU.mult)
            self.act(self.BT[:, n, :], self.AB[:, n, 8:16], AF.Sigmoid)

    def dn_head(self, l, hd):
        fw, d = self.fw, self.dram
        self.aoff = self.dn_base
        TMP = self.carve("TMPd", [128, NT])
        ZS = self.carve("ZS", [128, NT], BF16)
        SCR = self.carve("SCR", [128, NT])
        SQ = self.carve("SQd", [128, 512], BF16)
        Qb = self.carve("Qb", [128, NT], BF16)
        Kb = self.carve("Kb", [128, NT], BF16)
        Vb = self.carve("Vb", [128, NT], BF16)
        Oacc = Tile("Oacc", [128, 12, 128], F32, SCR.h.rearrange("p (a b) -> p a b", a=12), track=self.arena, base=SCR.base)
        w_in = d["w_in"][l]
        SCR2 = self.carve("SCR2", [128, NT])
        TMP2 = self.carve("TMP2", [128, NT])
        SQ2 = self.carve("SQd2", [128, 512], BF16)

        def chain(which, dst, SCRx, TMPx, SQx):
            w = self.wload(w_in, 0, 8, which * 512 + hd * 128, 128)
            yield
            for ti, (t0, tn) in enumerate(TTS):
                pb = self.bank()
                for kc in range(8):
                    self.mm(pb.all(), w[:, kc, 0:128], self.H[:, kc, t0:t0 + tn], start=(kc == 0), stop=(kc == 7))
                if which == 3:
                    self.act(ZS[:, t0:t0 + tn], pb.all(), AF.Silu)
                else:
                    self.cp(SCRx[:, t0:t0 + tn], pb.all(), eng="act")
                yield
            if which == 3:
                return
            ch = which * 4 + hd
            w3 = lambda k: self.fmv(l, "dnconv", ch * 3 + k)
            self.act(TMPx.all(), SCRx.all(), AF.Identity, scale=w3(1))
            yield
            for (s0, L, _) in SEGS:
                self.stt(TMPx[:, s0 + 1:s0 + L], SCRx[:, s0:s0 + L - 1], w3(0), TMPx[:, s0 + 1:s0 + L], ALU.mult, ALU.add)
            for (s0, L, _) in SEGS:
                self.stt(TMPx[:, s0:s0 + L - 1], SCRx[:, s0 + 1:s0 + L], w3(2), TMPx[:, s0:s0 + L - 1], ALU.mult, ALU.add)
            yield
            if which == 2:
                self.act(dst.all(), TMPx.all(), AF.Silu)
                return
            self.act(TMPx.all(), TMPx.all(), AF.Silu)
            yield
            ones_bf, eps_col = (self.ones128b, self.epsc[:, 1:2]) if which == 0 else (self.ones1b, self.epsc[:, 0:1])
            for ti, (t0, tn) in enumerate(TTS):
                self.act(SQx.all(), TMPx[:, t0:t0 + tn], AF.Square)
                pb = self.bank()
                self.mm(pb.all(), ones_bf.all(), SQx.all())
                self.act(SCRx[:, t0:t0 + tn], pb.all(), AF.Ln, bias=eps_col)
                yield
            self.act(SCRx.all(), SCRx.all(), AF.Exp, scale=-0.5)
            yield
            self.tt(dst.all(), TMPx.all(), SCRx.all(), ALU.mult)

        bufs = [(SCR, TMP, SQ), (SCR2, TMP2, SQ2)]
        pending = [(0, Qb), (1, Kb), (2, Vb)]
        active = [(chain(3, None, None, None, None), None)]
        ptick = 0
        while pending or active:
            ptick += 1
            if pending and bufs and ptick % 2 == 1:
                which, dst = pending.pop(0)
                bset = bufs.pop(0)
                active.append((chain(which, dst, *bset), bset))
            for item in list(active):
                try:
                    next(item[0])
                except StopIteration:
                    active.remove(item)
                    if item[1] is not None:
                        bufs.append(item[1])
        if self.dump(f"q{l}{hd}", Qb.all()) or self.dump(f"k{l}{hd}", Kb.all()):
            return
        for si in range(3):
            for dr in range(2):
                if SEGS[si][2] == 1:
                    fw.dma("sp", self.S[si][dr].all(), d["sdn"][l, dr, hd])
                else:
                    self.memset(self.S[si][dr].all(), 0.0)
                self.cp(self.Sb[si][dr].all(), self.S[si][dr].all(), eng="act")
        self.dstop("dnS0", self.S[2][0].all())
        NS = 5
        slots = []
        for i in range(NS):
            sl = {}
            sl["R1"] = self.carve(f"R1_{i}", [128, 2, 128])
            sl["R2"] = self.carve(f"R2_{i}", [128, 2, 128])
            for nm in ("Egc", "U"):
                sl[nm] = self.carve(f"{nm}{i}", [128, 128])
            for nm in ("AB", "TT"):
                sl[nm] = self.carve(f"{nm}{i}", [128, 2, 128])
            for nm in ("LL", "YW", "TTb"):
                sl[nm] = self.carve(f"{nm}{i}", [128, 2, 128], BF16)
            for nm in ("QgT", "QKmT", "Kd", "WT", "Vn"):
                sl[nm] = self.carve(f"{nm}{i}", [128, 128], BF16)
            sl["Gb"] = Tile(f"Gb{i}", [128, 128], F32, sl["R2"].h[:, 0, :], track=self.arena, base=sl["R2"].base)
            sl["Xs"] = self.carve(f"Xs{i}", [128, 256], BF16)
            sl["col"] = self.carve(f"col{i}", [128, 16])
            slots.append(sl)
        cm = self.cm
        ident = self.ident
        seqch = [(0, 2), (2, 2), (4, 8)]
        done = {}

        def mid2(v):
            ap = v.ap
            pat = [list(p) for p in ap.ap]
            return v.with_ap(bass.AP(ap.tensor, ap.offset, [pat[0], [0, 2], pat[1]]))

        def fl2(t):
            return t.all().with_ap(t.all().ap.rearrange("p a b -> p (a b)"))

        freeb = list(range(8))

        def problem(si, n, dr, sl):
            t0 = n * 128
            gi = dr * 4 + hd
            gcol = self.GT[:, n, gi:gi + 1]
            bcol = self.BT[:, n, gi:gi + 1]
            tri = cm["triF" if dr == 0 else "triB"]
            neg = cm["negF" if dr == 0 else "negB"]
            pos = cm["posF" if dr == 0 else "posB"]
            last = 127 if dr == 0 else 0
            col = sl["col"]
            gc, ngc, egc, bg, kd, gl, egl = (col[:, i:i + 1] for i in range(7))
            kbc, qbc, vbc = Kb[:, t0:t0 + 128], Qb[:, t0:t0 + 128], Vb[:, t0:t0 + 128]
            Kt, Vt = sl["R1"][:, 0, :], sl["R1"][:, 1, :]
            DcT, DcS = sl["R2"][:, 0, :], sl["R2"][:, 1, :]
            while not freeb:
                yield
            pg = self.ps[freeb.pop(0)]
            self.mm(pg[:, 0:128], kbc, self.identb.all())
            self.mm(pg[:, 128:256], vbc, self.identb.all())
            self.mm(pg[:, 256:384], kbc, kbc)
            self.mm(pg[:, 384:512], kbc, qbc)
            Gb = sl["Gb"]
            self.act(Gb.all(), self.ones.all(), AF.Identity, scale=gcol)
            while not freeb:
                yield
            pb = self.ps[freeb.pop(0)]
            self.mm(pb[:, 0:128], Gb.all(), tri.all())
            self.mm(pb[:, 384:385], tri.all(), gcol)
            yield
            self.cp(sl["R1"].all().with_ap(sl["R1"].all().ap.rearrange("p a b -> p (a b)")), pg[:, 0:256], eng="act")
            self.cp(gc, pb[:, 384:385])
            self.ts(ngc, pb[:, 384:385], -1.0, None, ALU.mult)
            self.cp(gl, pb[:, last:last + 1])
            self.act(egl, pb[:, last:last + 1], AF.Exp)
            self.act(sl["Egc"].all(), pb[:, 0:128], AF.Exp)
            self.tt(DcT, pb[:, 0:128], neg.all(), ALU.add)
            self.tt(DcS, pb[:, 0:128], pos.all(), ALU.add)
            self.act(DcT, DcT, AF.Exp, bias=ngc)
            self.act(DcS, DcS, AF.Exp, bias=gc, scale=-1.0)
            freeb.append(self.ps.index(pb))
            self.act(egc, gc, AF.Exp)
            self.tt(bg, bcol, egc, ALU.mult)
            self.act(kd, gc, AF.Exp, bias=gl, scale=-1.0)
            yield
            AB = sl["AB"]
            self.stt(AB[:, 0, :], DcS, bcol, pg[:, 256:384], ALU.mult, ALU.mult)
            self.tt(sl["QKmT"].all(), DcT, pg[:, 384:512], ALU.mult)
            freeb.append(self.ps.index(pg))
            while not freeb:
                yield
            pb = self.ps[freeb.pop(0)]
            self.tr(pb[:, 0:128], AB[:, 0, :])
            X = sl["Xs"]
            self.act(X[:, 0:128], Vt, AF.Identity, scale=bcol)
            self.act(X[:, 128:256], Kt, AF.Identity, scale=bg)
            self.tt(sl["QgT"].all(), qbc, sl["Egc"].all(), ALU.mult, eng="pool")
            self.act(sl["Kd"].all(), Kt, AF.Identity, scale=kd)
            yield
            self.cp(AB[:, 1, :], pb[:, 0:128], eng="act")
            freeb.append(self.ps.index(pb))
            TT, LL, YW, TTb = sl["TT"], sl["LL"], sl["YW"], sl["TTb"]
            self.tt(TT.all(), AB.all(), mid2(self.lvm[:, 0, :]), ALU.mult, eng="pool")
            self.tt(TTb.all(), mid2(ident.all()), TT.all(), ALU.subtract, eng="pool")
            self.tt(TT.all(), mid2(ident.all()), TT.all(), ALU.subtract, eng="pool")
            for lev in range(1, 7):
                self.tt(LL.all(), AB.all(), mid2(self.lvm[:, lev, :]), ALU.mult, eng="pool")
                yield
                while not freeb:
                    yield
                p1 = self.ps[freeb.pop(0)]
                self.mm(p1[:, 0:128], LL[:, 1, :], TTb[:, 0, :])
                self.mm(p1[:, 128:256], LL[:, 0, :], TTb[:, 1, :])
                yield
                self.cp(fl2(YW), p1[:, 0:256], eng="act")
                yield
                p2 = p1
                self.mm(p2[:, 256:384], TTb[:, 1, :], YW[:, 0, :])
                self.mm(p2[:, 384:512], TTb[:, 0, :], YW[:, 1, :])
                yield
                self.tt(fl2(TTb), fl2(TT), p2[:, 256:512], ALU.subtract)
                if lev < 6:
                    self.tt(fl2(TT), fl2(TT), p2[:, 256:512], ALU.subtract)
                freeb.append(self.ps.index(p2))
            yield
            while not freeb:
                yield
            pu = self.ps[freeb.pop(0)]
            self.mm(pu[:, 0:128], TTb[:, 1, :], X[:, 0:128])
            self.mm(pu[:, 128:256], X[:, 128:256], TTb[:, 1, :])
            yield
            self.cp(sl["U"].all(), pu[:, 0:128], eng="act")
            self.cp(sl["WT"].all(), pu[:, 128:256], eng="act")
            freeb.append(self.ps.index(pu))
            nloc = seqch[si][1]
            m = n - seqch[si][0]
            k = m if dr == 0 else nloc - 1 - m
            while done.get((si, dr), 0) < k:
                yield
            S = self.S[si][dr]
            Sb = self.Sb[si][dr]
            while not freeb:
                yield
            pa = self.ps[freeb.pop(0)]
            self.mm(pa[:, 0:128], sl["WT"].all(), Sb.all())
            yield
            self.tt(sl["Vn"].all(), sl["U"].all(), pa[:, 0:128], ALU.subtract)
            yield
            while not freeb:
                yield
            po = self.ps[freeb.pop(0)]
            self.mm(po[:, 0:128], sl["QgT"].all(), Sb.all(), start=True, stop=False)
            self.mm(po[:, 0:128], sl["QKmT"].all(), sl["Vn"].all(), start=False, stop=True)
            self.mm(pa[:, 256:384], sl["Kd"].all(), sl["Vn"].all())
            yield
            first = (m < nloc - 1 - m) if dr == 0 else (nloc - 1 - m < m)
            if first:
                self.cp(Oacc[:, n, :], po[:, 0:128], eng="act")
            else:
                self.tt(Oacc[:, n, :], Oacc[:, n, :], po[:, 0:128], ALU.add)
            Sn = self.S2[si][dr]
            self.stt(Sn.all(), S.all(), egl, pa[:, 256:384], ALU.mult, ALU.add)
            self.cp(Sb.all(), Sn.all(), eng="act")
            freeb.append(self.ps.index(pa))
            freeb.append(self.ps.index(po))
            self.S[si][dr], self.S2[si][dr] = Sn, S
            done[(si, dr)] = k + 1

        queue = []
        for step in range(8):
            for si, (c0, nc_) in enumerate(seqch):
                if step < nc_:
                    queue.append((si, c0 + step, 0))
                    queue.append((si, c0 + nc_ - 1 - step, 1))
        active = []
        free = list(range(NS))
        tick = 0
        STG = 2
        while queue or active:
            tick += 1
            if queue and free and tick % STG == 0:
                si, n, dr = queue.pop(0)
                i = free.pop(0)
                active.append((problem(si, n, dr, slots[i]), i))
            for item in list(active):
                try:
                    next(item[0])
                except StopIteration:
                    active.remove(item)
                    free.append(item[1])
        for si in range(2):
            for dr in range(2):
                fw.dma("sp", d["dn_out"][si, l, dr, hd], self.S[si][dr].all())
        for n in range(NCH):
            if n % 4 == 0:
                pb = self.bank()
            self.tr(pb[:, (n % 4) * 128:(n % 4 + 1) * 128], Oacc[:, n, :])
            if n % 4 == 3:
                self.cp(TMP[:, (n - 3) * 128:(n + 1) * 128], pb.all(), eng="act")
        self.pnorm(TMP, TMP, self.onesi128b, self.epsc[:, 0:1], SQ, SCR)
        self.stt(self.o_dn[:, hd, :], TMP.all(), self.fmv(l, "dnng", 0), ZS.all(), ALU.mult, ALU.mult)

    def pool(self, l):
        d = self.dram
        self.aoff = self.mix_base
        UP = self.carve("UP", [128, NT])
        PA = self.carve("PA", [128, 1024 + 32])
        PB_ = self.carve("PB", [128, 1024 + 32])
        PL = self.carve("PL", [128, NT], BF16)
        pw = self.pwt
        for gi in range(4):
            w = 2 << gi

            def ev(j, ti, pb):
                t0, tn = TTS[ti]
                self.cp(UP[:, t0:t0 + tn], pb.all(), eng="act")
            self.proj(d["w_in"][l], 2576 + gi * 128, 1, ev)
            for si, (s0, L, cj) in enumerate(SEGS):
                self.memset(PA.all(), 0.0)
                self.memset(PB_.all(), 0.0)
                self.cp(PA[:, 16:16 + L], UP[:, s0:s0 + L])
                cur, oth = PA, PB_
                m = 1
                while m < w:
                    self.tt(oth[:, 16:32 + L], cur[:, 16:32 + L], cur[:, 16 - m:32 + L - m], ALU.add)
                    cur, oth = oth, cur
                    m *= 2
                sh = w // 2 - 1
                hw = w // 2
                self.ts(oth[:, 16:16 + L], cur[:, 16 + sh:16 + sh + L], 1.0 / w, None, ALU.mult)
                self.tt(oth[:, 16:16 + hw], cur[:, 16 + sh:16 + sh + hw], self.rcnt[:, 0, gi * 16:gi * 16 + hw], ALU.mult)
                if hw > 1:
                    a = L - hw + 1
                    self.tt(oth[:, 16 + a:16 + L], cur[:, 16 + sh + a:16 + sh + L], self.rcnt[:, 1, gi * 16:gi * 16 + hw - 1], ALU.mult)
                self.tt(PL[:, s0:s0 + L], oth[:, 16:16 + L], UP[:, s0:s0 + L], ALU.subtract)
            self.fw.dma("pool", pw.all(), d["pool_w"][l, gi])
            for ti, (t0, tn) in enumerate(TTS):
                pb = self.bank()
                self.mm(pb.all(), pw.all(), PL[:, t0:t0 + tn])
                self.act(self.o_pool[:, gi, t0:t0 + tn], pb.all(), AF.Identity, scale=self.fmv(l, "pscale", gi))

    def merge(self, l):
        d = self.dram
        self.aoff = self.mix_base
        MG = self.carve("MG", [128, 8, NT], BF16)
        SG = self.carve("SG", [128, 512])
        TM = self.carve("TM", [128, 512])
        srcs = [(self.o_dn, d["w_branch_dn"][l]), (self.o_ssm, d["w_branch_ssm"][l]), (self.o_pool, d["w_branch_pool"][l])]
        for b, (osrc, wb) in enumerate(srcs):
            for j in range(8):
                wg = self.wload(d["w_in"][l], 0, 8, 3088 + b * 1024 + j * 128, 128)
                ww = self.wload(wb, 0, 4, j * 128, 128)
                pgs = [self.bank() for _ in TTS]
                for ti, (t0, tn) in enumerate(TTS):
                    for kc in range(8):
                        self.mm(pgs[ti].all(), wg[:, kc, 0:128], self.H[:, kc, t0:t0 + tn], start=(kc == 0), stop=(kc == 7))
                pvs = [self.bank() for _ in TTS]
                for ti, (t0, tn) in enumerate(TTS):
                    for kc in range(4):
                        self.mm(pvs[ti].all(), ww[:, kc, 0:128], osrc[:, kc, t0:t0 + tn], start=(kc == 0), stop=(kc == 3))
                for ti, (t0, tn) in enumerate(TTS):
                    pg, pv = pgs[ti], pvs[ti]
                    self.act(SG.all(), pg.all(), AF.Sigmoid)
                    if b == 0:
                        self.tt(MG[:, j, t0:t0 + tn], SG.all(), pv.all(), ALU.mult)
                    else:
                        self.tt(TM.all(), SG.all(), pv.all(), ALU.mult)
                        self.tt(MG[:, j, t0:t0 + tn], MG[:, j, t0:t0 + tn], TM.all(), ALU.add)

        def ev(j, ti, pb):
            t0, tn = TTS[ti]
            for (cj, a, b_) in ((0, 0, 512), (1, 512, NT)):
                lo, hi = max(a, t0), min(b_, t0 + tn)
                if lo < hi:
                    self.stt(self.X[:, j, lo:hi], pb[:, lo - t0:hi - t0], self.mods[:, 2, j, cj:cj + 1], self.X[:, j, lo:hi],
                             ALU.mult, ALU.add)
        self.proj(d["w_out"][l], 0, 8, ev, rhs=MG)

    def ffn(self, l):
        d = self.dram
        self.aoff = 0
        ACTT = self.carve("ACTT", [128, 22, NT], BF16)
        G0s = [self.carve(f"G0{i}", [128, NT]) for i in range(2)]
        G1 = self.carve("G1", [128, NT])
        V1 = self.carve("V1", [128, NT])
        wup = d["ffn_w_up"][l]

        def half(j, isval, G0):
            col = (D_FF if isval else 0) + j * 128
            w = self.wload(wup, 0, 8, col, 128)
            yield
            for ti, (t0, tn) in enumerate(TTS):
                pb = self.bank()
                for kc in range(8):
                    self.mm(pb.all(), w[:, kc, 0:128], self.H[:, kc, t0:t0 + tn], start=(kc == 0), stop=(kc == 7))
                self.cp(G0[:, t0:t0 + tn], pb.all(), eng="act")
                yield
            dst = V1 if isval else G1
            ch = (22 if isval else 0) + j
            w3 = lambda k: self.fmv(l, "fconv", ch * 3 + k)
            self.act(dst.all(), G0.all(), AF.Identity, scale=w3(1))
            yield
            for (s0, L, _) in SEGS:
                self.stt(dst[:, s0 + 1:s0 + L], G0[:, s0:s0 + L - 1], w3(0), dst[:, s0 + 1:s0 + L], ALU.mult, ALU.add)
            yield
            for (s0, L, _) in SEGS:
                self.stt(dst[:, s0:s0 + L - 1], G0[:, s0 + 1:s0 + L], w3(2), dst[:, s0:s0 + L - 1], ALU.mult, ALU.add)
            yield
            if not isval:
                self.act(G1.all(), G1.all(), AF.Silu)
            else:
                self.tt(ACTT[:, j, :], G1.all(), V1.all(), ALU.mult)

        work = []
        for j in range(22):
            work.append((j, False))
            work.append((j, True))
        bufs = list(G0s)
        active = []
        ftick = 0
        while work or active:
            ftick += 1
            if work and bufs and ftick % 2 == 1:
                j, isval = work.pop(0)
                g0 = bufs.pop(0)
                active.append((half(j, isval, g0), g0))
            for item in list(active):
                try:
                    next(item[0])
                except StopIteration:
                    active.remove(item)
                    bufs.append(item[1])
        wd = d["ffn_w_down"][l]
        for j in range(8):
            ws_ = [self.wload(wd, 0, 8, j * 128, 128), self.wload(wd, 1024, 8, j * 128, 128), self.wload(wd, 2048, 6, j * 128, 128)]
            pbs = [self.bank() for _ in TTS]
            for blk in range(3):
                for ti, (t0, tn) in enumerate(TTS):
                    for kk in range(8 if blk < 2 else 6):
                        kc = blk * 8 + kk
                        self.mm(pbs[ti].all(), ws_[blk][:, kk, 0:128], ACTT[:, kc, t0:t0 + tn], start=(kc == 0), stop=(kc == 21))
            for ti, (t0, tn) in enumerate(TTS):
                pb = pbs[ti]
                for (cj, a_, b_) in ((0, 0, 512), (1, 512, NT)):
                    lo, hi = max(a_, t0), min(b_, t0 + tn)
                    if lo < hi:
                        self.stt(self.X[:, j, lo:hi], pb[:, lo - t0:hi - t0], self.mods[:, 5, j, cj:cj + 1], self.X[:, j, lo:hi],
                                 ALU.mult, ALU.add)

    def final(self):
        fw, d = self.fw, self.dram
        self.aoff = 0
        sq = self.carve("fsq", [128, 8, 512], BF16)
        rs = self.carve("frs", [128, NT])
        Y = self.carve("Y", [128, 8, 512])
        stg = [self.carve(f"ostg{i}", [128, 1024]) for i in range(2)]
        for ti, (t0, tn) in enumerate(TTS):
            self.act(sq.all(), self.X[:, :, t0:t0 + tn], AF.Square)
            pb = self.bank()
            for c in range(8):
                self.mm(pb.all(), self.onesb.all(), sq[:, c, :], start=(c == 0), stop=(c == 7))
            self.act(rs[:, t0:t0 + tn], pb.all(), AF.Ln, bias=self.epsc[:, 0:1])
        self.act(rs.all(), rs.all(), AF.Exp, scale=-0.5)
        fo = FM_L * DEPTH
        for ti, (t0, tn) in enumerate(TTS):
            for c in range(8):
                self.stt(Y[:, c, :], self.X[:, c, t0:t0 + tn], self.fm[:, fo + c:fo + c + 1], rs[:, t0:t0 + tn], ALU.mult, ALU.mult)
            for b in range(4):
                s = stg[b % 2]
                for half in range(2):
                    pb = self.bank()
                    for cc in range(4):
                        c = half * 4 + cc
                        self.tr(pb[:, cc * 128:(cc + 1) * 128], Y[:, c, b * 128:(b + 1) * 128])
                    self.cp(s[:, half * 512:(half + 1) * 512], pb.all(), eng="act" if half else "dve")
                r0 = t0 + b * 128
                fw.dma("sp", d["y"][r0:r0 + 128, :], s.all())
        for ri, nm in ((0, "ssm_re_out"), (1, "ssm_im_out")):
            pb = self.bank()
            self.tr(pb[:, 0:128], self.fin[:, ri, :])
            s = stg[ri]
            self.cp(s[:, 0:128], pb[:, 0:128])
            fw.dma("sp", d[nm], s[:, 0:128])

    def build(self):
        try:
            self.build_()
        except Stop:
            pass

    def dstop(self, name, view):
        if self.dump(name, view):
            raise Stop()

    def build_(self):
        self.setup()
        self.load_x()
        if self.dump("x0", self.X[:, 0, :]):
            return
        for l in range(DEPTH):
            self.ada(l)
            if self.dump(f"mod{l}", self.modt.all().with_ap(self.modt.all().ap.rearrange("p a b -> p (a b)"))):
                return
            self.norm_mod(0)
            if self.dump(f"h{l}", self.H[:, 0, :]):
                return
            self.mixer_layout()
            self.dn_gates(l)
            for hd in range(4):
                self.dn_head(l, hd)
                if self.dbg in (f"q{l}{hd}", f"k{l}{hd}"):
                    return
                if self.dump(f"odn{l}{hd}", self.o_dn[:, hd, :]):
                    return
            if SKIP_MIX:
                self.memset(self.o_ssm.all(), 0.0)
                self.memset(self.o_pool.all(), 0.0)
            else:
                self.s5(l)
                if self.dump(f"ossm{l}", self.o_ssm[:, 0, :]):
                    return
                self.pool(l)
                if self.dump(f"opool{l}", self.o_pool[:, 0, :]):
                    return
            self.merge(l)
            if self.dump(f"xm{l}", self.X[:, 0, :]):
                return
            self.norm_mod(1)
            self.ffn(l)
            if self.dump(f"xf{l}", self.X[:, 0, :]):
                return
        self.final()


def _pos_embed():
    rows, dim, gw = 16, D, 64
    q = dim // 4
    omega = (1.0 / (10000.0 ** (np.arange(q, dtype=np.float32) / np.float32(q)))).astype(np.float32)
    r = np.repeat(np.arange(rows, dtype=np.float32), gw)
    col = np.tile(np.arange(gw, dtype=np.float32), rows)

    def sc(p):
        ang = (p[:, None] * omega[None, :]).astype(np.float32)
        return np.concatenate([np.sin(ang), np.cos(ang)], axis=-1)
    return np.concatenate([sc(r), sc(col)], axis=-1).astype(np.float32)


def _consts():
    c = np.zeros((128, CST_N), np.float32)
    i = np.arange(128)
    P, Fr = i[:, None], i[None, :]
    c[:, 0:128] = np.eye(128)
    c[:, 128:256] = (P <= Fr)
    c[:, 256:384] = (P >= Fr)
    c[:, 384:512] = np.where(Fr >= P, 0.0, -BIG)
    c[:, 512:640] = np.where(Fr <= P, 0.0, -BIG)
    c[:, 640:768] = np.where(Fr < P, 0.0, BIG)
    c[:, 768:896] = np.where(Fr > P, 0.0, BIG)
    c[:, 896:896 + 1026] = np.arange(1026)[None, :]
    o = 896 + 1026
    for gi in range(4):
        w = 2 << gi
        hw = w // 2
        for k in range(hw):
            c[:, o + gi * 16 + k] = 1.0 / (k + hw)
        for k in range(hw - 1):
            c[:, o + 128 + gi * 16 + k] = 1.0 / (w - 1 - k)
    o += 256
    for k in range(7):
        c[:, o + k * 128:o + (k + 1) * 128] = ((P >> (k + 1)) == (Fr >> (k + 1))) & ((P >> k) != (Fr >> k))
    return c


def _fm(a):
    return np.ascontiguousarray(a.reshape(-1, 128).T)


def _build(dbg=None, dbg_shape=None):
    nc = bass.Bass("TRN2", target_bir_lowering=False)
    dram = {}

    def inp(name, shape):
        dram[name] = nc.dram_tensor(name, list(shape), F32, kind="ExternalInput").ap()

    def outp(name, shape):
        dram[name] = nc.dram_tensor(name, list(shape), F32, kind="ExternalOutput").ap()
    inp("xin", [NT, D]); inp("pe", [1024, D]); inp("cond", [128, 8, 2]); inp("cst", [128, CST_N])
    inp("fm", [128, FM_TOT]); inp("bcp", [1, 32]); inp("sm", [128, 192]); inp("sdn", [2, 2, 4, 128, 128])
    inp("sre", [2, 128, 2, 16]); inp("sim", [2, 128, 2, 16])
    inp("w_ada", [2, D, 6 * D]); inp("w_in", [2, D, IN_COLS]); inp("w_branch_dn", [2, 512, D])
    inp("w_branch_ssm", [2, 512, D]); inp("w_branch_pool", [2, 512, D]); inp("w_out", [2, D, D])
    inp("ffn_w_up", [2, D, 2 * D_FF]); inp("ffn_w_down", [2, D_FF, D]); inp("ssm_glu_w", [2, 512, 512])
    inp("pool_w", [2, 4, 128, 128]); inp("bblk", [2, 16, 128, 2, 128]); inp("cblk", [2, 16, 128, 2, 128])
    outp("y", [NT, D]); outp("dn_out", [2, 2, 2, 4, 128, 128]); outp("ssm_re_out", [128, 128]); outp("ssm_im_out", [128, 128])
    if dbg:
        outp("dbg", list(dbg_shape))
    st = ExitStack()
    kb = KB(nc, st, dram, dbg=dbg)
    kb.build()
    kb.fw.emit()
    return nc, st, kb


def _prep(inputs):
    g = {k: np.asarray(v) for k, v in inputs.items()}
    f32 = np.float32
    shared = {}
    for k in ("w_ada", "w_in", "w_branch_dn", "w_branch_ssm", "w_branch_pool", "w_out", "ffn_w_up", "ffn_w_down",
              "ssm_glu_w", "pool_w"):
        shared[k] = np.ascontiguousarray(g[k], dtype=f32)
    shared["pe"] = _pos_embed()
    shared["cst"] = _consts()
    fm = np.zeros((128, FM_TOT), f32)
    for l in range(DEPTH):
        b = l * FM_L
        fm[:, b + FM_OFF["n1g"]:b + FM_OFF["n1g"] + 16] = np.repeat(_fm(g["norm1_g"][l]), 2, axis=1)
        fm[:, b + FM_OFF["n2g"]:b + FM_OFF["n2g"] + 16] = np.repeat(_fm(g["norm2_g"][l]), 2, axis=1)
        fm[:, b + FM_OFF["bada"]:b + FM_OFF["bada"] + 96] = np.repeat(_fm(g["b_ada"][l]), 2, axis=1)
        dc = g["dn_conv"][l]
        fm[:, b + FM_OFF["dnconv"]:b + FM_OFF["dnconv"] + 36] = dc.reshape(3, 12, 128).transpose(2, 1, 0).reshape(128, 36)
        fm[:, b + FM_OFF["dnng"]] = g["dn_norm_g"][l]
        fm[:, b + FM_OFF["ssmd"]:b + FM_OFF["ssmd"] + 4] = _fm(g["ssm_d"][l])
        fm[:, b + FM_OFF["glub"]:b + FM_OFF["glub"] + 4] = _fm(g["ssm_glu_b"][l])
        fm[:, b + FM_OFF["pscale"]:b + FM_OFF["pscale"] + 4] = _fm(g["pool_scale"][l])
        fc = g["ffn_conv"][l]
        fm[:, b + FM_OFF["fconv"]:b + FM_OFF["fconv"] + 132] = fc.reshape(3, 44, 128).transpose(2, 1, 0).reshape(128, 132)
    fm[:, FM_L * DEPTH:] = _fm(g["final_norm_g"])
    shared["fm"] = fm
    bcp = np.zeros((1, 32), f32)
    sm = np.zeros((128, 192), f32)
    for l in range(DEPTH):
        bcp[0, l * 16:l * 16 + 8] = g["dn_a_log"][l].reshape(8)
        bcp[0, l * 16 + 8:l * 16 + 16] = g["dn_dt_bias"][l].reshape(8)
        for dr in range(2):
            base = ((l * 2 + dr) * 3) * 16
            sm[:, base:base + 16] = g["ssm_lambda_re"][l, dr].reshape(16, 128).T
            sm[:, base + 16:base + 32] = g["ssm_lambda_im"][l, dr].reshape(16, 128).T
            sm[:, base + 32:base + 48] = np.repeat(g["ssm_log_step"][l, dr], 64).reshape(16, 128).T
    shared["bcp"], shared["sm"] = bcp, sm
    bblk = np.zeros((2, 16, 128, 2, 128), f32)
    cblk = np.zeros((2, 16, 128, 2, 128), f32)
    for l in range(DEPTH):
        for st_ in range(16):
            for gl in range(2):
                gg = 2 * st_ + gl
                k0 = 32 * (st_ % 4) + 16 * gl
                for ri, (bn, cn) in enumerate((("ssm_b_re", "ssm_c_re"), ("ssm_b_im", "ssm_c_im"))):
                    bblk[l, st_, k0:k0 + 16, ri, 64 * gl:64 * gl + 64] = g[bn][l, gg].T
                    cblk[l, st_, 64 * gl:64 * gl + 64, ri, k0:k0 + 16] = g[cn][l, gg].T
    shared["bblk"], shared["cblk"] = bblk, cblk
    maps = []
    for c in range(8):
        m = dict(shared)
        m["xin"] = np.ascontiguousarray(np.concatenate([g["x_prompt"][2 * c], g["x_prompt"][2 * c + 1], g["x_sample"][c]], axis=0), dtype=f32)
        cond = np.zeros((128, 8, 2), f32)
        cond[:, :, 0] = _fm(g["c_ctx"])
        cond[:, :, 1] = _fm(g["c"][c])
        m["cond"] = cond
        m["sdn"] = np.ascontiguousarray(g["state_dn"][c], dtype=f32)
        for nm, src in (("sre", "state_ssm_re"), ("sim", "state_ssm_im")):
            a = g[src][c].reshape(2, 2, 16, 128)
            m[nm] = np.ascontiguousarray(a.transpose(0, 3, 1, 2), dtype=f32)
        maps.append(m)
    return maps


_CACHE = {}


def kernel(**inputs):
    if "nc" not in _CACHE:
        _CACHE["nc"] = _build()
    nc, st, kb = _CACHE["nc"]
    maps = _prep(inputs)
    res = run_bass_kernel_spmd(nc, maps, core_ids=list(range(8)))
    R = res.results
    y_prompt = np.zeros((16, 256, D), np.float32)
    y_sample = np.zeros((8, 1024, D), np.float32)
    ndn = np.zeros((16, 2, 2, 4, 128, 128), np.float32)
    nre = np.zeros((16, 2, 2, 32, 64), np.float32)
    nim = np.zeros((16, 2, 2, 32, 64), np.float32)
    for c in range(8):
        y = R[c]["y"]
        y_prompt[2 * c] = y[0:256]
        y_prompt[2 * c + 1] = y[256:512]
        y_sample[c] = y[512:]
        ndn[2 * c:2 * c + 2] = R[c]["dn_out"]
        nre[2 * c:2 * c + 2] = R[c]["ssm_re_out"].reshape(2, 2, 2, 16, 128).reshape(2, 2, 2, 32, 64)
        nim[2 * c:2 * c + 2] = R[c]["ssm_im_out"].reshape(2, 2, 2, 16, 128).reshape(2, 2, 2, 32, 64)
    return (y_prompt, y_sample, ndn, nre, nim)
```
